# Optimizing a Trainium2 kernel written in Bass

```python
import math
import jax, jax.numpy as jnp
from jax import lax
import numpy as np

D_MODEL = 2048
BATCH = 2
SEQ = 4096
DEPTH = 4
DEC_BATCH = 8
DEC_SEQ = 4
PAST_LEN = 16384
PAGE_SIZE = 128

N_A_LAYERS = DEPTH // 2
N_B_LAYERS = DEPTH - N_A_LAYERS
LRU_WIDTH = D_MODEL
N_GATE_BLOCKS = 8
GATE_BLOCK = LRU_WIDTH // N_GATE_BLOCKS
CONV_WIDTH = 4
RG_C = 8.0
N_HEADS = 16
HEAD_DIM = 64
V_DIM = 2 * HEAD_DIM
QK_WIDTH = N_HEADS * 2 * HEAD_DIM
V_WIDTH = N_HEADS * V_DIM
D_FF = ((-(-8 * D_MODEL // 3) + 255) // 256) * 256
ROPE_THETA = 10000.0
EPS = 1e-6
Q_BLOCK = 128

kernel_name = 'yoco_griffin_diff_attn_step'


def rmsnorm(x, g):
    x32 = x.astype(jnp.float32)
    y = x32 * lax.rsqrt(jnp.mean(x32 * x32, axis=-1, keepdims=True) + EPS)
    return y.astype(x.dtype) * g


def rope(x, pos):
    half = HEAD_DIM // 2
    inv = ROPE_THETA ** (-jnp.arange(half, dtype=jnp.float32) / half)
    ang = pos.astype(jnp.float32)[:, None] * inv[None, :]
    cos = jnp.cos(ang)[None, :, None, None, :]
    sin = jnp.sin(ang)[None, :, None, None, :]
    x32 = x.astype(jnp.float32)
    x1, x2 = x32[..., :half], x32[..., half:]
    return jnp.concatenate([x1 * cos - x2 * sin, x2 * cos + x1 * sin], axis=-1).astype(x.dtype)


def causal_conv(u, buf, w, b):
    T = u.shape[1]
    up = jnp.concatenate([buf, u], axis=1)
    out = b + up[:, CONV_WIDTH - 1:CONV_WIDTH - 1 + T] * w[0]
    for j in range(1, CONV_WIDTH):
        out = out + up[:, CONV_WIDTH - 1 - j:CONV_WIDTH - 1 - j + T] * w[j]
    return out, up[:, T:]


def block_diag(x, w, b):
    B, T, W = x.shape
    xb = x.reshape(B, T, N_GATE_BLOCKS, GATE_BLOCK)
    return jnp.einsum('btnc,ncd->btnd', xb, w).reshape(B, T, W) + b


def rg_lru(x, h0, w_a, b_a, w_i, b_i, lam):
    r = jax.nn.sigmoid(block_diag(x, w_a, b_a).astype(jnp.float32))
    i = jax.nn.sigmoid(block_diag(x, w_i, b_i).astype(jnp.float32))
    log_a = -RG_C * r * jax.nn.softplus(-lam.astype(jnp.float32))
    a = jnp.exp(log_a)
    mult = jnp.sqrt(-jnp.expm1(2.0 * log_a))
    bterm = mult * i * x.astype(jnp.float32)
    bterm = bterm.at[:, 0].add(a[:, 0] * h0.astype(jnp.float32))

    def combine(left, right):
        a1, b1 = left
        a2, b2 = right
        return a1 * a2, a2 * b1 + b2

    _, h = lax.associative_scan(combine, (a, bterm), axis=1)
    return h.astype(x.dtype), h[:, -1].astype(x.dtype)


def recurrent_block(xn, conv_buf, h0, w_x, w_gate, conv_w, conv_b, w_a, b_a, w_i, b_i, lam, w_out):
    gate = jax.nn.gelu(xn @ w_gate)
    u = xn @ w_x
    u, new_buf = causal_conv(u, conv_buf, conv_w, conv_b)
    h, h_last = rg_lru(u, h0, w_a, b_a, w_i, b_i, lam)
    return (h * gate) @ w_out, new_buf, h_last


def shared_kv(x, pos, kv_norm, w_kv):
    B, T, _ = x.shape
    kv = rmsnorm(x, kv_norm) @ w_kv
    k = rope(kv[..., :QK_WIDTH].reshape(B, T, N_HEADS, 2, HEAD_DIM), pos)
    v = kv[..., QK_WIDTH:].reshape(B, T, N_HEADS, V_DIM)
    return k, v


def diff_attend(q, k, v, q_pos, k_pos, lam):
    s = jnp.einsum('bqhcd,bkhcd->bhcqk', q, k).astype(jnp.float32) * (HEAD_DIM ** -0.5)
    mask = k_pos[None, :] <= q_pos[:, None]
    s = jnp.where(mask[None, None, None], s, -jnp.inf)
    p = jax.nn.softmax(s, axis=-1)
    a = p[:, :, 0] - lam * p[:, :, 1]
    return jnp.einsum('bhqk,bkhd->bqhd', a.astype(v.dtype), v)


def diff_attention(q, k, v, q_pos, k_pos, lam):
    B, T = q.shape[0], q.shape[1]
    if T <= Q_BLOCK or T % Q_BLOCK != 0:
        return diff_attend(q, k, v, q_pos, k_pos, lam)
    nb = T // Q_BLOCK
    qb = q.reshape(B, nb, Q_BLOCK, N_HEADS, 2, HEAD_DIM).swapaxes(0, 1)
    pb = q_pos.reshape(nb, Q_BLOCK)
    out = lax.map(lambda args: diff_attend(args[0], k, v, args[1], k_pos, lam), (qb, pb))
    return out.swapaxes(0, 1).reshape(B, T, N_HEADS, V_DIM)


def diff_block(xn, pos, k_all, v_all, k_pos, w_q, lq1, lk1, lq2, lk2, subln, w_o, lam_init):
    B, T, _ = xn.shape
    q = rope((xn @ w_q).reshape(B, T, N_HEADS, 2, HEAD_DIM), pos)
    lam = (jnp.exp(jnp.sum(lq1.astype(jnp.float32) * lk1.astype(jnp.float32)))
           - jnp.exp(jnp.sum(lq2.astype(jnp.float32) * lk2.astype(jnp.float32))) + lam_init)
    o = diff_attention(q, k_all, v_all, pos, k_pos, lam)
    o = rmsnorm(o, subln) * (1.0 - lam_init)
    return o.reshape(B, T, V_WIDTH) @ w_o


def swiglu(xn, w_gate, w_up, w_down):
    return (jax.nn.silu(xn @ w_gate) * (xn @ w_up)) @ w_down


def trunk(x, pos, conv_bufs, h0s, past_k, past_v, p):
    new_bufs, new_hs = [], []
    k_new = v_new = k_all = v_all = k_pos = None
    for l in range(DEPTH):
        xn = rmsnorm(x, p['norm_mix'][l])
        if l < N_A_LAYERS:
            y, nb, nh = recurrent_block(xn, conv_bufs[l], h0s[l], p['rg_w_x'][l], p['rg_w_gate'][l],
                                        p['rg_conv_w'][l], p['rg_conv_b'][l], p['rg_w_a'][l], p['rg_b_a'][l],
                                        p['rg_w_i'][l], p['rg_b_i'][l], p['rg_lambda'][l], p['rg_w_out'][l])
            new_bufs.append(nb)
            new_hs.append(nh)
        else:
            j = l - N_A_LAYERS
            if j == 0:
                k_new, v_new = shared_kv(x, pos, p['kv_norm'], p['w_kv'])
                if past_k is None:
                    k_all, v_all, k_pos = k_new, v_new, pos
                else:
                    k_all = jnp.concatenate([past_k, k_new], axis=1)
                    v_all = jnp.concatenate([past_v, v_new], axis=1)
                    k_pos = jnp.concatenate([jnp.arange(past_k.shape[1], dtype=jnp.int32), pos])
            lam_init = 0.8 - 0.6 * math.exp(-0.3 * l)
            y = diff_block(xn, pos, k_all, v_all, k_pos, p['dif_w_q'][j], p['dif_lq1'][j], p['dif_lk1'][j],
                           p['dif_lq2'][j], p['dif_lk2'][j], p['dif_subln'][j], p['dif_w_o'][j], lam_init)
        x = x + y
        x = x + swiglu(rmsnorm(x, p['norm_ffn'][l]), p['ffn_w_gate'][l], p['ffn_w_up'][l], p['ffn_w_down'][l])
    return rmsnorm(x, p['norm_final']), k_new, v_new, jnp.stack(new_bufs), jnp.stack(new_hs)


def setup_inputs(seed: int = 0) -> dict:
    key = jax.random.key(seed)
    ks = iter(jax.random.split(key, 48))

    def nrm(shape, scale):
        return jax.random.normal(next(ks), shape, jnp.float32) * scale

    n_pages = PAST_LEN // PAGE_SIZE
    n_phys = (5 * DEC_BATCH * n_pages) // 4
    page_table = jax.random.permutation(next(ks), n_phys)[:DEC_BATCH * n_pages]
    page_table = page_table.reshape(DEC_BATCH, n_pages).astype(jnp.int32)
    u = jax.random.uniform(next(ks), (N_A_LAYERS, LRU_WIDTH), jnp.float32, 0.9, 0.999)
    a0 = u ** (1.0 / RG_C)
    rg_lambda = jnp.log(a0) - jnp.log1p(-a0)
    out_scale = (2 * DEPTH) ** -0.5
    return {
        'x_prompt': nrm((BATCH, SEQ, D_MODEL), 1.0),
        'x_sample': nrm((DEC_BATCH, DEC_SEQ, D_MODEL), 1.0),
        'cache_k': nrm((n_phys, PAGE_SIZE, N_HEADS, 2, HEAD_DIM), 1.0),
        'cache_v': nrm((n_phys, PAGE_SIZE, N_HEADS, V_DIM), 1.0),
        'page_table': page_table,
        'state_conv': nrm((N_A_LAYERS, DEC_BATCH, CONV_WIDTH - 1, LRU_WIDTH), 0.5),
        'state_rglru': nrm((N_A_LAYERS, DEC_BATCH, LRU_WIDTH), 0.5),
        'norm_mix': 1.0 + nrm((DEPTH, D_MODEL), 0.02),
        'norm_ffn': 1.0 + nrm((DEPTH, D_MODEL), 0.02),
        'norm_final': 1.0 + nrm((D_MODEL,), 0.02),
        'rg_w_x': nrm((N_A_LAYERS, D_MODEL, LRU_WIDTH), D_MODEL ** -0.5),
        'rg_w_gate': nrm((N_A_LAYERS, D_MODEL, LRU_WIDTH), D_MODEL ** -0.5),
        'rg_conv_w': nrm((N_A_LAYERS, CONV_WIDTH, LRU_WIDTH), CONV_WIDTH ** -0.5),
        'rg_conv_b': nrm((N_A_LAYERS, LRU_WIDTH), 0.01),
        'rg_w_a': nrm((N_A_LAYERS, N_GATE_BLOCKS, GATE_BLOCK, GATE_BLOCK), GATE_BLOCK ** -0.5),
        'rg_b_a': nrm((N_A_LAYERS, LRU_WIDTH), 0.01),
        'rg_w_i': nrm((N_A_LAYERS, N_GATE_BLOCKS, GATE_BLOCK, GATE_BLOCK), GATE_BLOCK ** -0.5),
        'rg_b_i': nrm((N_A_LAYERS, LRU_WIDTH), 0.01),
        'rg_lambda': rg_lambda,
        'rg_w_out': nrm((N_A_LAYERS, LRU_WIDTH, D_MODEL), LRU_WIDTH ** -0.5 * out_scale),
        'kv_norm': 1.0 + nrm((D_MODEL,), 0.02),
        'w_kv': nrm((D_MODEL, QK_WIDTH + V_WIDTH), D_MODEL ** -0.5),
        'dif_w_q': nrm((N_B_LAYERS, D_MODEL, QK_WIDTH), D_MODEL ** -0.5),
        'dif_lq1': nrm((N_B_LAYERS, HEAD_DIM), 0.1),
        'dif_lk1': nrm((N_B_LAYERS, HEAD_DIM), 0.1),
        'dif_lq2': nrm((N_B_LAYERS, HEAD_DIM), 0.1),
        'dif_lk2': nrm((N_B_LAYERS, HEAD_DIM), 0.1),
        'dif_subln': 1.0 + nrm((N_B_LAYERS, V_DIM), 0.02),
        'dif_w_o': nrm((N_B_LAYERS, V_WIDTH, D_MODEL), V_WIDTH ** -0.5 * out_scale),
        'ffn_w_gate': nrm((DEPTH, D_MODEL, D_FF), D_MODEL ** -0.5),
        'ffn_w_up': nrm((DEPTH, D_MODEL, D_FF), D_MODEL ** -0.5),
        'ffn_w_down': nrm((DEPTH, D_FF, D_MODEL), D_FF ** -0.5 * out_scale),
    }


def reference(x_prompt, x_sample, cache_k, cache_v, page_table, state_conv, state_rglru,
              norm_mix, norm_ffn, norm_final, rg_w_x, rg_w_gate, rg_conv_w, rg_conv_b,
              rg_w_a, rg_b_a, rg_w_i, rg_b_i, rg_lambda, rg_w_out, kv_norm, w_kv,
              dif_w_q, dif_lq1, dif_lk1, dif_lq2, dif_lk2, dif_subln, dif_w_o,
              ffn_w_gate, ffn_w_up, ffn_w_down):
    p = dict(norm_mix=norm_mix, norm_ffn=norm_ffn, norm_final=norm_final, rg_w_x=rg_w_x,
             rg_w_gate=rg_w_gate, rg_conv_w=rg_conv_w, rg_conv_b=rg_conv_b, rg_w_a=rg_w_a,
             rg_b_a=rg_b_a, rg_w_i=rg_w_i, rg_b_i=rg_b_i, rg_lambda=rg_lambda, rg_w_out=rg_w_out,
             kv_norm=kv_norm, w_kv=w_kv, dif_w_q=dif_w_q, dif_lq1=dif_lq1, dif_lk1=dif_lk1,
             dif_lq2=dif_lq2, dif_lk2=dif_lk2, dif_subln=dif_subln, dif_w_o=dif_w_o,
             ffn_w_gate=ffn_w_gate, ffn_w_up=ffn_w_up, ffn_w_down=ffn_w_down)

    b_p, t_p = x_prompt.shape[0], x_prompt.shape[1]
    pos_p = jnp.arange(t_p, dtype=jnp.int32)
    conv0 = jnp.zeros((N_A_LAYERS, b_p, CONV_WIDTH - 1, LRU_WIDTH), x_prompt.dtype)
    h0 = jnp.zeros((N_A_LAYERS, b_p, LRU_WIDTH), x_prompt.dtype)
    y_prompt, k_prompt, v_prompt, conv_prompt, h_prompt = trunk(x_prompt, pos_p, conv0, h0, None, None, p)

    b_s, t_s = x_sample.shape[0], x_sample.shape[1]
    past_len = page_table.shape[1] * cache_k.shape[1]
    k_past = cache_k[page_table].reshape(b_s, past_len, N_HEADS, 2, HEAD_DIM)
    v_past = cache_v[page_table].reshape(b_s, past_len, N_HEADS, V_DIM)
    pos_s = past_len + jnp.arange(t_s, dtype=jnp.int32)
    y_sample, k_sample, v_sample, conv_sample, h_sample = trunk(x_sample, pos_s, state_conv, state_rglru,
                                                                k_past, v_past, p)
    return (y_prompt, y_sample, k_prompt, v_prompt, conv_prompt, h_prompt,
            k_sample, v_sample, conv_sample, h_sample)
```

```python
import math
from contextlib import ExitStack

import numpy as np
import concourse.bass as bass
import concourse.mybir as mybir
from concourse.bass_utils import run_bass_kernel_spmd

F32 = mybir.dt.float32
BF16 = mybir.dt.bfloat16
I32 = mybir.dt.int32
ALU = mybir.AluOpType
AF = mybir.ActivationFunctionType
AX = mybir.AxisListType

D = 2048
KC = 16
SEG = 512
NSMP = 32
T = 2 * SEG + NSMP
FF = 5632
NH = 16
EPS = 1e-6
PAST = 16384
NPG = 128
TT = ((0, 512), (512, 512), (1024, 32))
ENGS = ("pe", "act", "dve", "pool", "sp")
SAME_ENGINE_SYNC = True

V_NMIX, V_NFFN, V_NFIN, V_NKV, V_CW, V_CB, V_BA, V_BI, V_LAM = 0, 4, 8, 9, 10, 18, 20, 22, 24
NV = 26


class Res:
    __slots__ = ("name", "w", "r")

    def __init__(self, name):
        self.name = name
        self.w = None
        self.r = {}


class Node:
    __slots__ = ("eng", "fn", "deps", "kind", "signal", "sem", "val")

    def __init__(self, eng, fn, deps, kind):
        self.eng, self.fn, self.deps, self.kind = eng, fn, deps, kind
        self.signal = False
        self.sem = None
        self.val = 0


class Prog:
    NDMA = {"sp": 20, "pool": 20, "act": 6}

    def __init__(self):
        self.ops = {e: [] for e in ENGS}
        self.dma_rr = {q: 0 for q in self.NDMA}
        self.dma_cnt = {}
        self.cc_cnt = 0
        self.uid = 0
        self.dma_events = []

    def _deps(self, reads, writes):
        deps = set()
        for r in reads:
            if r.w is not None:
                deps.add(r.w)
        for w in writes:
            if w.w is not None:
                deps.add(w.w)
            deps.update(w.r.values())
        return deps

    def _commit(self, ev, key, reads, writes):
        for w in writes:
            w.w = ev
            w.r = {}
        for r in reads:
            if r not in writes:
                r.r[key] = ev

    def C(self, eng, fn, r=(), w=()):
        deps = self._deps(r, w)
        idx = len(self.ops[eng])
        self.ops[eng].append(Node(eng, fn, deps, "c"))
        ev = ("c", eng, idx)
        self._commit(ev, eng, r, w)
        return ev

    def DMA(self, q, fn, r=(), w=()):
        deps = self._deps(r, w)
        i = self.dma_rr[q]
        self.dma_rr[q] = (i + 1) % self.NDMA[q]
        n = self.dma_cnt.get((q, i), 0)
        if n > 0:
            deps.add(("d", (q, i), n * 16))
        node = Node(q, fn, deps, "d")
        node.sem = (q, i)
        node.val = (n + 1) * 16
        self.dma_cnt[(q, i)] = n + 1
        self.ops[q].append(node)
        ev = ("d", (q, i), node.val)
        self.uid += 1
        self._commit(ev, ("d", self.uid), r, w)
        self.dma_events.append(ev)
        return ev

    def CC(self, fn, r=(), w=()):
        deps = self._deps(r, w)
        node = Node("pool", fn, deps, "cc")
        self.cc_cnt += 1
        node.sem = "cc"
        node.val = self.cc_cnt
        self.ops["pool"].append(node)
        ev = ("d", "cc", node.val)
        self.uid += 1
        self._commit(ev, ("d", self.uid), r, w)
        self.dma_events.append(ev)
        return ev

    def barrier(self):
        last = {}
        for e in ENGS:
            for i in range(len(self.ops[e]) - 1, -1, -1):
                if self.ops[e][i].kind == "c":
                    last[e] = ("c", e, i)
                    break
        deps = set(last.values()) | set(self.dma_events)
        self.dma_events = []
        for e in ENGS:
            self.ops[e].append(Node(e, None, set(deps), "b"))

    def emit(self, nc, block_engs):
        for e in ENGS:
            for node in self.ops[e]:
                for d in node.deps:
                    if d[0] == "c":
                        if d[1] == e and (e == "pe" or not SAME_ENGINE_SYNC):
                            continue
                        self.ops[d[1]][d[2]].signal = True
        sigcount = {}
        for e in ENGS:
            c = 0
            arr = []
            for node in self.ops[e]:
                if node.kind == "c" and node.signal:
                    c += 1
                arr.append(c)
            sigcount[e] = arr
        self.sigcount = sigcount
        return sigcount


def build_program(nseq_groups=2):
    nc = bass.Bass("TRN2", target_bir_lowering=False)
    P = Prog()
    es = ExitStack()

    def din(name, shape, dt=F32):
        return nc.dram_tensor(name, list(shape), dt, kind="ExternalInput").ap()

    def dout(name, shape, dt=F32):
        return nc.dram_tensor(name, list(shape), dt, kind="ExternalOutput").ap()

    def dtmp(name, shape, dt=F32):
        return nc.dram_tensor(name, list(shape), dt).ap()

    def sb(name, shape, dt=F32):
        return es.enter_context(nc.sbuf_tensor(name, list(shape), dt))

    xp = din("xp", [1024, D])
    xs = din("xs", [NSMP, D])
    vecs = din("vecs", [NV * 16, 128])
    cosd = din("cos32", [128, 9, 32])
    sind = din("sin32", [128, 9, 32])
    maskd = din("maskd", [128, 1024])
    coefd = din("coefd", [128, 64])
    smaskd = din("smaskd", [32, 64])
    ptab = din("ptab", [NPG, 8], I32)
    iotad = din("iotad", [128, 8])
    stc = din("state_conv", [2, 8, 3, D])
    sth = din("state_rglru", [2, 8, D])
    ckT = din("ckT", [1280, 128, 2, 128])
    cv = din("cv", [1280, 128, 2, 128])
    w_x = din("rg_w_x", [2, D, D]); w_g = din("rg_w_gate", [2, D, D]); w_o = din("rg_w_out", [2, D, D])
    w_a = din("rg_w_a", [2, 8, 256, 256]); w_i = din("rg_w_i", [2, 8, 256, 256])
    w_kv = din("w_kv", [D, 2 * D])
    w_q = din("dif_w_q", [2, D, D]); w_do = din("dif_w_o", [2, D, D])
    lqk = din("lqk", [4, 2, 64])
    subln = din("dif_subln", [2, 128])
    f_g = din("ffn_w_gate", [4, D, FF]); f_u = din("ffn_w_up", [4, D, FF]); f_d = din("ffn_w_down", [4, FF, D])

    y_p = dout("y_p", [1024, D]); y_s = dout("y_s", [NSMP, D])
    k_p = dout("k_p", [1024, D]); v_p = dout("v_p", [1024, D])
    k_s = dout("k_s", [NSMP, D]); v_s = dout("v_s", [NSMP, D])
    conv_p = dout("conv_p", [2, 3, D]); h_p = dout("h_p", [2, D])
    conv_s = dout("conv_s", [2, 8, 3, D]); h_s = dout("h_s", [2, 8, D])

    halo_in_d = dtmp("halo_in_d", [128, 96]); halo_g_d = dtmp("halo_g_d", [4 * 128, 96])
    car_in_d = dtmp("car_in_d", [128, 64]); car_g_d = dtmp("car_g_d", [4 * 128, 64])
    x1_d = dtmp("x1_d", [128, KC, T], BF16); x2_d = dtmp("x2_d", [128, KC, 1024], BF16)
    kT_loc = [dtmp("kT_loc%d" % q, [4 * 128, 1024], BF16) for q in range(4)]; kT_g = [dtmp("kT_g%d" % q, [4 * 4 * 128, 1024], BF16) for q in range(4)]
    v_loc = [dtmp("v_loc%d" % q, [4 * 2 * 128, 512], BF16) for q in range(4)]; v_g = [dtmp("v_g%d" % q, [4 * 4 * 2 * 128, 512], BF16) for q in range(4)]
    vs_d = dtmp("vs_d", [NSMP, D], BF16)
    os_in_d = dtmp("os_in_d", [128, 64]); os_g_d = dtmp("os_g_d", [8 * 128, 64])

    xT = sb("xT", [128, KC, T]); R_x = Res("xT")
    xn = sb("xn", [128, KC, T], BF16); R_xn = Res("xn")
    AW = 16384
    arena = sb("arena", [128, AW]); arena_b = arena[:].bitcast(BF16)
    wbuf = sb("wbuf", [128, 2, 16 * 256], BF16); R_wb = [Res("wb0"), Res("wb1")]
    ps = es.enter_context(nc.psum_tensor("ps", [128, 8, 512], F32)); R_ps = [Res("ps%d" % i) for i in range(8)]
    pv = sb("pv", [128, NV * 16]); R_pv = Res("pv")
    ident_f = sb("ident_f", [128, 128]); ident_b = sb("ident_b", [128, 128], BF16)
    ones_f = sb("ones_f", [128, 128]); ones_b = sb("ones_b", [128, 128], BF16)
    R_const = Res("const")
    mtab = sb("mtab", [128, 1024], BF16)
    zeros_b = mtab[:, 0:512]
    cos32 = sb("cos32_sb", [128, 9, 32]); sin32 = sb("sin32_sb", [128, 9, 32])
    coef = sb("coef", [128, 64])
    smask = sb("smask", [32, 64])
    pt_i = sb("pt_i", [128, 8], I32); pt_f = sb("pt_f", [128, 8]); iota8 = sb("iota8", [128, 8])
    nsp = sb("nsp", [128, 2, 2, KC])
    lamt = sb("lamt", [128, 2, 4])
    gsub = sb("gsub", [128, 2]); gsub_row = sb("gsub_row", [32, 2, 128])
    lq_sb = arena[:, 1024:1536].rearrange("p (a l d) -> p a l d", a=4, l=2)
    halo_own = sb("halo_own", [128, KC, 2, 3]); halo_sel = sb("halo_sel", [128, KC, 2, 3]); halo_all = sb("halo_all", [128, 4, 96])
    hsmp = sb("hsmp", [128, KC, 24]); h0s = sb("h0s", [128, KC, 8])
    car = sb("car", [128, 2, KC, 2]); car_all = halo_all[:, :, 0:64]; Hc = sb("Hc", [128, 9, KC]); Hin = sb("Hin", [128, KC, 2])
    hlast_s = sb("hlast_s", [128, KC, 8]); csmp = sb("csmp", [128, KC, 24])
    ksT = sb("ksT", [128, NH, NSMP], BF16); ksT_sel = sb("ksT_sel", [128, 2, NSMP], BF16)
    vnew_sel = sb("vnew_sel", [32, 2, 132], BF16)
    qbd = sb("qbd", [128, 2, 64], BF16)
    R_small = {n: Res(n) for n in ("nsp", "lam", "gsub", "halo_own", "halo_sel", "halo_all", "hsmp", "h0s", "car", "car_all",
                                   "Hc", "Hin", "hlast_s", "csmp", "ksT", "ksT_sel", "vnew", "vnew_sel", "qbd", "lq", "pt")}
    R_small["car_all"] = R_small["halo_all"]

    def af(off, n):
        return arena[:, off:off + n]

    def ab(off, n):
        return arena_b[:, 2 * off:2 * off + n]

    NORM_OFF = AW - 3 * T
    sq_t = [af(NORM_OFF + i * T, T) for i in range(2)]; R_sq = [Res("sq0"), Res("sq1")]
    rs_t = af(NORM_OFF + 2 * T, T); R_rs = Res("rs")

    def ps_seg(s):
        return ps[:, 3 * s:3 * s + 2, :]

    def ps_smp(s):
        return ps[:, 3 * s + 2, 0:NSMP]

    def R_set(s):
        return R_ps[3 * s:3 * s + 3]

    def segv(ap2d):
        return ap2d[:, 0:1024].rearrange("p (s w) -> p s w", w=512)

    def smpv(ap2d):
        return ap2d[:, 1024:T]

    C, DMA = P.C, P.DMA

    C("dve", lambda e: e.memset(ones_f[:], 1.0), w=[R_const])
    C("dve", lambda e: e.memset(ones_b[:], 1.0), w=[R_const])
    identd = din("identd", [128, 128])
    DMA("sp", lambda e: e.dma_start(out=ident_f[:], in_=identd), w=[R_const])
    C("dve", lambda e: e.tensor_copy(out=ident_b[:], in_=ident_f[:]), r=[R_const], w=[R_const])
    DMA("sp", lambda e: e.dma_start(out=cos32[:], in_=cosd), w=[R_const])
    DMA("sp", lambda e: e.dma_start(out=sin32[:], in_=sind), w=[R_const])
    DMA("sp", lambda e: e.dma_start(out=coef[:], in_=coefd), w=[R_const])
    DMA("sp", lambda e: e.dma_start(out=smask[:], in_=smaskd), w=[R_const])
    DMA("sp", lambda e: e.dma_start(out=pt_i[:], in_=ptab), w=[R_small["pt"]])
    DMA("sp", lambda e: e.dma_start(out=iota8[:], in_=iotad), w=[R_const])
    C("dve", lambda e: e.tensor_copy(out=pt_f[:], in_=pt_i[:]), r=[R_small["pt"]], w=[R_small["pt"]])
    DMA("pool", lambda e: e.dma_start(out=mtab[:], in_=maskd), w=[R_const])
    vin = af(0, 4 * 128).rearrange("p (a b) -> p a b", b=128)
    R_vin = Res("vin")
    for a in range(4):
        rows = min(128, NV * 16 - a * 128)
        DMA("sp", lambda e, a=a, rows=rows: e.dma_start(out=vin[0:rows, a, :], in_=vecs[a * 128:a * 128 + rows, :]), w=[R_vin])
    for a in range(4):
        rows = min(128, NV * 16 - a * 128)
        C("pe", lambda e, a=a, rows=rows: e.transpose(ps[:, 0, a * 128:a * 128 + rows], vin[0:rows, a, :], ident_f[0:rows, 0:rows]),
          r=[R_vin, R_const], w=[R_ps[0]])
    C("dve", lambda e: e.tensor_copy(out=pv[:], in_=ps[:, 0, 0:NV * 16]), r=[R_ps[0]], w=[R_pv])

    def pvc(v, c):
        return pv[:, v * 16 + c:v * 16 + c + 1]

    lam_v = pv[:, V_LAM * 16:(V_LAM + 2) * 16].rearrange("p (l c) -> p l c", c=KC)
    C("act", lambda e: e.activation(out=nsp[:, :, 0, :], in_=lam_v, func=AF.Exp, scale=-1.0), r=[R_pv], w=[R_small["nsp"]])
    C("act", lambda e: e.activation(out=nsp[:, :, 0, :], in_=nsp[:, :, 0, :], func=AF.Ln, bias=1.0, scale=1.0), r=[R_small["nsp"]], w=[R_small["nsp"]])
    C("dve", lambda e: e.tensor_scalar(out=nsp[:, :, 1, :], in0=nsp[:, :, 0, :], scalar1=-16.0, scalar2=None, op0=ALU.mult), r=[R_small["nsp"]], w=[R_small["nsp"]])
    C("dve", lambda e: e.tensor_scalar(out=nsp[:, :, 0, :], in0=nsp[:, :, 0, :], scalar1=-8.0, scalar2=None, op0=ALU.mult), r=[R_small["nsp"]], w=[R_small["nsp"]])
    DMA("sp", lambda e: e.dma_start(out=arena[:, 1024:1536],
                                     in_=lqk.rearrange("a l d -> (a l d)").partition_broadcast(128)), w=[R_small["lq"]])
    lprod = af(2048, 256).rearrange("p (a l d) -> p a l d", a=2, l=2)
    lsum = af(2304, 4).rearrange("p (a l) -> p a l", a=2)
    R_lt = Res("ltmp")
    for a in range(2):
        C("dve", lambda e, a=a: e.tensor_tensor(out=lprod[:, a, :, :], in0=lq_sb[:, 2 * a, :, :], in1=lq_sb[:, 2 * a + 1, :, :], op=ALU.mult),
          r=[R_small["lq"], R_vin], w=[R_lt])
    C("dve", lambda e: e.tensor_reduce(out=lsum, in_=lprod, axis=AX.X, op=ALU.add), r=[R_lt], w=[R_lt])
    C("act", lambda e: e.activation(out=lsum, in_=lsum, func=AF.Exp), r=[R_lt], w=[R_lt])
    for j in range(2):
        lam_init = 0.8 - 0.6 * math.exp(-0.3 * (j + 2))
        C("dve", lambda e, j=j: e.tensor_tensor(out=lamt[:, j, 0:1], in0=lsum[:, 0, j:j + 1], in1=lsum[:, 1, j:j + 1], op=ALU.subtract),
          r=[R_lt], w=[R_small["lam"]])
        C("dve", lambda e, j=j, li=lam_init: e.tensor_scalar(out=lamt[:, j, 0:1], in0=lamt[:, j, 0:1], scalar1=li, scalar2=None, op0=ALU.add),
          r=[R_small["lam"]], w=[R_small["lam"]])
        C("dve", lambda e, j=j: e.tensor_scalar(out=lamt[:, j, 1:2], in0=lamt[:, j, 0:1], scalar1=-1.0, scalar2=None, op0=ALU.mult),
          r=[R_small["lam"]], w=[R_small["lam"]])
        DMA("sp", lambda e, j=j: e.dma_start(out=gsub[:, j:j + 1], in_=subln[j:j + 1, :].rearrange("o d -> d o")), w=[R_small["gsub"]])
        DMA("sp", lambda e, j=j: e.dma_start(out=gsub_row[:, j, :], in_=subln[j, :].partition_broadcast(32)), w=[R_small["gsub"]])
        C("dve", lambda e, j=j, li=lam_init: e.tensor_scalar(out=gsub[:, j:j + 1], in0=gsub[:, j:j + 1], scalar1=1.0 - li, scalar2=None, op0=ALU.mult),
          r=[R_small["gsub"]], w=[R_small["gsub"]])
        C("dve", lambda e, j=j, li=lam_init: e.tensor_scalar(out=gsub_row[:, j, :], in0=gsub_row[:, j, :], scalar1=1.0 - li, scalar2=None, op0=ALU.mult),
          r=[R_small["gsub"]], w=[R_small["gsub"]])


    def mm(out, lhsT, rhs, st, sp):
        return lambda e: e.matmul(out, lhsT=lhsT, rhs=rhs, start=st, stop=sp)

    def tr(out, in_, idn):
        return lambda e: e.transpose(out, in_, idn)

    def actf(out, in_, func, bias=None, scale=None):
        kw = {}
        if bias is not None:
            kw["bias"] = bias
        if scale is not None:
            kw["scale"] = scale
        return lambda e: e.activation(out=out, in_=in_, func=func, **kw)

    def tt(out, in0, in1, op):
        return lambda e: e.tensor_tensor(out=out, in0=in0, in1=in1, op=op)

    def ts(out, in0, s1, op0, s2=None, op1=None):
        if op1 is None:
            return lambda e: e.tensor_scalar(out=out, in0=in0, scalar1=s1, scalar2=None, op0=op0)
        return lambda e: e.tensor_scalar(out=out, in0=in0, scalar1=s1, scalar2=s2, op0=op0, op1=op1)

    def stt(out, in0, scalar, in1, op0, op1):
        return lambda e: e.scalar_tensor_tensor(out=out, in0=in0, scalar=scalar, in1=in1, op0=op0, op1=op1)

    def cp(out, in_):
        return lambda e: e.tensor_copy(out=out, in_=in_)

    def acp(out, in_):
        return lambda e: e.activation(out=out, in_=in_, func=AF.Copy)

    def dma(out, in_):
        return lambda e: e.dma_start(out=out, in_=in_)

    def mset(ap, v):
        return lambda e: e.memset(ap, v)

    GROUPS4 = [[0, 1, 2, 3], [4, 5, 6, 7]]
    GROUPS8 = [list(range(8))]
    R_hd_in, R_hd_g, R_cd_in, R_cd_g, R_x1d, R_x2d = (Res(n) for n in ("hd_in", "hd_g", "cd_in", "cd_g", "x1d", "x2d"))
    R_kTl, R_kTg, R_vl, R_vg, R_vsd, R_os_in, R_os_g = (Res(n) for n in ("kTl", "kTg", "vl", "vg", "vsd", "os_in", "os_g"))
    eps_t = sb("eps_t", [128, 1])
    C("dve", mset(eps_t[:], EPS), w=[R_const])
    one_t = sb("one_t", [128, 1])
    C("dve", mset(one_t[:], 1.0), w=[R_const])
    ebias = sb("ebias", [128, 16])
    ebiasd = din("ebiasd", [128, 16])
    DMA("sp", dma(ebias[:], ebiasd), w=[R_const])
    P.barrier()

    ws_pending, ws_loaded, ws_free = [], [], [0, 1]

    def ws_pump():
        while ws_free and ws_pending:
            src3, kcn, ncols = ws_pending.pop(0)
            s = ws_free.pop(0)
            dst = wbuf[:, s, 0:kcn * ncols].rearrange("p (k n) -> p k n", n=ncols)
            DMA("pool", dma(dst, src3), w=[R_wb[s]])
            ws_loaded.append((s, dst))

    def ws_queue(blocks):
        ws_pending.extend(blocks)
        ws_pump()

    def ws_take():
        ws_pump()
        return ws_loaded.pop(0)

    def ws_release(blk):
        ws_free.append(blk[0])
        ws_pump()

    def wblk(W2d, r0, kcn, c0, ncols=256):
        return (W2d[r0:r0 + kcn * 128, c0:c0 + ncols].rearrange("(k p) n -> p k n", p=128), kcn, ncols)

    pset = [0]

    def next_set():
        s = pset[0]
        pset[0] ^= 1
        return s

    def mm_chunk(blk, kcn, mm_i, rhs3, R_rhs, s):
        slot, wd = blk
        for k in range(kcn):
            for ti, (c0, w) in enumerate(TT):
                C("pe", mm(ps[:, 3 * s + ti, 0:w], wd[:, k, mm_i * 128:(mm_i + 1) * 128], rhs3[:, k, c0:c0 + w], k == 0, k == kcn - 1),
                  r=[R_wb[slot], R_rhs], w=[R_ps[3 * s + ti]])

    tmpo = sb("tmpo", [128, 384]); R_tmpo = Res("tmpo")
    tmpo2 = sb("tmpo2", [128, 128]); R_tmpo2 = Res("tmpo2")

    def out_fm(src_view, A, dram_rows, R_src):
        n = A * KC
        C("dve", cp(tmpo[:, 0:n].rearrange("p (a c) -> p a c", c=KC), src_view), r=[R_src], w=[R_tmpo])
        for g0 in range(0, n, 128):
            gw = min(128, n - g0)
            C("pe", tr(ps[0:gw, 7, 0:128], tmpo[:, g0:g0 + gw], ident_f[:]), r=[R_tmpo, R_const], w=[R_ps[7]])
            C("dve", cp(tmpo2[0:gw, :], ps[0:gw, 7, 0:128]), r=[R_ps[7]], w=[R_tmpo2])
            DMA("sp", dma(dram_rows[g0:g0 + gw, :], tmpo2[0:gw, :]), r=[R_tmpo2])

    tmpi = af(4096, 2048); R_tmpi = Res("tmpi")

    def in_tm(dram2d, Rr, dst_view, R_dst):
        DMA("sp", dma(tmpi[0:Rr, :], dram2d), w=[R_tmpi])
        for c in range(KC):
            C("pe", tr(ps[:, 7, c * Rr:(c + 1) * Rr], tmpi[0:Rr, c * 128:(c + 1) * 128], ident_f[0:Rr, 0:Rr]), r=[R_tmpi, R_const], w=[R_ps[7]])
        C("dve", cp(dst_view, ps[:, 7, 0:KC * Rr].rearrange("p (c r) -> p c r", r=Rr)), r=[R_ps[7]], w=[R_dst])

    xin = [af(i * 2048, 2048) for i in range(2)]
    R_xin = [Res("xin0"), Res("xin1")]
    for tk in range(9):
        rows = 128 if tk < 8 else NSMP
        src = xp[tk * 128:(tk + 1) * 128, :] if tk < 8 else xs
        b = tk % 2
        DMA("sp", dma(xin[b][0:rows, :], src), w=[R_xin[b]])
        for cg in range(4):
            bank = 6 + (cg % 2)
            for cc in range(4):
                c = cg * 4 + cc
                C("pe", tr(ps[:, bank, cc * 128:cc * 128 + rows], xin[b][0:rows, c * 128:(c + 1) * 128], ident_f[0:rows, 0:rows]),
                  r=[R_xin[b], R_const], w=[R_ps[bank]])
            srcv = ps[:, bank, :].rearrange("p (a b) -> p a b", b=128)[:, :, 0:rows]
            dstv = xT[:, cg * 4:(cg + 1) * 4, tk * 128:tk * 128 + rows]
            if cg % 2:
                C("act", acp(dstv, srcv), r=[R_ps[bank]], w=[R_x])
            else:
                C("dve", cp(dstv, srcv), r=[R_ps[bank]], w=[R_x])
    P.barrier()

    def rmsnorm(v, out3=None, R_out=None, out_is_x=False):
        for c in range(KC):
            i = c % 2
            C("act", actf(sq_t[i], xT[:, c, :], AF.Square), r=[R_x], w=[R_sq[i]])
            for ti, (c0, w) in enumerate(TT):
                C("pe", mm(ps[:, ti, 0:w], ones_f[:], sq_t[i][:, c0:c0 + w], c == 0, c == KC - 1), r=[R_sq[i], R_const], w=[R_ps[ti]])
        C("act", actf(segv(rs_t), ps_seg(0), AF.Sqrt, bias=eps_t[:], scale=1.0 / D), r=[R_ps[0], R_ps[1], R_const], w=[R_rs])
        C("act", actf(smpv(rs_t), ps_smp(0), AF.Sqrt, bias=eps_t[:], scale=1.0 / D), r=[R_ps[2], R_const], w=[R_rs])
        C("dve", lambda e: e.reciprocal(out=rs_t, in_=rs_t), r=[R_rs], w=[R_rs])
        for c in range(KC):
            if out_is_x:
                C("dve", stt(xT[:, c, :], xT[:, c, :], pvc(v, c), rs_t, ALU.mult, ALU.mult), r=[R_x, R_rs, R_pv], w=[R_x])
            else:
                C("dve", stt(xn[:, c, :], xT[:, c, :], pvc(v, c), rs_t, ALU.mult, ALU.mult), r=[R_x, R_rs, R_pv], w=[R_xn])

    def resid_add(m, s):
        C("dve", tt(segv(xT[:, m, :]), ps_seg(s), segv(xT[:, m, :]), ALU.add), r=[R_ps[3 * s], R_ps[3 * s + 1]], w=[R_x])
        C("dve", tt(smpv(xT[:, m, :]), ps_smp(s), smpv(xT[:, m, :]), ALU.add), r=[R_ps[3 * s + 2]], w=[R_x])

    hT = ab(0, 12 * T).rearrange("p (f t) -> p f t", t=T)
    R_hT = Res("hT")
    sg_t = [ab(6336 + i * 528, T) for i in range(2)]
    FPARTS = (6, 6, 5, 5)
    R_sg = [Res("sg0"), Res("sg1")]

    def ffn_blocks(l):
        blocks = []
        b0 = 0
        for nbk in FPARTS:
            for fb in range(nbk):
                c0 = (b0 + fb) * 256
                blocks.append(wblk(f_g[l], 0, KC, c0))
                blocks.append(wblk(f_u[l], 0, KC, c0))
            for mb in range(8):
                blocks.append(wblk(f_d[l], b0 * 256, 2 * nbk, mb * 256))
            b0 += nbk
        return blocks

    def ffn(l):
        rmsnorm(V_NFFN + l)
        for nbk in FPARTS:
            for fb in range(nbk):
                bg = ws_take()
                bu = ws_take()
                for mi in range(2):
                    f = fb * 2 + mi
                    sgi = f % 2
                    s0 = next_set()
                    mm_chunk(bg, KC, mi, xn, R_xn, s0)
                    s1 = next_set()
                    mm_chunk(bu, KC, mi, xn, R_xn, s1)
                    C("act", actf(segv(sg_t[sgi]), ps_seg(s0), AF.Silu), r=[R_ps[3 * s0], R_ps[3 * s0 + 1]], w=[R_sg[sgi]])
                    C("act", actf(smpv(sg_t[sgi]), ps_smp(s0), AF.Silu), r=[R_ps[3 * s0 + 2]], w=[R_sg[sgi]])
                    C("dve", tt(segv(hT[:, f, :]), ps_seg(s1), segv(sg_t[sgi]), ALU.mult), r=[R_ps[3 * s1], R_ps[3 * s1 + 1], R_sg[sgi]], w=[R_hT])
                    C("dve", tt(smpv(hT[:, f, :]), ps_smp(s1), smpv(sg_t[sgi]), ALU.mult), r=[R_ps[3 * s1 + 2], R_sg[sgi]], w=[R_hT])
                ws_release(bg)
                ws_release(bu)
            for mb in range(8):
                bd = ws_take()
                for mi in range(2):
                    s = next_set()
                    mm_chunk(bd, 2 * nbk, mi, hT, R_hT, s)
                    resid_add(mb * 2 + mi, s)
                ws_release(bd)
        P.barrier()

    UBW = 1088
    ub = af(0, 2 * UBW).rearrange("p (m w) -> p m w", w=UBW); R_ub = [Res("ub0"), Res("ub1")]
    gt = ab(2176, 2 * T).rearrange("p (m w) -> p m w", w=T); R_gt = [Res("gt0"), Res("gt1")]
    uc = af(3232, 2 * T).rearrange("p (m w) -> p m w", w=T); R_uc = [Res("uc0"), Res("uc1")]
    ucb = ab(5344, 2 * T).rearrange("p (m w) -> p m w", w=T); R_ucb = Res("ucb")
    t1, t2, t3, t4 = (af(6400 + i * T, T) for i in range(4))
    R_t = [Res("t%d" % i) for i in range(4)]
    X1b = [ab(10624 + i * 528, T) for i in range(2)]; R_X1 = [Res("X1b0"), Res("X1b1")]
    X2b = [ab(11680 + i * 512, 1024) for i in range(2)]; R_X2 = [Res("X2b0"), Res("X2b1")]
    wab = ab(12704, 1024).rearrange("p (a k m) -> p a k m", a=2, k=2); R_wab = Res("wab")
    tmp8 = sb("tmp8", [128, 8]); R_tmp8 = Res("tmp8")
    xh = ab(0, KC * 6).rearrange("p (c w) -> p c w", w=6); R_xh = Res("xh")

    def ubseg(mi):
        return ub[:, mi, 0:1030].rearrange("p (s w) -> p s w", w=515)

    def ubsmp(mi):
        return ub[:, mi, 1030:1086].rearrange("p (s w) -> p s w", w=7)

    def smp84(ap1d):
        return ap1d.rearrange("p (s t) -> p s t", t=4)

    def rec_blocks(l):
        blocks = [wblk(w_x[l], 0, KC, nb * 256) for nb in range(8)]
        for n in range(8):
            blocks.append(wblk(w_x[l], 0, KC, n * 256))
            blocks.append(wblk(w_g[l], 0, KC, n * 256))
        blocks += [wblk(w_o[l], 0, KC, mb * 256) for mb in range(8)]
        return blocks

    def recurrent(l):
        rmsnorm(V_NMIX + l)
        in_tm(stc[l].rearrange("s t f -> (s t) f"), 24, hsmp[:], R_small["hsmp"])
        in_tm(sth[l], 8, h0s[:], R_small["h0s"])
        C("dve", cp(xh[:, :, 0:3], xn[:, :, 509:512]), r=[R_xn], w=[R_xh])
        C("dve", cp(xh[:, :, 3:6], xn[:, :, 1021:1024]), r=[R_xn], w=[R_xh])
        for nb in range(8):
            blk = ws_take()
            for mi in range(2):
                m = nb * 2 + mi
                for k in range(KC):
                    C("pe", mm(ps[:, 6, m * 6:(m + 1) * 6], blk[1][:, k, mi * 128:(mi + 1) * 128], xh[:, k, :], k == 0, k == KC - 1),
                      r=[R_wb[blk[0]], R_xh], w=[R_ps[6]])
            ws_release(blk)
        C("dve", cp(halo_own[:].rearrange("p c s t -> p c (s t)"), ps[:, 6, 0:96].rearrange("p (c w) -> p c w", w=6)), r=[R_ps[6]], w=[R_small["halo_own"]])
        DMA("sp", dma(halo_in_d, halo_own[:].rearrange("p c s t -> p (c s t)")), r=[R_small["halo_own"]], w=[R_hd_in])
        P.CC(lambda e: e.collective_compute("AllGather", ALU.bypass, replica_groups=GROUPS4, ins=[halo_in_d.opt()], outs=[halo_g_d.opt()]),
             r=[R_hd_in], w=[R_hd_g])
        DMA("sp", dma(halo_all[:], halo_g_d.rearrange("(r p) f -> p r f", p=128)), r=[R_hd_g], w=[R_small["halo_all"]])
        hav = halo_all[:].rearrange("p r (c s t) -> p r c s t", s=2, t=3)
        for s in range(2):
            first = True
            for rk in range(4):
                for s2 in range(2):
                    idx = s * 8 + rk * 2 + s2
                    if first:
                        C("dve", ts(halo_sel[:, :, s, :], hav[:, rk, :, s2, :], coef[:, idx:idx + 1], ALU.mult), r=[R_small["halo_all"], R_const], w=[R_small["halo_sel"]])
                        first = False
                    else:
                        C("dve", stt(halo_sel[:, :, s, :], hav[:, rk, :, s2, :], coef[:, idx:idx + 1], halo_sel[:, :, s, :], ALU.mult, ALU.add),
                          r=[R_small["halo_all"], R_const], w=[R_small["halo_sel"]])
        out_fm(hav[:, 0, :, 1, :].rearrange("p c t -> p t c"), 3, conv_p[l].rearrange("t (c p) -> (t c) p", p=128), R_small["halo_all"])
        for n in range(8):
            DMA("pool", dma(wab[:, 0, :, :], w_a[l, n].rearrange("(k p) m -> p k m", p=128)), w=[R_wab])
            DMA("pool", dma(wab[:, 1, :, :], w_i[l, n].rearrange("(k p) m -> p k m", p=128)), w=[R_wab])
            bx = ws_take()
            bgt = ws_take()
            su = [next_set(), next_set()]
            for mi in range(2):
                c = 2 * n + mi
                mm_chunk(bx, KC, mi, xn, R_xn, su[mi])
                s = su[mi]
                C("act", acp(ubseg(mi)[:, :, 3:515], ps_seg(s)), r=[R_ps[3 * s], R_ps[3 * s + 1]], w=[R_ub[mi]])
                C("act", acp(ubsmp(mi)[:, :, 3:7], smp84(ps_smp(s))), r=[R_ps[3 * s + 2]], w=[R_ub[mi]])
                C("dve", cp(ubseg(mi)[:, :, 0:3], halo_sel[:, c, :, :]), r=[R_small["halo_sel"]], w=[R_ub[mi]])
                C("dve", cp(ubsmp(mi)[:, :, 0:3], hsmp[:, c, :].rearrange("p (s t) -> p s t", t=3)), r=[R_small["hsmp"]], w=[R_ub[mi]])
                C("dve", cp(csmp[:, c, :].rearrange("p (s t) -> p s t", t=3), ubsmp(mi)[:, :, 4:7]), r=[R_ub[mi]], w=[R_small["csmp"]])
                for (uv, ov) in ((lambda j, mi=mi: ubseg(mi)[:, :, 3 - j:515 - j], segv(uc[:, mi, :])),
                                 (lambda j, mi=mi: ubsmp(mi)[:, :, 3 - j:7 - j], smp84(smpv(uc[:, mi, :])))):
                    C("act", actf(ov, uv(0), AF.Identity, bias=pvc(V_CB + l, c), scale=pvc(V_CW + 4 * l + 0, c)), r=[R_ub[mi], R_pv], w=[R_uc[mi]])
                    for j in range(1, 4):
                        C("dve", stt(ov, uv(j), pvc(V_CW + 4 * l + j, c), ov, ALU.mult, ALU.add), r=[R_ub[mi], R_pv], w=[R_uc[mi]])
                C("act", acp(ucb[:, mi, :], uc[:, mi, :]), r=[R_uc[mi]], w=[R_ucb])
            ws_release(bx)
            for mi in range(2):
                s = next_set()
                mm_chunk(bgt, KC, mi, xn, R_xn, s)
                for (pv_, ov, tv) in ((ps_seg(s), segv(gt[:, mi, :]), segv(t4)), (ps_smp(s), smpv(gt[:, mi, :]), smpv(t4))):
                    rr = [R_ps[3 * s], R_ps[3 * s + 1], R_ps[3 * s + 2]]
                    C("act", actf(tv, pv_, AF.Square, scale=math.sqrt(0.044715)), r=rr, w=[R_t[3]])
                    C("dve", stt(tv, tv, 1.0, pv_, ALU.add, ALU.mult), r=rr, w=[R_t[3]])
                    C("act", actf(tv, tv, AF.Sigmoid, scale=1.5957691216057308), r=[R_t[3]], w=[R_t[3]])
                    C("dve", tt(ov, tv, pv_, ALU.mult), r=rr + [R_t[3]], w=[R_gt[mi]])
            ws_release(bgt)
            for mi in range(2):
                c = 2 * n + mi
                par = c % 2
                sr = next_set()
                si = next_set()
                for (gi, s) in ((0, sr), (1, si)):
                    for kk in range(2):
                        for ti, (c0, w) in enumerate(TT):
                            C("pe", mm(ps[:, 3 * s + ti, 0:w], wab[:, gi, kk, mi * 128:(mi + 1) * 128], ucb[:, kk, c0:c0 + w], kk == 0, kk == 1),
                              r=[R_wab, R_ucb], w=[R_ps[3 * s + ti]])
                C("act", actf(segv(t1), ps_seg(sr), AF.Sigmoid, bias=pvc(V_BA + l, c)), r=[R_ps[3 * sr], R_ps[3 * sr + 1], R_pv], w=[R_t[0]])
                C("act", actf(smpv(t1), ps_smp(sr), AF.Sigmoid, bias=pvc(V_BA + l, c)), r=[R_ps[3 * sr + 2], R_pv], w=[R_t[0]])
                C("act", actf(segv(t3), ps_seg(si), AF.Sigmoid, bias=pvc(V_BI + l, c)), r=[R_ps[3 * si], R_ps[3 * si + 1], R_pv], w=[R_t[2]])
                C("act", actf(smpv(t3), ps_smp(si), AF.Sigmoid, bias=pvc(V_BI + l, c)), r=[R_ps[3 * si + 2], R_pv], w=[R_t[2]])
                C("act", actf(t2, t1, AF.Exp, scale=nsp[:, l, 1, c:c + 1]), r=[R_t[0], R_small["nsp"]], w=[R_t[1]])
                C("act", actf(t1, t1, AF.Exp, scale=nsp[:, l, 0, c:c + 1]), r=[R_t[0], R_small["nsp"]], w=[R_t[0]])
                C("act", actf(t2, t2, AF.Sqrt, bias=one_t[:], scale=-1.0), r=[R_t[1], R_const], w=[R_t[1]])
                C("dve", tt(t2, t2, t3, ALU.mult), r=[R_t[1], R_t[2]], w=[R_t[1]])
                C("dve", tt(t2, t2, uc[:, mi, :], ALU.mult), r=[R_t[1], R_uc[mi]], w=[R_t[1]])
                C("dve", tt(tmp8[:], smp84(smpv(t1))[:, :, 0], h0s[:, c, :], ALU.mult), r=[R_t[0], R_small["h0s"]], w=[R_tmp8])
                C("dve", tt(smp84(smpv(t2))[:, :, 0], smp84(smpv(t2))[:, :, 0], tmp8[:], ALU.add), r=[R_tmp8, R_t[1]], w=[R_t[1]])
                C("dve", mset(smp84(smpv(t1))[:, :, 0], 0.0), r=[R_tmp8], w=[R_t[0]])
                for sg_ in range(2):
                    sl = slice(sg_ * 512, (sg_ + 1) * 512)
                    C("dve", lambda e, sl=sl: e.tensor_tensor_scan(out=t3[:, sl], data0=t1[:, sl], data1=t2[:, sl], initial=0.0, op0=ALU.mult, op1=ALU.add),
                      r=[R_t[0], R_t[1]], w=[R_t[2]])
                    C("dve", lambda e, sl=sl: e.tensor_tensor_scan(out=t4[:, sl], data0=t1[:, sl], data1=zeros_b[:], initial=1.0, op0=ALU.mult, op1=ALU.add),
                      r=[R_t[0], R_const], w=[R_t[3]])
                C("dve", lambda e: e.tensor_tensor_scan(out=smpv(t3), data0=smpv(t1), data1=smpv(t2), initial=0.0, op0=ALU.mult, op1=ALU.add),
                  r=[R_t[0], R_t[1]], w=[R_t[2]])
                C("dve", cp(car[:, 0, c, :], t4[:, 511:1024:512]), r=[R_t[3]], w=[R_small["car"]])
                C("dve", cp(car[:, 1, c, :], t3[:, 511:1024:512]), r=[R_t[2]], w=[R_small["car"]])
                C("dve", cp(hlast_s[:, c, :], smp84(smpv(t3))[:, :, 3]), r=[R_t[2]], w=[R_small["hlast_s"]])
                C("dve", tt(X1b[par], t3, gt[:, mi, :], ALU.mult), r=[R_t[2], R_gt[mi]], w=[R_X1[par]])
                C("dve", tt(X2b[par], t4[:, 0:1024], gt[:, mi, 0:1024], ALU.mult), r=[R_t[3], R_gt[mi]], w=[R_X2[par]])
                DMA("sp", dma(x1_d[:, c, :], X1b[par]), r=[R_X1[par]], w=[R_x1d])
                DMA("sp", dma(x2_d[:, c, :], X2b[par]), r=[R_X2[par]], w=[R_x2d])
        out_fm(hlast_s[:].rearrange("p c s -> p s c"), 8, h_s[l].rearrange("s (c p) -> (s c) p", p=128), R_small["hlast_s"])
        out_fm(csmp[:].rearrange("p c st -> p st c"), 24, conv_s[l].rearrange("s t (c p) -> (s t c) p", p=128), R_small["csmp"])
        DMA("sp", dma(car_in_d, car[:].rearrange("p a c s -> p (a c s)")), r=[R_small["car"]], w=[R_cd_in])
        P.CC(lambda e: e.collective_compute("AllGather", ALU.bypass, replica_groups=GROUPS4, ins=[car_in_d.opt()], outs=[car_g_d.opt()]),
             r=[R_cd_in], w=[R_cd_g])
        DMA("sp", dma(car_all, car_g_d.rearrange("(r p) f -> p r f", p=128)), r=[R_cd_g], w=[R_small["car_all"]])
        cav = car_all.rearrange("p r (a c s) -> p r a c s", a=2, s=2)
        C("dve", mset(Hc[:, 0, :], 0.0), w=[R_small["Hc"]])
        for g in range(8):
            rk, s2 = (g, 0) if g < 4 else (7 - g, 1)
            C("dve", tt(Hc[:, g + 1, :], cav[:, rk, 0, :, s2], Hc[:, g, :], ALU.mult), r=[R_small["car_all"]], w=[R_small["Hc"]])
            C("dve", tt(Hc[:, g + 1, :], Hc[:, g + 1, :], cav[:, rk, 1, :, s2], ALU.add), r=[R_small["car_all"]], w=[R_small["Hc"]])
        for s in range(2):
            for g in range(8):
                idx = 16 + s * 8 + g
                if g == 0:
                    C("dve", ts(Hin[:, :, s], Hc[:, g, :], coef[:, idx:idx + 1], ALU.mult), r=[R_small["Hc"], R_const], w=[R_small["Hin"]])
                else:
                    C("dve", stt(Hin[:, :, s], Hc[:, g, :], coef[:, idx:idx + 1], Hin[:, :, s], ALU.mult, ALU.add), r=[R_small["Hc"], R_const], w=[R_small["Hin"]])
        out_fm(Hc[:, 8:9, :], 1, h_p[l:l + 1, :].rearrange("o (c p) -> (o c) p", p=128), R_small["Hc"])
        for c in range(KC):
            par = c % 2
            DMA("sp", dma(X1b[par], x1_d[:, c, :]), r=[R_x1d], w=[R_X1[par]])
            DMA("sp", dma(X2b[par], x2_d[:, c, :]), r=[R_x2d], w=[R_X2[par]])
            for s in range(2):
                sl = slice(s * 512, (s + 1) * 512)
                C("dve", stt(xn[:, c, sl], X2b[par][:, sl], Hin[:, c, s:s + 1], X1b[par][:, sl], ALU.mult, ALU.add),
                  r=[R_X1[par], R_X2[par], R_small["Hin"]], w=[R_xn])
            C("dve", cp(xn[:, c, 1024:T], X1b[par][:, 1024:T]), r=[R_X1[par]], w=[R_xn])
        for mb in range(8):
            blk = ws_take()
            for mi in range(2):
                s = next_set()
                mm_chunk(blk, KC, mi, xn, R_xn, s)
                resid_add(mb * 2 + mi, s)
            ws_release(blk)
        P.barrier()

    kf_t = [af(i * 256, 256) for i in range(2)]; R_kf = [Res("kf0"), Res("kf1")]
    kr_t = af(512, 256); R_kr = Res("kr")
    kb_t = [ab(768 + i * 128, 256) for i in range(2)]; R_kb = [Res("kb0"), Res("kb1")]
    kTh = ab(1024, 2 * T).rearrange("p (h t) -> p h t", t=T); R_kTh = Res("kTh")
    QT = ab(4096, NH * T).rearrange("p (h t) -> p h t", t=T); R_QT = Res("QT")
    psb = [ps[:, b, :].bitcast(BF16) for b in range(8)]
    tmcnt = [0]

    def proj_tm(blocks_cols, W2d, rope, sink):
        for nb, c0 in enumerate(blocks_cols):
            blk = ws_take()
            for tk in range(9):
                rows = 128 if tk < 8 else NSMP
                tc0 = tk * 128
                bank = tmcnt[0] % 2
                tmcnt[0] += 1
                for k in range(KC):
                    C("pe", mm(ps[0:rows, bank, 0:256], xn[:, k, tc0:tc0 + rows], blk[1][:, k, :], k == 0, k == KC - 1),
                      r=[R_wb[blk[0]], R_xn], w=[R_ps[bank]])
                sink(nb, tk, rows, bank)
            ws_release(blk)

    def rope_evac(tk, rows, bank, i):
        psv = ps[0:rows, bank, 0:256].rearrange("p (g d) -> p g d", d=64)
        kfv = kf_t[i][0:rows, :].rearrange("p (g d) -> p g d", d=64)
        krv = kr_t[0:rows, :].rearrange("p (g d) -> p g d", d=64)
        sn = sin32[0:rows, tk:tk + 1, :].broadcast_to([rows, 4, 32])
        for hf in range(2):
            cbh = cos32[0:rows, tk:tk + 1, :].broadcast_to([rows, 4, 32])
            C("dve", tt(kfv[:, :, hf * 32:(hf + 1) * 32], psv[:, :, hf * 32:(hf + 1) * 32], cbh, ALU.mult), r=[R_ps[bank], R_const], w=[R_kf[i]])
        C("dve", stt(krv[:, :, 0:32], psv[:, :, 32:64], -1.0, sn, ALU.mult, ALU.mult), r=[R_ps[bank], R_const], w=[R_kr])
        C("dve", tt(krv[:, :, 32:64], psv[:, :, 0:32], sn, ALU.mult), r=[R_ps[bank], R_const], w=[R_kr])
        C("dve", tt(kf_t[i][0:rows, :], kf_t[i][0:rows, :], kr_t[0:rows, :], ALU.add), r=[R_kr], w=[R_kf[i]])
        C("act", acp(kb_t[i][0:rows, :], kf_t[i][0:rows, :]), r=[R_kf[i]], w=[R_kb[i]])

    def transpose_heads(tk, rows, i, dst3, R_dst, h0):
        tb = 2 + (tk % 2)
        for hh in range(2):
            C("pe", tr(psb[tb][:, hh * 128:hh * 128 + rows], kb_t[i][0:rows, hh * 128:(hh + 1) * 128], ident_b[0:rows, 0:rows]),
              r=[R_kb[i], R_const], w=[R_ps[tb]])
        C("act", acp(dst3[:, h0:h0 + 2, tk * 128:tk * 128 + rows], psb[tb][:, 0:256].rearrange("p (h t) -> p h t", t=128)[:, :, 0:rows]),
          r=[R_ps[tb]], w=[R_dst])

    vnew4 = ab(2560, 4096).rearrange("p (s h d) -> p s h d", h=4, d=128)
    vlv = [v_loc[q].rearrange("(h s p) (i d) -> p h s i d", s=2, p=128, d=128) for q in range(4)]

    def kv_phase():
        rmsnorm(V_NKV)
        C("dve", mset(vnew_sel[:], 1.0), w=[R_small["vnew_sel"]])
        cnt = [0]

        def sink(nb, tk, rows, bank):
            i = cnt[0] % 2
            cnt[0] += 1
            if nb < 8:
                rope_evac(tk, rows, bank, i)
                dst = k_p[tk * 128:(tk + 1) * 128, nb * 256:(nb + 1) * 256] if tk < 8 else k_s[:, nb * 256:(nb + 1) * 256]
                DMA("sp", dma(dst, kf_t[i][0:rows, :]), r=[R_kf[i]])
                transpose_heads(tk, rows, i, kTh, R_kTh, 0)
                if tk == 8:
                    DMA("sp", dma(kT_loc[nb // 2][(nb % 2) * 256:(nb % 2 + 1) * 256, :].rearrange("(h p) t -> p h t", p=128), kTh[:, :, 0:1024]), r=[R_kTh], w=[R_kTl])
                    C("dve", cp(ksT[:, 2 * nb:2 * nb + 2, :], kTh[:, :, 1024:T]), r=[R_kTh], w=[R_small["ksT"]])
            else:
                vb_ = nb - 8
                C("act", acp(kf_t[i][0:rows, :], ps[0:rows, bank, 0:256]), r=[R_ps[bank]], w=[R_kf[i]])
                C("dve", cp(kb_t[i][0:rows, :], kf_t[i][0:rows, :]), r=[R_kf[i]], w=[R_kb[i]])
                if tk < 8:
                    DMA("sp", dma(v_p[tk * 128:(tk + 1) * 128, vb_ * 256:(vb_ + 1) * 256], kf_t[i][0:rows, :]), r=[R_kf[i]])
                    DMA("sp", dma(vlv[vb_ // 2][:, 2 * (vb_ % 2):2 * (vb_ % 2) + 2, tk // 4, tk % 4, :], kb_t[i][:, :].rearrange("p (h d) -> p h d", d=128)), r=[R_kb[i]], w=[R_vl])
                else:
                    DMA("sp", dma(v_s[:, vb_ * 256:(vb_ + 1) * 256], kf_t[i][0:rows, :]), r=[R_kf[i]])
                    for hh in range(2):
                        for hx in range(2):
                            idx = 32 + hh * 16 + 2 * vb_ + hx
                            srcv = kb_t[i][0:NSMP, hx * 128:(hx + 1) * 128]
                            if vb_ == 0 and hx == 0:
                                C("dve", ts(vnew_sel[:, hh, 0:128], srcv, coef[0:NSMP, idx:idx + 1], ALU.mult), r=[R_kb[i], R_const], w=[R_small["vnew_sel"]])
                            else:
                                C("dve", stt(vnew_sel[:, hh, 0:128], srcv, coef[0:NSMP, idx:idx + 1], vnew_sel[:, hh, 0:128], ALU.mult, ALU.add),
                                  r=[R_kb[i], R_const], w=[R_small["vnew_sel"]])

        proj_tm([nb * 256 for nb in range(16)], w_kv, True, sink)
        for q in range(4):
            P.CC(lambda e, q=q: e.collective_compute("AllGather", ALU.bypass, replica_groups=GROUPS4, ins=[kT_loc[q].opt()], outs=[kT_g[q].opt()]), r=[R_kTl], w=[R_kTg])
            P.CC(lambda e, q=q: e.collective_compute("AllGather", ALU.bypass, replica_groups=GROUPS4, ins=[v_loc[q].opt()], outs=[v_g[q].opt()]), r=[R_vl], w=[R_vg])
        for hh in range(2):
            for h in range(NH):
                idx = 32 + hh * 16 + h
                if h == 0:
                    C("dve", ts(ksT_sel[:, hh, :], ksT[:, h, :], coef[:, idx:idx + 1], ALU.mult), r=[R_small["ksT"], R_const], w=[R_small["ksT_sel"]])
                else:
                    C("dve", stt(ksT_sel[:, hh, :], ksT[:, h, :], coef[:, idx:idx + 1], ksT_sel[:, hh, :], ALU.mult, ALU.add),
                      r=[R_small["ksT"], R_const], w=[R_small["ksT_sel"]])
        P.barrier()

    kp_t = [ab(i * 512, 512) for i in range(4)]; vp_t = [ab(i * 512 + 256, 512).rearrange("p (i d) -> p i d", d=128) for i in range(4)]
    R_kp = [Res("kp%d" % i) for i in range(4)]; R_vp = [Res("vp%d" % i) for i in range(4)]
    pt_t = [ab(2048 + i * 256, 512) for i in range(4)]; R_pt = [Res("pt%d" % i) for i in range(4)]
    f0 = af(3072, 512); f1 = af(3584, 512); R_f = [Res("f0"), Res("f1")]
    qs_t = sb("qs_t", [128, 2, NSMP]); R_qs = Res("qs")
    kTgv = [kT_g[q].rearrange("(r h p) t -> r h p t", h=4, p=128) for q in range(4)]
    kTlv = [kT_loc[q].rearrange("(h p) t -> h p t", p=128) for q in range(4)]
    vgv = [v_g[q].rearrange("(r h s p) (i d) -> r h s p i d", h=4, s=2, p=128, d=128) for q in range(4)]
    vllv = [v_loc[q].rearrange("(h s p) (i d) -> h s p i d", s=2, p=128, d=128) for q in range(4)]

    def attn_blocks(j):
        return [wblk(w_q[j], 0, KC, nb * 256) for nb in range(8)] + [wblk(w_do[j], 0, KC, mb * 256) for mb in range(8)]

    def attention(j):
        l = j + 2
        rmsnorm(V_NMIX + l)
        cnt = [0]

        def sink(nb, tk, rows, bank):
            i = cnt[0] % 2
            cnt[0] += 1
            rope_evac(tk, rows, bank, i)
            transpose_heads(tk, rows, i, QT, R_QT, 2 * nb)

        proj_tm([nb * 256 for nb in range(8)], w_q[j], True, sink)
        for hh in range(2):
            for h in range(NH):
                idx = 32 + hh * 16 + h
                if h == 0:
                    C("dve", ts(qs_t[:, hh, :], QT[:, h, 1024:T], coef[:, idx:idx + 1], ALU.mult), r=[R_QT, R_const], w=[R_qs])
                else:
                    C("dve", stt(qs_t[:, hh, :], QT[:, h, 1024:T], coef[:, idx:idx + 1], qs_t[:, hh, :], ALU.mult, ALU.add), r=[R_QT, R_const], w=[R_qs])
        C("dve", mset(qbd[:], 0.0), w=[R_small["qbd"]])
        for hh in range(2):
            C("dve", cp(qbd[0:64, hh, 0:32].rearrange("p (t s) -> p t s", s=8), qs_t[0:64, hh, :].rearrange("p (s t) -> p t s", t=4)), r=[R_qs], w=[R_small["qbd"]])
            C("dve", cp(qbd[64:128, hh, 32:64].rearrange("p (t s) -> p t s", s=8), qs_t[64:128, hh, :].rearrange("p (s t) -> p t s", t=4)), r=[R_qs], w=[R_small["qbd"]])
        P.barrier()
        LA = 2
        ucount = [0]
        spiece = [0]
        sample_setup()
        pc = [0]
        ptc = [0]
        pending = []

        def finalize(h, qb):
            qcols = slice(qb * 512, (qb + 1) * 512)
            C("dve", lambda e: e.reciprocal(out=f0, in_=ps[:, 2, :]), r=[R_ps[2]], w=[R_f[0]])
            C("dve", tt(f0, ps[:, 0, :], f0, ALU.mult), r=[R_ps[0]], w=[R_f[0]])
            C("dve", lambda e: e.reciprocal(out=f1, in_=ps[:, 3, :]), r=[R_ps[3]], w=[R_f[1]])
            C("dve", tt(f1, ps[:, 1, :], f1, ALU.mult), r=[R_ps[1]], w=[R_f[1]])
            C("dve", stt(f0, f1, lamt[:, j, 1:2], f0, ALU.mult, ALU.add), r=[R_f[1], R_small["lam"]], w=[R_f[0]])
            C("act", actf(f1, f0, AF.Square), r=[R_f[0]], w=[R_f[1]])
            C("pe", mm(ps[:, 7, :], ones_f[:], f1, True, True), r=[R_f[1], R_const], w=[R_ps[7]])
            C("act", actf(f1, ps[:, 7, :], AF.Sqrt, bias=eps_t[:], scale=1.0 / 128), r=[R_ps[7], R_const], w=[R_f[1]])
            C("dve", lambda e: e.reciprocal(out=f1, in_=f1), r=[R_f[1]], w=[R_f[1]])
            C("dve", stt(xn[:, h, qcols], f0, gsub[:, j:j + 1], f1, ALU.mult, ALU.mult), r=[R_f[0], R_f[1], R_small["gsub"]], w=[R_xn])

        def emit_av(u):
            (h, qb, sl, c, i, pi, first, last, fin) = u
            C("pe", mm(ps[:, c, :], vp_t[sl][:, i, :], pt_t[pi], first, last), r=[R_vp[sl], R_pt[pi]], w=[R_ps[c]])
            C("pe", mm(ps[:, 2 + c, :], ones_b[:], pt_t[pi], first, last), r=[R_const, R_pt[pi]], w=[R_ps[2 + c]])
            if fin:
                finalize(h, qb)

        def flush(keep):
            while len(pending) > keep:
                emit_av(pending.pop(0))

        for h in range(NH):
            for qb in range(2):
                qcols = slice(qb * 512, (qb + 1) * 512)
                plist = [("g", g) for g in range(3 if qb == 0 else 7)] + [("d", qb)]
                nun = len(plist) * 4
                un = [0, 0]
                for (kind, g) in plist:
                    sl = pc[0] % 4
                    pc[0] += 1
                    if kind == "g":
                        rk, s2 = (g, 0) if g < 4 else (7 - g, 1)
                        DMA("sp", dma(kp_t[sl], kTgv[h // 4][rk, h % 4, :, s2 * 512:(s2 + 1) * 512]), r=[R_kTg], w=[R_kp[sl]])
                        DMA("sp", dma(vp_t[sl], vgv[h // 4][rk, h % 4, s2, :, :, :]), r=[R_vg], w=[R_vp[sl]])
                        bias_ap = ebias[:, qb * 8 + g:qb * 8 + g + 1]
                    else:
                        DMA("sp", dma(kp_t[sl], kTlv[h // 4][h % 4, :, qb * 512:(qb + 1) * 512]), r=[R_kTl], w=[R_kp[sl]])
                        DMA("sp", dma(vp_t[sl], vllv[h // 4][h % 4, qb, :, :, :]), r=[R_vl], w=[R_vp[sl]])
                        bias_ap = ebias[:, 15:16]
                    for i in range(4):
                        for c in range(2):
                            rs_ = slice(c * 64, (c + 1) * 64)
                            sb_ = 4 + (ptc[0] % 2)
                            pi = ptc[0] % 4
                            ptc[0] += 1
                            C("pe", mm(ps[:, sb_, :], kp_t[sl][rs_, i * 128:(i + 1) * 128], QT[rs_, h, qcols], True, True),
                              r=[R_kp[sl], R_QT], w=[R_ps[sb_]])
                            C("act", actf(pt_t[pi], ps[:, sb_, :], AF.Exp, bias=bias_ap, scale=0.125), r=[R_ps[sb_], R_const], w=[R_pt[pi]])
                            if kind == "d":
                                C("dve", tt(pt_t[pi], pt_t[pi], mtab[:, 512 - 128 * i:1024 - 128 * i], ALU.mult), r=[R_const], w=[R_pt[pi]])
                            first = un[c] == 0
                            last = un[c] == nun - 1
                            un[c] += 1
                            pending.append((h, qb, sl, c, i, pi, first, last, last and c == 1))
                            flush(LA)
                            ucount[0] += 1
                            if ucount[0] % 5 == 0 and spiece[0] < NPIECE:
                                sample_step(spiece[0])
                                spiece[0] += 1
        flush(0)
        while spiece[0] < NPIECE:
            sample_step(spiece[0])
            spiece[0] += 1
        P.barrier()
        sample_finish(j)
        for mb in range(8):
            blk = ws_take()
            for mi in range(2):
                s = next_set()
                mm_chunk(blk, KC, mi, xn, R_xn, s)
                resid_add(mb * 2 + mi, s)
            ws_release(blk)
        P.barrier()

    SA = 12544
    kpc = [ab(SA + i * 512, 1024).rearrange("p (g h t) -> p g h t", h=2, t=128) for i in range(2)]; R_kpc = [Res("kpc0"), Res("kpc1")]
    vpc = [ab(SA + 1024 + i * 528, 1056).rearrange("p (g h d) -> p g h d", h=2, d=132) for i in range(2)]
    R_vpc = [[Res("vpc00"), Res("vpc01")], [Res("vpc10"), Res("vpc11")]]
    pexp = [ab(SA + 2080 + i * 256, 512).rearrange("p (h w) -> p h w", w=256) for i in range(2)]; R_pexp = [Res("pexp0"), Res("pexp1")]
    Et = [ab(SA + 2592 + i * 16, 32) for i in range(2)]; R_E = [Res("E0"), Res("E1")]
    vm_t = [ab(SA + 2624 + i * 16, 32) for i in range(2)]; R_vm = [Res("vm0"), Res("vm1")]
    pnew = ab(SA + 2656, 128).rearrange("p (h w) -> p h w", w=64); R_pnew = Res("pnew")
    oacc = af(SA + 2720, 264).rearrange("p (h d) -> p h d", d=132); R_oacc = Res("oacc")
    onrm = af(SA + 2984, 256).rearrange("p (h d) -> p h d", d=128); R_onrm = Res("onrm")
    od_t = af(SA + 3240, 256).rearrange("p (h d) -> p h d", d=128); R_od = Res("od")
    osq = af(SA + 3496, 256).rearrange("p (h d) -> p h d", d=128); R_osq = Res("osq")
    ss_t = af(SA + 3752, 4); R_ss = Res("ss")
    osend = af(SA + 3756, 64); R_osend = Res("osend")
    oall = af(0, 512).rearrange("p (r w) -> p r w", w=64); R_oall = Res("oall")
    ckv = ckT.rearrange("g p h t -> p g h t")
    cvv = cv.rearrange("g p h d -> p g h d")
    NPIECE = 320
    ptb4 = pt_f[:, :].unsqueeze(1).broadcast_to([128, 4, 8])
    iob4 = iota8[:, 0:4].unsqueeze(2).broadcast_to([128, 4, 8])
    pb7 = ps[0:64, 7, 0:264].rearrange("p (h d) -> p h d", d=132)[:, :, 0:129]

    def sample_setup():
        for i in range(2):
            C("dve", mset(vpc[i][:], 1.0), w=R_vpc[i])

    def sample_av(pc_):
        bj = pc_ % 2
        for hh in range(2):
            for g in range(4):
                C("pe", mm(ps[0:64, 7, hh * 132:hh * 132 + 129], pexp[bj][:, hh, g * 64:(g + 1) * 64], vpc[bj][:, g, hh, 0:129], g == 0, g == 3),
                  r=[R_pexp[bj], R_vpc[bj][hh]], w=[R_ps[7]])
        if pc_ == 0:
            C("dve", cp(oacc[0:64, :, 0:129], pb7), r=[R_ps[7]], w=[R_oacc])
        else:
            C("dve", tt(oacc[0:64, :, 0:129], oacc[0:64, :, 0:129], pb7, ALU.add), r=[R_ps[7]], w=[R_oacc])

    def sample_step(piece):
        bi = piece % 2
        g0 = piece * 4
        DMA("pool", dma(kpc[bi], ckv[:, g0:g0 + 4, :, :]), w=[R_kpc[bi]])
        for hh in range(2):
            DMA("pool", dma(vpc[bi][:, :, hh, 0:128], cvv[:, g0:g0 + 4, hh, :]), w=[R_vpc[bi][hh]])
        C("dve", stt(Et[bi].rearrange("p (a s) -> p a s", s=8), ptb4, float(g0), iob4, ALU.subtract, ALU.is_equal), r=[R_small["pt"], R_const], w=[R_E[bi]])
        C("pe", mm(ps[:, 7, 448:480], ones_b[:], Et[bi], True, True), r=[R_E[bi], R_const], w=[R_ps[7]])
        C("act", acp(vm_t[bi], ps[:, 7, 448:480]), r=[R_ps[7]], w=[R_vm[bi]])
        for hh in range(2):
            for g in range(4):
                C("pe", mm(ps[:, 6, hh * 256 + g * 64:hh * 256 + (g + 1) * 64], kpc[bi][:, g, hh, :], qbd[:, hh, :], True, True),
                  r=[R_kpc[bi], R_small["qbd"]], w=[R_ps[6]])
        C("act", actf(pexp[bi], ps[:, 6, :].rearrange("p (h w) -> p h w", w=256), AF.Exp, scale=0.125), r=[R_ps[6]], w=[R_pexp[bi]])
        vmb = vm_t[bi].rearrange("p (a s) -> p a s", s=8).unsqueeze(2).broadcast_to([128, 4, 8, 8])
        for hh in range(2):
            pv4 = pexp[bi][:, hh, :].rearrange("p (a c s) -> p a c s", c=8, s=8)
            C("dve", tt(pv4, pv4, vmb, ALU.mult), r=[R_vm[bi]], w=[R_pexp[bi]])
        if piece > 0:
            sample_av(piece - 1)

    def sample_finish(j):
        sample_av(NPIECE - 1)
        for hh in range(2):
            C("pe", mm(ps[0:32, 6, hh * 64:(hh + 1) * 64], ksT_sel[:, hh, :], qbd[:, hh, :], True, True), r=[R_small["ksT_sel"], R_small["qbd"]], w=[R_ps[6]])
        C("act", actf(pnew[0:32, :, :], ps[0:32, 6, 0:128].rearrange("p (h w) -> p h w", w=64), AF.Exp, scale=0.125), r=[R_ps[6]], w=[R_pnew])
        for hh in range(2):
            C("dve", tt(pnew[0:32, hh, :], pnew[0:32, hh, :], smask[:], ALU.mult), r=[R_const], w=[R_pnew])
            C("pe", mm(ps[0:64, 7, hh * 132:hh * 132 + 129], pnew[0:32, hh, :], vnew_sel[:, hh, 0:129], True, True), r=[R_pnew, R_small["vnew_sel"]], w=[R_ps[7]])
        C("dve", tt(oacc[0:64, :, 0:129], oacc[0:64, :, 0:129], pb7, ALU.add), r=[R_ps[7]], w=[R_oacc])
        for hh in range(2):
            C("dve", lambda e, hh=hh: e.reciprocal(out=ss_t[0:64, hh:hh + 1], in_=oacc[0:64, hh, 128:129]), r=[R_oacc], w=[R_ss])
            C("dve", ts(onrm[0:64, hh, :], oacc[0:64, hh, 0:128], ss_t[0:64, hh:hh + 1], ALU.mult), r=[R_oacc, R_ss], w=[R_onrm])
        C("dve", cp(osq[0:32, :, :], onrm[32:64, :, :]), r=[R_onrm], w=[R_osq])
        C("dve", stt(od_t[0:32, :, :], osq[0:32, :, :], lamt[0:32, j, 1:2], onrm[0:32, :, :], ALU.mult, ALU.add), r=[R_onrm, R_osq, R_small["lam"]], w=[R_od])
        C("dve", tt(osq[0:32, :, :], od_t[0:32, :, :], od_t[0:32, :, :], ALU.mult), r=[R_od], w=[R_osq])
        C("dve", lambda e: e.tensor_reduce(out=ss_t[0:32, 2:4], in_=osq[0:32, :, :], axis=AX.X, op=ALU.add), r=[R_osq], w=[R_ss])
        C("act", actf(ss_t[0:32, 2:4], ss_t[0:32, 2:4], AF.Sqrt, bias=eps_t[0:32, :], scale=1.0 / 128), r=[R_ss, R_const], w=[R_ss])
        C("dve", lambda e: e.reciprocal(out=ss_t[0:32, 2:4], in_=ss_t[0:32, 2:4]), r=[R_ss], w=[R_ss])
        for hh in range(2):
            C("dve", stt(osq[0:32, hh, :], od_t[0:32, hh, :], ss_t[0:32, 2 + hh:3 + hh], gsub_row[:, j, :], ALU.mult, ALU.mult),
              r=[R_od, R_ss, R_small["gsub"]], w=[R_osq])
        for hh in range(2):
            C("pe", tr(ps[:, 7, 128 + hh * 32:128 + (hh + 1) * 32], osq[0:32, hh, :], ident_f[0:32, 0:32]), r=[R_osq, R_const], w=[R_ps[7]])
        C("dve", cp(osend.rearrange("p (h s t) -> p h s t", h=2, t=4), ps[:, 7, 128:192].rearrange("p (h t s) -> p h s t", h=2, t=4)), r=[R_ps[7]], w=[R_osend])
        DMA("sp", dma(os_in_d, osend), r=[R_osend], w=[R_os_in])
        P.CC(lambda e: e.collective_compute("AllGather", ALU.bypass, replica_groups=GROUPS8, ins=[os_in_d.opt()], outs=[os_g_d.opt()]), r=[R_os_in], w=[R_os_g])
        DMA("sp", dma(oall, os_g_d.rearrange("(r p) f -> p r f", p=128)), r=[R_os_g], w=[R_oall])
        C("dve", cp(xn[:, :, 1024:T], oall.rearrange("p r (h w) -> p (r h) w", w=NSMP)), r=[R_oall], w=[R_xn])

    ytok = [af(i * 2048, 2048) for i in range(2)]
    R_ytok = [Res("ytok0"), Res("ytok1")]

    def final_out():
        rmsnorm(V_NFIN, out_is_x=True)
        for tk in range(9):
            rows = 128 if tk < 8 else NSMP
            b = tk % 2
            for cg in range(4):
                bank = 6 + (cg % 2)
                for cc in range(4):
                    c = cg * 4 + cc
                    C("pe", tr(ps[0:rows, bank, cc * 128:(cc + 1) * 128], xT[:, c, tk * 128:tk * 128 + rows], ident_f[:]), r=[R_x, R_const], w=[R_ps[bank]])
                if cg % 2:
                    C("act", acp(ytok[b][0:rows, cg * 512:(cg + 1) * 512], ps[0:rows, bank, :]), r=[R_ps[bank]], w=[R_ytok[b]])
                else:
                    C("dve", cp(ytok[b][0:rows, cg * 512:(cg + 1) * 512], ps[0:rows, bank, :]), r=[R_ps[bank]], w=[R_ytok[b]])
            dst = y_p[tk * 128:(tk + 1) * 128, :] if tk < 8 else y_s
            DMA("sp", dma(dst, ytok[b][0:rows, :]), r=[R_ytok[b]])

    phases = []
    for l in range(2):
        phases.append((rec_blocks(l), lambda l=l: recurrent(l)))
        phases.append((ffn_blocks(l), lambda l=l: ffn(l)))
    phases.append(([wblk(w_kv, 0, KC, nb * 256) for nb in range(16)], kv_phase))
    for j in range(2):
        phases.append((attn_blocks(j), lambda j=j: attention(j)))
        phases.append((ffn_blocks(j + 2), lambda j=j: ffn(j + 2)))
    phases = [phases[i] for i in PHASE_SEL] if PHASE_SEL is not None else phases[:NPHASE]
    ws_queue(phases[0][0])
    for i, (blks, fn) in enumerate(phases):
        if i + 1 < len(phases):
            ws_pending.extend(phases[i + 1][0])
        fn()
    if NPHASE >= 9:
        final_out()
    P.barrier()
    return nc, P, es


def emit_program(nc, P, es):
    sig = P.emit(nc, None)
    sem_eng = {e: es.enter_context(nc.semaphore("s_" + e)) for e in ENGS}
    sem_dma = {}
    for q, n in P.NDMA.items():
        for i in range(n):
            sem_dma[(q, i)] = es.enter_context(nc.semaphore("d_%s%d" % (q, i)))
    sem_dma["cc"] = es.enter_context(nc.semaphore("cc"))

    def run(eh, ename):
        waited = {}
        for node in P.ops[ename]:
            for d in sorted(node.deps, key=str):
                if d[0] == "c":
                    E = d[1]
                    if E == ename and (E == "pe" or not SAME_ENGINE_SYNC):
                        continue
                    sem = sem_eng[E]
                    v = sig[E][d[2]]
                    key = E
                else:
                    sem = sem_dma[d[1]]
                    v = d[2]
                    key = d[1]
                if waited.get(key, 0) < v:
                    eh.wait_ge(sem, v)
                    waited[key] = v
            if node.kind == "b":
                continue
            ins = node.fn(eh)
            if node.kind == "c":
                if node.signal:
                    ins.then_inc(sem_eng[ename], 1)
            elif node.kind == "d":
                ins.then_inc(sem_dma[node.sem], 16)
            else:
                ins.then_inc(sem_dma["cc"])

    with nc.Block() as block:
        @block.tensor
        def _(e):
            run(e, "pe")

        @block.scalar
        def _(e):
            run(e, "act")

        @block.vector
        def _(e):
            run(e, "dve")

        @block.gpsimd
        def _(e):
            run(e, "pool")

        @block.sync
        def _(e):
            run(e, "sp")


NPHASE = 9
PHASE_SEL = None
_CACHE = {}


def _get_program():
    if "nc" not in _CACHE:
        nc, P, es = build_program()
        emit_program(nc, P, es)
        _CACHE["nc"] = nc
        _CACHE["es"] = es
    return _CACHE["nc"]


def _host_tables(r):
    b, j = r // 4, r % 4
    g_seg = (j, 7 - j)
    pos = np.zeros((9, 128), np.float32)
    for tk in range(8):
        g = g_seg[tk // 4]
        pos[tk] = g * 512 + (tk % 4) * 128 + np.arange(128)
    pos[8] = PAST + (np.arange(128) % 4)
    inv = (10000.0 ** (-np.arange(32, dtype=np.float32) / 32)).astype(np.float32)
    ang = (pos[:, :, None].astype(np.float32) * inv[None, None, :]).astype(np.float32)
    cos32 = np.ascontiguousarray(np.cos(ang).astype(np.float32).transpose(1, 0, 2))
    sin32 = np.ascontiguousarray(np.sin(ang).astype(np.float32).transpose(1, 0, 2))
    coef = np.zeros((64,), np.float32)
    for s in range(2):
        for rk in range(4):
            for s2 in range(2):
                blk = rk if s2 == 0 else 7 - rk
                if blk == g_seg[s] - 1:
                    coef[s * 8 + rk * 2 + s2] = 1.0
        coef[16 + s * 8 + g_seg[s]] = 1.0
    for hh in range(2):
        coef[32 + hh * 16 + 2 * r + hh] = 1.0
    ebias = np.zeros((16,), np.float32)
    for qb in range(2):
        for g in range(7):
            ebias[qb * 8 + g] = 0.0 if g < g_seg[qb] else -30000.0
    return cos32, sin32, np.tile(coef[None, :], (128, 1)), np.tile(ebias[None, :], (128, 1))


def kernel(x_prompt, x_sample, cache_k, cache_v, page_table, state_conv, state_rglru,
           norm_mix, norm_ffn, norm_final, rg_w_x, rg_w_gate, rg_conv_w, rg_conv_b,
           rg_w_a, rg_b_a, rg_w_i, rg_b_i, rg_lambda, rg_w_out, kv_norm, w_kv,
           dif_w_q, dif_lq1, dif_lk1, dif_lq2, dif_lk2, dif_subln, dif_w_o,
           ffn_w_gate, ffn_w_up, ffn_w_down):
    f32 = lambda a: np.ascontiguousarray(np.asarray(a, dtype=np.float32))
    x_prompt = f32(x_prompt); x_sample = f32(x_sample)
    cache_k = np.asarray(cache_k, dtype=np.float32); cache_v = np.asarray(cache_v, dtype=np.float32)
    nc = _get_program()
    vec_list = [norm_mix[i] for i in range(4)] + [norm_ffn[i] for i in range(4)] + [norm_final, kv_norm]
    vec_list += [rg_conv_w[l, jj] for l in range(2) for jj in range(4)]
    vec_list += [rg_conv_b[l] for l in range(2)] + [rg_b_a[l] for l in range(2)] + [rg_b_i[l] for l in range(2)] + [rg_lambda[l] for l in range(2)]
    vecs = np.concatenate([f32(v).reshape(16, 128) for v in vec_list], axis=0)
    assert vecs.shape[0] == NV * 16
    kq = np.arange(128)
    xx = np.arange(1024) - 512
    maskd = (xx[None, :] >= kq[:, None]).astype(np.float32)
    smaskd = np.zeros((32, 64), np.float32)
    for s_ in range(8):
        for t_ in range(4):
            for q in range(4):
                if t_ <= q:
                    for c_ in range(2):
                        smaskd[s_ * 4 + t_, c_ * 32 + q * 8 + s_] = 1.0
    common = {
        "xs": f32(x_sample.reshape(NSMP, D)), "vecs": vecs, "maskd": maskd, "smaskd": smaskd,
        "ptab": np.ascontiguousarray(np.asarray(page_table, dtype=np.int32).reshape(8, NPG).T),
        "state_conv": f32(state_conv), "state_rglru": f32(state_rglru),
        "rg_w_x": f32(rg_w_x), "rg_w_gate": f32(rg_w_gate), "rg_w_out": f32(rg_w_out),
        "rg_w_a": f32(rg_w_a), "rg_w_i": f32(rg_w_i), "w_kv": f32(w_kv),
        "dif_w_q": f32(dif_w_q), "dif_w_o": f32(dif_w_o),
        "lqk": f32(np.stack([dif_lq1, dif_lk1, dif_lq2, dif_lk2], axis=0)),
        "dif_subln": f32(dif_subln),
        "ffn_w_gate": f32(ffn_w_gate), "ffn_w_up": f32(ffn_w_up), "ffn_w_down": f32(ffn_w_down),
        "identd": np.eye(128, dtype=np.float32),
        "iotad": np.tile(np.arange(8, dtype=np.float32)[None, :], (128, 1)),
    }
    in_maps = []
    for r in range(8):
        b, j = r // 4, r % 4
        cos32, sin32, coef, ebias = _host_tables(r)
        xp = np.concatenate([x_prompt[b, j * 512:(j + 1) * 512], x_prompt[b, (7 - j) * 512:(8 - j) * 512]], axis=0)
        ck = cache_k[:, :, 2 * r:2 * r + 2]
        ckT = np.ascontiguousarray(ck.transpose(0, 3, 4, 2, 1)).reshape(1280, 128, 2, 128)
        cvr = np.ascontiguousarray(cache_v[:, :, 2 * r:2 * r + 2, :])
        m = dict(common)
        m.update({"xp": f32(xp), "cos32": cos32, "sin32": sin32, "coefd": coef, "ebiasd": ebias, "ckT": ckT, "cv": cvr})
        in_maps.append(m)
    res = run_bass_kernel_spmd(nc, in_maps, core_ids=list(range(8)))
    R = res.results
    y_prompt = np.zeros((2, 4096, D), np.float32)
    k_prompt = np.zeros((2, 4096, D), np.float32)
    v_prompt = np.zeros((2, 4096, D), np.float32)
    for r in range(8):
        b, j = r // 4, r % 4
        for (dst, key) in ((y_prompt, "y_p"), (k_prompt, "k_p"), (v_prompt, "v_p")):
            dst[b, j * 512:(j + 1) * 512] = R[r][key][0:512]
            dst[b, (7 - j) * 512:(8 - j) * 512] = R[r][key][512:1024]
    conv_prompt = np.stack([R[0]["conv_p"], R[4]["conv_p"]], axis=1).astype(np.float32)
    h_prompt = np.stack([R[0]["h_p"], R[4]["h_p"]], axis=1).astype(np.float32)
    return (y_prompt,
            np.asarray(R[0]["y_s"], np.float32).reshape(8, 4, D),
            k_prompt.reshape(2, 4096, NH, 2, 64),
            v_prompt.reshape(2, 4096, NH, 128),
            conv_prompt, h_prompt,
            np.asarray(R[0]["k_s"], np.float32).reshape(8, 4, NH, 2, 64),
            np.asarray(R[0]["v_s"], np.float32).reshape(8, 4, NH, 128),
            np.asarray(R[0]["conv_s"], np.float32), np.asarray(R[0]["h_s"], np.float32))
```

```python
import math
from contextlib import ExitStack

import numpy as np
import concourse.bass as bass
import concourse.mybir as mybir
from concourse.bass_utils import run_bass_kernel_spmd

F32 = mybir.dt.float32
BF16 = mybir.dt.bfloat16
I32 = mybir.dt.int32
ALU = mybir.AluOpType
AF = mybir.ActivationFunctionType
AX = mybir.AxisListType

D = 2048
KC = 16
SEG = 512
NSMP = 32
T = 2 * SEG + NSMP
FF = 5632
NH = 16
EPS = 1e-6
PAST = 16384
NPG = 128
TT = ((0, 512), (512, 512), (1024, 32))
ENGS = ("pe", "act", "dve", "pool", "sp")
SAME_ENGINE_SYNC = True

V_NMIX, V_NFFN, V_NFIN, V_NKV, V_CW, V_CB, V_BA, V_BI, V_LAM = 0, 4, 8, 9, 10, 18, 20, 22, 24
NV = 26


class Res:
    __slots__ = ("name", "w", "r")

    def __init__(self, name):
        self.name = name
        self.w = None
        self.r = {}


class Node:
    __slots__ = ("eng", "fn", "deps", "kind", "signal", "sem", "val")

    def __init__(self, eng, fn, deps, kind):
        self.eng, self.fn, self.deps, self.kind = eng, fn, deps, kind
        self.signal = False
        self.sem = None
        self.val = 0


class Prog:
    NDMA = {"sp": 20, "pool": 20, "act": 6}

    def __init__(self):
        self.ops = {e: [] for e in ENGS}
        self.dma_rr = {q: 0 for q in self.NDMA}
        self.dma_cnt = {}
        self.cc_cnt = 0
        self.uid = 0
        self.dma_events = []

    def _deps(self, reads, writes):
        deps = set()
        for r in reads:
            if r.w is not None:
                deps.add(r.w)
        for w in writes:
            if w.w is not None:
                deps.add(w.w)
            deps.update(w.r.values())
        return deps

    def _commit(self, ev, key, reads, writes):
        for w in writes:
            w.w = ev
            w.r = {}
        for r in reads:
            if r not in writes:
                r.r[key] = ev

    def C(self, eng, fn, r=(), w=()):
        deps = self._deps(r, w)
        idx = len(self.ops[eng])
        self.ops[eng].append(Node(eng, fn, deps, "c"))
        ev = ("c", eng, idx)
        self._commit(ev, eng, r, w)
        return ev

    def DMA(self, q, fn, r=(), w=()):
        deps = self._deps(r, w)
        i = self.dma_rr[q]
        self.dma_rr[q] = (i + 1) % self.NDMA[q]
        n = self.dma_cnt.get((q, i), 0)
        if n > 0:
            deps.add(("d", (q, i), n * 16))
        node = Node(q, fn, deps, "d")
        node.sem = (q, i)
        node.val = (n + 1) * 16
        self.dma_cnt[(q, i)] = n + 1
        self.ops[q].append(node)
        ev = ("d", (q, i), node.val)
        self.uid += 1
        self._commit(ev, ("d", self.uid), r, w)
        self.dma_events.append(ev)
        return ev

    def CC(self, fn, r=(), w=()):
        deps = self._deps(r, w)
        node = Node("pool", fn, deps, "cc")
        self.cc_cnt += 1
        node.sem = "cc"
        node.val = self.cc_cnt
        self.ops["pool"].append(node)
        ev = ("d", "cc", node.val)
        self.uid += 1
        self._commit(ev, ("d", self.uid), r, w)
        self.dma_events.append(ev)
        return ev

    def barrier(self):
        last = {}
        for e in ENGS:
            for i in range(len(self.ops[e]) - 1, -1, -1):
                if self.ops[e][i].kind == "c":
                    last[e] = ("c", e, i)
                    break
        deps = set(last.values()) | set(self.dma_events)
        self.dma_events = []
        for e in ENGS:
            self.ops[e].append(Node(e, None, set(deps), "b"))

    def emit(self, nc, block_engs):
        for e in ENGS:
            for node in self.ops[e]:
                for d in node.deps:
                    if d[0] == "c":
                        if d[1] == e and (e == "pe" or not SAME_ENGINE_SYNC):
                            continue
                        self.ops[d[1]][d[2]].signal = True
        sigcount = {}
        for e in ENGS:
            c = 0
            arr = []
            for node in self.ops[e]:
                if node.kind == "c" and node.signal:
                    c += 1
                arr.append(c)
            sigcount[e] = arr
        self.sigcount = sigcount
        return sigcount


def build_program(nseq_groups=2):
    nc = bass.Bass("TRN2", target_bir_lowering=False)
    P = Prog()
    es = ExitStack()

    def din(name, shape, dt=F32):
        return nc.dram_tensor(name, list(shape), dt, kind="ExternalInput").ap()

    def dout(name, shape, dt=F32):
        return nc.dram_tensor(name, list(shape), dt, kind="ExternalOutput").ap()

    def dtmp(name, shape, dt=F32):
        return nc.dram_tensor(name, list(shape), dt).ap()

    def sb(name, shape, dt=F32):
        return es.enter_context(nc.sbuf_tensor(name, list(shape), dt))

    xp = din("xp", [1024, D])
    xs = din("xs", [NSMP, D])
    vecs = din("vecs", [NV * 16, 128])
    cosd = din("cos32", [128, 9, 32])
    sind = din("sin32", [128, 9, 32])
    maskd = din("maskd", [128, 1024])
    coefd = din("coefd", [128, 64])
    smaskd = din("smaskd", [32, 64])
    ptab = din("ptab", [NPG, 8], I32)
    iotad = din("iotad", [128, 8])
    stc = din("state_conv", [2, 8, 3, D])
    sth = din("state_rglru", [2, 8, D])
    ckT = din("ckT", [1280, 128, 2, 128])
    cv = din("cv", [1280, 128, 2, 128])
    w_x = din("rg_w_x", [2, D, D]); w_g = din("rg_w_gate", [2, D, D]); w_o = din("rg_w_out", [2, D, D])
    w_a = din("rg_w_a", [2, 8, 256, 256]); w_i = din("rg_w_i", [2, 8, 256, 256])
    w_kv = din("w_kv", [D, 2 * D])
    w_q = din("dif_w_q", [2, D, D]); w_do = din("dif_w_o", [2, D, D])
    lqk = din("lqk", [4, 2, 64])
    subln = din("dif_subln", [2, 128])
    f_g = din("ffn_w_gate", [4, D, FF]); f_u = din("ffn_w_up", [4, D, FF]); f_d = din("ffn_w_down", [4, FF, D])

    y_p = dout("y_p", [1024, D]); y_s = dout("y_s", [NSMP, D])
    k_p = dout("k_p", [1024, D]); v_p = dout("v_p", [1024, D])
    k_s = dout("k_s", [NSMP, D]); v_s = dout("v_s", [NSMP, D])
    conv_p = dout("conv_p", [2, 3, D]); h_p = dout("h_p", [2, D])
    conv_s = dout("conv_s", [2, 8, 3, D]); h_s = dout("h_s", [2, 8, D])

    halo_in_d = dtmp("halo_in_d", [128, 96]); halo_g_d = dtmp("halo_g_d", [4 * 128, 96])
    car_in_d = dtmp("car_in_d", [128, 64]); car_g_d = dtmp("car_g_d", [4 * 128, 64])
    x1_d = dtmp("x1_d", [128, KC, T], BF16); x2_d = dtmp("x2_d", [128, KC, 1024], BF16)
    kT_loc = [dtmp("kT_loc%d" % q, [4 * 128, 1024], BF16) for q in range(4)]; kT_g = [dtmp("kT_g%d" % q, [4 * 4 * 128, 1024], BF16) for q in range(4)]
    v_loc = [dtmp("v_loc%d" % q, [4 * 2 * 128, 512], BF16) for q in range(4)]; v_g = [dtmp("v_g%d" % q, [4 * 4 * 2 * 128, 512], BF16) for q in range(4)]
    vs_d = dtmp("vs_d", [NSMP, D], BF16)
    os_in_d = dtmp("os_in_d", [128, 64]); os_g_d = dtmp("os_g_d", [8 * 128, 64])

    xT = sb("xT", [128, KC, T]); R_x = Res("xT")
    xn = sb("xn", [128, KC, T], BF16); R_xn = Res("xn")
    AW = 16384
    arena = sb("arena", [128, AW]); arena_b = arena[:].bitcast(BF16)
    wbuf = sb("wbuf", [128, 2, 16 * 256], BF16); R_wb = [Res("wb0"), Res("wb1")]
    ps = es.enter_context(nc.psum_tensor("ps", [128, 8, 512], F32)); R_ps = [Res("ps%d" % i) for i in range(8)]
    pv = sb("pv", [128, NV * 16]); R_pv = Res("pv")
    ident_f = sb("ident_f", [128, 128]); ident_b = sb("ident_b", [128, 128], BF16)
    ones_f = sb("ones_f", [128, 128]); ones_b = sb("ones_b", [128, 128], BF16)
    R_const = Res("const")
    mtab = sb("mtab", [128, 1024], BF16)
    zeros_b = mtab[:, 0:512]
    cos32 = sb("cos32_sb", [128, 9, 32]); sin32 = sb("sin32_sb", [128, 9, 32])
    coef = sb("coef", [128, 64])
    smask = sb("smask", [32, 64])
    pt_i = sb("pt_i", [128, 8], I32); pt_f = sb("pt_f", [128, 8]); iota8 = sb("iota8", [128, 8])
    nsp = sb("nsp", [128, 2, 2, KC])
    lamt = sb("lamt", [128, 2, 4])
    gsub = sb("gsub", [128, 2]); gsub_row = sb("gsub_row", [32, 2, 128])
    lq_sb = arena[:, 1024:1536].rearrange("p (a l d) -> p a l d", a=4, l=2)
    halo_own = sb("halo_own", [128, KC, 2, 3]); halo_sel = sb("halo_sel", [128, KC, 2, 3]); halo_all = sb("halo_all", [128, 4, 96])
    hsmp = sb("hsmp", [128, KC, 24]); h0s = sb("h0s", [128, KC, 8])
    car = sb("car", [128, 2, KC, 2]); car_all = halo_all[:, :, 0:64]; Hc = sb("Hc", [128, 9, KC]); Hin = sb("Hin", [128, KC, 2])
    hlast_s = sb("hlast_s", [128, KC, 8]); csmp = sb("csmp", [128, KC, 24])
    ksT = sb("ksT", [128, NH, NSMP], BF16); ksT_sel = sb("ksT_sel", [128, 2, NSMP], BF16)
    vnew_sel = sb("vnew_sel", [32, 2, 132], BF16)
    qbd = sb("qbd", [128, 2, 64], BF16)
    R_small = {n: Res(n) for n in ("nsp", "lam", "gsub", "halo_own", "halo_sel", "halo_all", "hsmp", "h0s", "car", "car_all",
                                   "Hc", "Hin", "hlast_s", "csmp", "ksT", "ksT_sel", "vnew", "vnew_sel", "qbd", "lq", "pt")}
    R_small["car_all"] = R_small["halo_all"]

    def af(off, n):
        return arena[:, off:off + n]

    def ab(off, n):
        return arena_b[:, 2 * off:2 * off + n]

    NORM_OFF = AW - 3 * T
    sq_t = [af(NORM_OFF + i * T, T) for i in range(2)]; R_sq = [Res("sq0"), Res("sq1")]
    rs_t = af(NORM_OFF + 2 * T, T); R_rs = Res("rs")

    def ps_seg(s):
        return ps[:, 3 * s:3 * s + 2, :]

    def ps_smp(s):
        return ps[:, 3 * s + 2, 0:NSMP]

    def R_set(s):
        return R_ps[3 * s:3 * s + 3]

    def segv(ap2d):
        return ap2d[:, 0:1024].rearrange("p (s w) -> p s w", w=512)

    def smpv(ap2d):
        return ap2d[:, 1024:T]

    C, DMA = P.C, P.DMA

    C("dve", lambda e: e.memset(ones_f[:], 1.0), w=[R_const])
    C("dve", lambda e: e.memset(ones_b[:], 1.0), w=[R_const])
    identd = din("identd", [128, 128])
    DMA("sp", lambda e: e.dma_start(out=ident_f[:], in_=identd), w=[R_const])
    C("dve", lambda e: e.tensor_copy(out=ident_b[:], in_=ident_f[:]), r=[R_const], w=[R_const])
    DMA("sp", lambda e: e.dma_start(out=cos32[:], in_=cosd), w=[R_const])
    DMA("sp", lambda e: e.dma_start(out=sin32[:], in_=sind), w=[R_const])
    DMA("sp", lambda e: e.dma_start(out=coef[:], in_=coefd), w=[R_const])
    DMA("sp", lambda e: e.dma_start(out=smask[:], in_=smaskd), w=[R_const])
    DMA("sp", lambda e: e.dma_start(out=pt_i[:], in_=ptab), w=[R_small["pt"]])
    DMA("sp", lambda e: e.dma_start(out=iota8[:], in_=iotad), w=[R_const])
    C("dve", lambda e: e.tensor_copy(out=pt_f[:], in_=pt_i[:]), r=[R_small["pt"]], w=[R_small["pt"]])
    DMA("pool", lambda e: e.dma_start(out=mtab[:], in_=maskd), w=[R_const])
    vin = af(0, 4 * 128).rearrange("p (a b) -> p a b", b=128)
    R_vin = Res("vin")
    for a in range(4):
        rows = min(128, NV * 16 - a * 128)
        DMA("sp", lambda e, a=a, rows=rows: e.dma_start(out=vin[0:rows, a, :], in_=vecs[a * 128:a * 128 + rows, :]), w=[R_vin])
    for a in range(4):
        rows = min(128, NV * 16 - a * 128)
        C("pe", lambda e, a=a, rows=rows: e.transpose(ps[:, 0, a * 128:a * 128 + rows], vin[0:rows, a, :], ident_f[0:rows, 0:rows]),
          r=[R_vin, R_const], w=[R_ps[0]])
    C("dve", lambda e: e.tensor_copy(out=pv[:], in_=ps[:, 0, 0:NV * 16]), r=[R_ps[0]], w=[R_pv])

    def pvc(v, c):
        return pv[:, v * 16 + c:v * 16 + c + 1]

    lam_v = pv[:, V_LAM * 16:(V_LAM + 2) * 16].rearrange("p (l c) -> p l c", c=KC)
    C("act", lambda e: e.activation(out=nsp[:, :, 0, :], in_=lam_v, func=AF.Exp, scale=-1.0), r=[R_pv], w=[R_small["nsp"]])
    C("act", lambda e: e.activation(out=nsp[:, :, 0, :], in_=nsp[:, :, 0, :], func=AF.Ln, bias=1.0, scale=1.0), r=[R_small["nsp"]], w=[R_small["nsp"]])
    C("dve", lambda e: e.tensor_scalar(out=nsp[:, :, 1, :], in0=nsp[:, :, 0, :], scalar1=-16.0, scalar2=None, op0=ALU.mult), r=[R_small["nsp"]], w=[R_small["nsp"]])
    C("dve", lambda e: e.tensor_scalar(out=nsp[:, :, 0, :], in0=nsp[:, :, 0, :], scalar1=-8.0, scalar2=None, op0=ALU.mult), r=[R_small["nsp"]], w=[R_small["nsp"]])
    DMA("sp", lambda e: e.dma_start(out=arena[:, 1024:1536],
                                     in_=lqk.rearrange("a l d -> (a l d)").partition_broadcast(128)), w=[R_small["lq"]])
    lprod = af(2048, 256).rearrange("p (a l d) -> p a l d", a=2, l=2)
    lsum = af(2304, 4).rearrange("p (a l) -> p a l", a=2)
    R_lt = Res("ltmp")
    for a in range(2):
        C("dve", lambda e, a=a: e.tensor_tensor(out=lprod[:, a, :, :], in0=lq_sb[:, 2 * a, :, :], in1=lq_sb[:, 2 * a + 1, :, :], op=ALU.mult),
          r=[R_small["lq"], R_vin], w=[R_lt])
    C("dve", lambda e: e.tensor_reduce(out=lsum, in_=lprod, axis=AX.X, op=ALU.add), r=[R_lt], w=[R_lt])
    C("act", lambda e: e.activation(out=lsum, in_=lsum, func=AF.Exp), r=[R_lt], w=[R_lt])
    for j in range(2):
        lam_init = 0.8 - 0.6 * math.exp(-0.3 * (j + 2))
        C("dve", lambda e, j=j: e.tensor_tensor(out=lamt[:, j, 0:1], in0=lsum[:, 0, j:j + 1], in1=lsum[:, 1, j:j + 1], op=ALU.subtract),
          r=[R_lt], w=[R_small["lam"]])
        C("dve", lambda e, j=j, li=lam_init: e.tensor_scalar(out=lamt[:, j, 0:1], in0=lamt[:, j, 0:1], scalar1=li, scalar2=None, op0=ALU.add),
          r=[R_small["lam"]], w=[R_small["lam"]])
        C("dve", lambda e, j=j: e.tensor_scalar(out=lamt[:, j, 1:2], in0=lamt[:, j, 0:1], scalar1=-1.0, scalar2=None, op0=ALU.mult),
          r=[R_small["lam"]], w=[R_small["lam"]])
        DMA("sp", lambda e, j=j: e.dma_start(out=gsub[:, j:j + 1], in_=subln[j:j + 1, :].rearrange("o d -> d o")), w=[R_small["gsub"]])
        DMA("sp", lambda e, j=j: e.dma_start(out=gsub_row[:, j, :], in_=subln[j, :].partition_broadcast(32)), w=[R_small["gsub"]])
        C("dve", lambda e, j=j, li=lam_init: e.tensor_scalar(out=gsub[:, j:j + 1], in0=gsub[:, j:j + 1], scalar1=1.0 - li, scalar2=None, op0=ALU.mult),
          r=[R_small["gsub"]], w=[R_small["gsub"]])
        C("dve", lambda e, j=j, li=lam_init: e.tensor_scalar(out=gsub_row[:, j, :], in0=gsub_row[:, j, :], scalar1=1.0 - li, scalar2=None, op0=ALU.mult),
          r=[R_small["gsub"]], w=[R_small["gsub"]])


    def mm(out, lhsT, rhs, st, sp):
        return lambda e: e.matmul(out, lhsT=lhsT, rhs=rhs, start=st, stop=sp)

    def tr(out, in_, idn):
        return lambda e: e.transpose(out, in_, idn)

    def actf(out, in_, func, bias=None, scale=None):
        kw = {}
        if bias is not None:
            kw["bias"] = bias
        if scale is not None:
            kw["scale"] = scale
        return lambda e: e.activation(out=out, in_=in_, func=func, **kw)

    def tt(out, in0, in1, op):
        return lambda e: e.tensor_tensor(out=out, in0=in0, in1=in1, op=op)

    def ts(out, in0, s1, op0, s2=None, op1=None):
        if op1 is None:
            return lambda e: e.tensor_scalar(out=out, in0=in0, scalar1=s1, scalar2=None, op0=op0)
        return lambda e: e.tensor_scalar(out=out, in0=in0, scalar1=s1, scalar2=s2, op0=op0, op1=op1)

    def stt(out, in0, scalar, in1, op0, op1):
        return lambda e: e.scalar_tensor_tensor(out=out, in0=in0, scalar=scalar, in1=in1, op0=op0, op1=op1)

    def cp(out, in_):
        return lambda e: e.tensor_copy(out=out, in_=in_)

    def acp(out, in_):
        return lambda e: e.activation(out=out, in_=in_, func=AF.Copy)

    def dma(out, in_):
        return lambda e: e.dma_start(out=out, in_=in_)

    def mset(ap, v):
        return lambda e: e.memset(ap, v)

    GROUPS4 = [[0, 1, 2, 3], [4, 5, 6, 7]]
    GROUPS8 = [list(range(8))]
    R_hd_in, R_hd_g, R_cd_in, R_cd_g, R_x1d, R_x2d = (Res(n) for n in ("hd_in", "hd_g", "cd_in", "cd_g", "x1d", "x2d"))
    R_kTl, R_kTg, R_vl, R_vg, R_vsd, R_os_in, R_os_g = (Res(n) for n in ("kTl", "kTg", "vl", "vg", "vsd", "os_in", "os_g"))
    eps_t = sb("eps_t", [128, 1])
    C("dve", mset(eps_t[:], EPS), w=[R_const])
    one_t = sb("one_t", [128, 1])
    C("dve", mset(one_t[:], 1.0), w=[R_const])
    ebias = sb("ebias", [128, 16])
    ebiasd = din("ebiasd", [128, 16])
    DMA("sp", dma(ebias[:], ebiasd), w=[R_const])
    P.barrier()

    ws_pending, ws_loaded, ws_free = [], [], [0, 1]

    def ws_pump():
        while ws_free and ws_pending:
            src3, kcn, ncols = ws_pending.pop(0)
            s = ws_free.pop(0)
            dst = wbuf[:, s, 0:kcn * ncols].rearrange("p (k n) -> p k n", n=ncols)
            DMA("pool", dma(dst, src3), w=[R_wb[s]])
            ws_loaded.append((s, dst))

    def ws_queue(blocks):
        ws_pending.extend(blocks)
        ws_pump()

    def ws_take():
        ws_pump()
        return ws_loaded.pop(0)

    def ws_release(blk):
        ws_free.append(blk[0])
        ws_pump()

    def wblk(W2d, r0, kcn, c0, ncols=256):
        return (W2d[r0:r0 + kcn * 128, c0:c0 + ncols].rearrange("(k p) n -> p k n", p=128), kcn, ncols)

    pset = [0]

    def next_set():
        s = pset[0]
        pset[0] ^= 1
        return s

    def mm_chunk(blk, kcn, mm_i, rhs3, R_rhs, s):
        slot, wd = blk
        for k in range(kcn):
            for ti, (c0, w) in enumerate(TT):
                C("pe", mm(ps[:, 3 * s + ti, 0:w], wd[:, k, mm_i * 128:(mm_i + 1) * 128], rhs3[:, k, c0:c0 + w], k == 0, k == kcn - 1),
                  r=[R_wb[slot], R_rhs], w=[R_ps[3 * s + ti]])

    tmpo = sb("tmpo", [128, 384]); R_tmpo = Res("tmpo")
    tmpo2 = sb("tmpo2", [128, 128]); R_tmpo2 = Res("tmpo2")

    def out_fm(src_view, A, dram_rows, R_src):
        n = A * KC
        C("dve", cp(tmpo[:, 0:n].rearrange("p (a c) -> p a c", c=KC), src_view), r=[R_src], w=[R_tmpo])
        for g0 in range(0, n, 128):
            gw = min(128, n - g0)
            C("pe", tr(ps[0:gw, 7, 0:128], tmpo[:, g0:g0 + gw], ident_f[:]), r=[R_tmpo, R_const], w=[R_ps[7]])
            C("dve", cp(tmpo2[0:gw, :], ps[0:gw, 7, 0:128]), r=[R_ps[7]], w=[R_tmpo2])
            DMA("sp", dma(dram_rows[g0:g0 + gw, :], tmpo2[0:gw, :]), r=[R_tmpo2])

    tmpi = af(4096, 2048); R_tmpi = Res("tmpi")

    def in_tm(dram2d, Rr, dst_view, R_dst):
        DMA("sp", dma(tmpi[0:Rr, :], dram2d), w=[R_tmpi])
        for c in range(KC):
            C("pe", tr(ps[:, 7, c * Rr:(c + 1) * Rr], tmpi[0:Rr, c * 128:(c + 1) * 128], ident_f[0:Rr, 0:Rr]), r=[R_tmpi, R_const], w=[R_ps[7]])
        C("dve", cp(dst_view, ps[:, 7, 0:KC * Rr].rearrange("p (c r) -> p c r", r=Rr)), r=[R_ps[7]], w=[R_dst])

    xin = [af(i * 2048, 2048) for i in range(2)]
    R_xin = [Res("xin0"), Res("xin1")]
    for tk in range(9):
        rows = 128 if tk < 8 else NSMP
        src = xp[tk * 128:(tk + 1) * 128, :] if tk < 8 else xs
        b = tk % 2
        DMA("sp", dma(xin[b][0:rows, :], src), w=[R_xin[b]])
        for cg in range(4):
            bank = 6 + (cg % 2)
            for cc in range(4):
                c = cg * 4 + cc
                C("pe", tr(ps[:, bank, cc * 128:cc * 128 + rows], xin[b][0:rows, c * 128:(c + 1) * 128], ident_f[0:rows, 0:rows]),
                  r=[R_xin[b], R_const], w=[R_ps[bank]])
            srcv = ps[:, bank, :].rearrange("p (a b) -> p a b", b=128)[:, :, 0:rows]
            dstv = xT[:, cg * 4:(cg + 1) * 4, tk * 128:tk * 128 + rows]
            if cg % 2:
                C("act", acp(dstv, srcv), r=[R_ps[bank]], w=[R_x])
            else:
                C("dve", cp(dstv, srcv), r=[R_ps[bank]], w=[R_x])
    P.barrier()

    def rmsnorm(v, out3=None, R_out=None, out_is_x=False):
        for c in range(KC):
            i = c % 2
            C("act", actf(sq_t[i], xT[:, c, :], AF.Square), r=[R_x], w=[R_sq[i]])
            for ti, (c0, w) in enumerate(TT):
                C("pe", mm(ps[:, ti, 0:w], ones_f[:], sq_t[i][:, c0:c0 + w], c == 0, c == KC - 1), r=[R_sq[i], R_const], w=[R_ps[ti]])
        C("act", actf(segv(rs_t), ps_seg(0), AF.Sqrt, bias=eps_t[:], scale=1.0 / D), r=[R_ps[0], R_ps[1], R_const], w=[R_rs])
        C("act", actf(smpv(rs_t), ps_smp(0), AF.Sqrt, bias=eps_t[:], scale=1.0 / D), r=[R_ps[2], R_const], w=[R_rs])
        C("dve", lambda e: e.reciprocal(out=rs_t, in_=rs_t), r=[R_rs], w=[R_rs])
        for c in range(KC):
            if out_is_x:
                C("dve", stt(xT[:, c, :], xT[:, c, :], pvc(v, c), rs_t, ALU.mult, ALU.mult), r=[R_x, R_rs, R_pv], w=[R_x])
            else:
                C("dve", stt(xn[:, c, :], xT[:, c, :], pvc(v, c), rs_t, ALU.mult, ALU.mult), r=[R_x, R_rs, R_pv], w=[R_xn])

    def resid_add(m, s):
        C("dve", tt(segv(xT[:, m, :]), ps_seg(s), segv(xT[:, m, :]), ALU.add), r=[R_ps[3 * s], R_ps[3 * s + 1]], w=[R_x])
        C("dve", tt(smpv(xT[:, m, :]), ps_smp(s), smpv(xT[:, m, :]), ALU.add), r=[R_ps[3 * s + 2]], w=[R_x])

    hT = ab(0, 12 * T).rearrange("p (f t) -> p f t", t=T)
    R_hT = Res("hT")
    sg_t = [ab(6336 + i * 528, T) for i in range(2)]
    FPARTS = (6, 6, 5, 5)
    R_sg = [Res("sg0"), Res("sg1")]

    def ffn_blocks(l):
        blocks = []
        b0 = 0
        for nbk in FPARTS:
            for fb in range(nbk):
                c0 = (b0 + fb) * 256
                blocks.append(wblk(f_g[l], 0, KC, c0))
                blocks.append(wblk(f_u[l], 0, KC, c0))
            for mb in range(8):
                blocks.append(wblk(f_d[l], b0 * 256, 2 * nbk, mb * 256))
            b0 += nbk
        return blocks

    def ffn(l):
        rmsnorm(V_NFFN + l)
        for nbk in FPARTS:
            for fb in range(nbk):
                bg = ws_take()
                bu = ws_take()
                for mi in range(2):
                    f = fb * 2 + mi
                    sgi = f % 2
                    s0 = next_set()
                    mm_chunk(bg, KC, mi, xn, R_xn, s0)
                    s1 = next_set()
                    mm_chunk(bu, KC, mi, xn, R_xn, s1)
                    C("act", actf(segv(sg_t[sgi]), ps_seg(s0), AF.Silu), r=[R_ps[3 * s0], R_ps[3 * s0 + 1]], w=[R_sg[sgi]])
                    C("act", actf(smpv(sg_t[sgi]), ps_smp(s0), AF.Silu), r=[R_ps[3 * s0 + 2]], w=[R_sg[sgi]])
                    C("dve", tt(segv(hT[:, f, :]), ps_seg(s1), segv(sg_t[sgi]), ALU.mult), r=[R_ps[3 * s1], R_ps[3 * s1 + 1], R_sg[sgi]], w=[R_hT])
                    C("dve", tt(smpv(hT[:, f, :]), ps_smp(s1), smpv(sg_t[sgi]), ALU.mult), r=[R_ps[3 * s1 + 2], R_sg[sgi]], w=[R_hT])
                ws_release(bg)
                ws_release(bu)
            for mb in range(8):
                bd = ws_take()
                for mi in range(2):
                    s = next_set()
                    mm_chunk(bd, 2 * nbk, mi, hT, R_hT, s)
                    resid_add(mb * 2 + mi, s)
                ws_release(bd)
        P.barrier()

    UBW = 1088
    ub = af(0, 2 * UBW).rearrange("p (m w) -> p m w", w=UBW); R_ub = [Res("ub0"), Res("ub1")]
    gt = ab(2176, 2 * T).rearrange("p (m w) -> p m w", w=T); R_gt = [Res("gt0"), Res("gt1")]
    uc = af(3232, 2 * T).rearrange("p (m w) -> p m w", w=T); R_uc = [Res("uc0"), Res("uc1")]
    ucb = ab(5344, 2 * T).rearrange("p (m w) -> p m w", w=T); R_ucb = Res("ucb")
    t1, t2, t3, t4 = (af(6400 + i * T, T) for i in range(4))
    R_t = [Res("t%d" % i) for i in range(4)]
    X1b = [ab(10624 + i * 528, T) for i in range(2)]; R_X1 = [Res("X1b0"), Res("X1b1")]
    X2b = [ab(11680 + i * 512, 1024) for i in range(2)]; R_X2 = [Res("X2b0"), Res("X2b1")]
    wab = ab(12704, 1024).rearrange("p (a k m) -> p a k m", a=2, k=2); R_wab = Res("wab")
    tmp8 = sb("tmp8", [128, 8]); R_tmp8 = Res("tmp8")
    xh = ab(0, KC * 6).rearrange("p (c w) -> p c w", w=6); R_xh = Res("xh")

    def ubseg(mi):
        return ub[:, mi, 0:1030].rearrange("p (s w) -> p s w", w=515)

    def ubsmp(mi):
        return ub[:, mi, 1030:1086].rearrange("p (s w) -> p s w", w=7)

    def smp84(ap1d):
        return ap1d.rearrange("p (s t) -> p s t", t=4)

    def rec_blocks(l):
        blocks = [wblk(w_x[l], 0, KC, nb * 256) for nb in range(8)]
        for n in range(8):
            blocks.append(wblk(w_x[l], 0, KC, n * 256))
            blocks.append(wblk(w_g[l], 0, KC, n * 256))
        blocks += [wblk(w_o[l], 0, KC, mb * 256) for mb in range(8)]
        return blocks

    def recurrent(l):
        rmsnorm(V_NMIX + l)
        in_tm(stc[l].rearrange("s t f -> (s t) f"), 24, hsmp[:], R_small["hsmp"])
        in_tm(sth[l], 8, h0s[:], R_small["h0s"])
        C("dve", cp(xh[:, :, 0:3], xn[:, :, 509:512]), r=[R_xn], w=[R_xh])
        C("dve", cp(xh[:, :, 3:6], xn[:, :, 1021:1024]), r=[R_xn], w=[R_xh])
        for nb in range(8):
            blk = ws_take()
            for mi in range(2):
                m = nb * 2 + mi
                for k in range(KC):
                    C("pe", mm(ps[:, 6, m * 6:(m + 1) * 6], blk[1][:, k, mi * 128:(mi + 1) * 128], xh[:, k, :], k == 0, k == KC - 1),
                      r=[R_wb[blk[0]], R_xh], w=[R_ps[6]])
            ws_release(blk)
        C("dve", cp(halo_own[:].rearrange("p c s t -> p c (s t)"), ps[:, 6, 0:96].rearrange("p (c w) -> p c w", w=6)), r=[R_ps[6]], w=[R_small["halo_own"]])
        DMA("sp", dma(halo_in_d, halo_own[:].rearrange("p c s t -> p (c s t)")), r=[R_small["halo_own"]], w=[R_hd_in])
        P.CC(lambda e: e.collective_compute("AllGather", ALU.bypass, replica_groups=GROUPS4, ins=[halo_in_d.opt()], outs=[halo_g_d.opt()]),
             r=[R_hd_in], w=[R_hd_g])
        DMA("sp", dma(halo_all[:], halo_g_d.rearrange("(r p) f -> p r f", p=128)), r=[R_hd_g], w=[R_small["halo_all"]])
        hav = halo_all[:].rearrange("p r (c s t) -> p r c s t", s=2, t=3)
        for s in range(2):
            first = True
            for rk in range(4):
                for s2 in range(2):
                    idx = s * 8 + rk * 2 + s2
                    if first:
                        C("dve", ts(halo_sel[:, :, s, :], hav[:, rk, :, s2, :], coef[:, idx:idx + 1], ALU.mult), r=[R_small["halo_all"], R_const], w=[R_small["halo_sel"]])
                        first = False
                    else:
                        C("dve", stt(halo_sel[:, :, s, :], hav[:, rk, :, s2, :], coef[:, idx:idx + 1], halo_sel[:, :, s, :], ALU.mult, ALU.add),
                          r=[R_small["halo_all"], R_const], w=[R_small["halo_sel"]])
        out_fm(hav[:, 0, :, 1, :].rearrange("p c t -> p t c"), 3, conv_p[l].rearrange("t (c p) -> (t c) p", p=128), R_small["halo_all"])
        for n in range(8):
            DMA("pool", dma(wab[:, 0, :, :], w_a[l, n].rearrange("(k p) m -> p k m", p=128)), w=[R_wab])
            DMA("pool", dma(wab[:, 1, :, :], w_i[l, n].rearrange("(k p) m -> p k m", p=128)), w=[R_wab])
            bx = ws_take()
            bgt = ws_take()
            su = [next_set(), next_set()]
            for mi in range(2):
                c = 2 * n + mi
                mm_chunk(bx, KC, mi, xn, R_xn, su[mi])
                s = su[mi]
                C("act", acp(ubseg(mi)[:, :, 3:515], ps_seg(s)), r=[R_ps[3 * s], R_ps[3 * s + 1]], w=[R_ub[mi]])
                C("act", acp(ubsmp(mi)[:, :, 3:7], smp84(ps_smp(s))), r=[R_ps[3 * s + 2]], w=[R_ub[mi]])
                C("dve", cp(ubseg(mi)[:, :, 0:3], halo_sel[:, c, :, :]), r=[R_small["halo_sel"]], w=[R_ub[mi]])
                C("dve", cp(ubsmp(mi)[:, :, 0:3], hsmp[:, c, :].rearrange("p (s t) -> p s t", t=3)), r=[R_small["hsmp"]], w=[R_ub[mi]])
                C("dve", cp(csmp[:, c, :].rearrange("p (s t) -> p s t", t=3), ubsmp(mi)[:, :, 4:7]), r=[R_ub[mi]], w=[R_small["csmp"]])
                for (uv, ov) in ((lambda j, mi=mi: ubseg(mi)[:, :, 3 - j:515 - j], segv(uc[:, mi, :])),
                                 (lambda j, mi=mi: ubsmp(mi)[:, :, 3 - j:7 - j], smp84(smpv(uc[:, mi, :])))):
                    C("act", actf(ov, uv(0), AF.Identity, bias=pvc(V_CB + l, c), scale=pvc(V_CW + 4 * l + 0, c)), r=[R_ub[mi], R_pv], w=[R_uc[mi]])
                    for j in range(1, 4):
                        C("dve", stt(ov, uv(j), pvc(V_CW + 4 * l + j, c), ov, ALU.mult, ALU.add), r=[R_ub[mi], R_pv], w=[R_uc[mi]])
                C("act", acp(ucb[:, mi, :], uc[:, mi, :]), r=[R_uc[mi]], w=[R_ucb])
            ws_release(bx)
            for mi in range(2):
                s = next_set()
                mm_chunk(bgt, KC, mi, xn, R_xn, s)
                for (pv_, ov, tv) in ((ps_seg(s), segv(gt[:, mi, :]), segv(t4)), (ps_smp(s), smpv(gt[:, mi, :]), smpv(t4))):
                    rr = [R_ps[3 * s], R_ps[3 * s + 1], R_ps[3 * s + 2]]
                    C("act", actf(tv, pv_, AF.Square, scale=math.sqrt(0.044715)), r=rr, w=[R_t[3]])
                    C("dve", stt(tv, tv, 1.0, pv_, ALU.add, ALU.mult), r=rr, w=[R_t[3]])
                    C("act", actf(tv, tv, AF.Sigmoid, scale=1.5957691216057308), r=[R_t[3]], w=[R_t[3]])
                    C("dve", tt(ov, tv, pv_, ALU.mult), r=rr + [R_t[3]], w=[R_gt[mi]])
            ws_release(bgt)
            for mi in range(2):
                c = 2 * n + mi
                par = c % 2
                sr = next_set()
                si = next_set()
                for (gi, s) in ((0, sr), (1, si)):
                    for kk in range(2):
                        for ti, (c0, w) in enumerate(TT):
                            C("pe", mm(ps[:, 3 * s + ti, 0:w], wab[:, gi, kk, mi * 128:(mi + 1) * 128], ucb[:, kk, c0:c0 + w], kk == 0, kk == 1),
                              r=[R_wab, R_ucb], w=[R_ps[3 * s + ti]])
                C("act", actf(segv(t1), ps_seg(sr), AF.Sigmoid, bias=pvc(V_BA + l, c)), r=[R_ps[3 * sr], R_ps[3 * sr + 1], R_pv], w=[R_t[0]])
                C("act", actf(smpv(t1), ps_smp(sr), AF.Sigmoid, bias=pvc(V_BA + l, c)), r=[R_ps[3 * sr + 2], R_pv], w=[R_t[0]])
                C("act", actf(segv(t3), ps_seg(si), AF.Sigmoid, bias=pvc(V_BI + l, c)), r=[R_ps[3 * si], R_ps[3 * si + 1], R_pv], w=[R_t[2]])
                C("act", actf(smpv(t3), ps_smp(si), AF.Sigmoid, bias=pvc(V_BI + l, c)), r=[R_ps[3 * si + 2], R_pv], w=[R_t[2]])
                C("act", actf(t2, t1, AF.Exp, scale=nsp[:, l, 1, c:c + 1]), r=[R_t[0], R_small["nsp"]], w=[R_t[1]])
                C("act", actf(t1, t1, AF.Exp, scale=nsp[:, l, 0, c:c + 1]), r=[R_t[0], R_small["nsp"]], w=[R_t[0]])
                C("act", actf(t2, t2, AF.Sqrt, bias=one_t[:], scale=-1.0), r=[R_t[1], R_const], w=[R_t[1]])
                C("dve", tt(t2, t2, t3, ALU.mult), r=[R_t[1], R_t[2]], w=[R_t[1]])
                C("dve", tt(t2, t2, uc[:, mi, :], ALU.mult), r=[R_t[1], R_uc[mi]], w=[R_t[1]])
                C("dve", tt(tmp8[:], smp84(smpv(t1))[:, :, 0], h0s[:, c, :], ALU.mult), r=[R_t[0], R_small["h0s"]], w=[R_tmp8])
                C("dve", tt(smp84(smpv(t2))[:, :, 0], smp84(smpv(t2))[:, :, 0], tmp8[:], ALU.add), r=[R_tmp8, R_t[1]], w=[R_t[1]])
                C("dve", mset(smp84(smpv(t1))[:, :, 0], 0.0), r=[R_tmp8], w=[R_t[0]])
                for sg_ in range(2):
                    sl = slice(sg_ * 512, (sg_ + 1) * 512)
                    C("dve", lambda e, sl=sl: e.tensor_tensor_scan(out=t3[:, sl], data0=t1[:, sl], data1=t2[:, sl], initial=0.0, op0=ALU.mult, op1=ALU.add),
                      r=[R_t[0], R_t[1]], w=[R_t[2]])
                    C("dve", lambda e, sl=sl: e.tensor_tensor_scan(out=t4[:, sl], data0=t1[:, sl], data1=zeros_b[:], initial=1.0, op0=ALU.mult, op1=ALU.add),
                      r=[R_t[0], R_const], w=[R_t[3]])
                C("dve", lambda e: e.tensor_tensor_scan(out=smpv(t3), data0=smpv(t1), data1=smpv(t2), initial=0.0, op0=ALU.mult, op1=ALU.add),
                  r=[R_t[0], R_t[1]], w=[R_t[2]])
                C("dve", cp(car[:, 0, c, :], t4[:, 511:1024:512]), r=[R_t[3]], w=[R_small["car"]])
                C("dve", cp(car[:, 1, c, :], t3[:, 511:1024:512]), r=[R_t[2]], w=[R_small["car"]])
                C("dve", cp(hlast_s[:, c, :], smp84(smpv(t3))[:, :, 3]), r=[R_t[2]], w=[R_small["hlast_s"]])
                C("dve", tt(X1b[par], t3, gt[:, mi, :], ALU.mult), r=[R_t[2], R_gt[mi]], w=[R_X1[par]])
                C("dve", tt(X2b[par], t4[:, 0:1024], gt[:, mi, 0:1024], ALU.mult), r=[R_t[3], R_gt[mi]], w=[R_X2[par]])
                DMA("sp", dma(x1_d[:, c, :], X1b[par]), r=[R_X1[par]], w=[R_x1d])
                DMA("sp", dma(x2_d[:, c, :], X2b[par]), r=[R_X2[par]], w=[R_x2d])
        out_fm(hlast_s[:].rearrange("p c s -> p s c"), 8, h_s[l].rearrange("s (c p) -> (s c) p", p=128), R_small["hlast_s"])
        out_fm(csmp[:].rearrange("p c st -> p st c"), 24, conv_s[l].rearrange("s t (c p) -> (s t c) p", p=128), R_small["csmp"])
        DMA("sp", dma(car_in_d, car[:].rearrange("p a c s -> p (a c s)")), r=[R_small["car"]], w=[R_cd_in])
        P.CC(lambda e: e.collective_compute("AllGather", ALU.bypass, replica_groups=GROUPS4, ins=[car_in_d.opt()], outs=[car_g_d.opt()]),
             r=[R_cd_in], w=[R_cd_g])
        DMA("sp", dma(car_all, car_g_d.rearrange("(r p) f -> p r f", p=128)), r=[R_cd_g], w=[R_small["car_all"]])
        cav = car_all.rearrange("p r (a c s) -> p r a c s", a=2, s=2)
        C("dve", mset(Hc[:, 0, :], 0.0), w=[R_small["Hc"]])
        for g in range(8):
            rk, s2 = (g, 0) if g < 4 else (7 - g, 1)
            C("dve", tt(Hc[:, g + 1, :], cav[:, rk, 0, :, s2], Hc[:, g, :], ALU.mult), r=[R_small["car_all"]], w=[R_small["Hc"]])
            C("dve", tt(Hc[:, g + 1, :], Hc[:, g + 1, :], cav[:, rk, 1, :, s2], ALU.add), r=[R_small["car_all"]], w=[R_small["Hc"]])
        for s in range(2):
            for g in range(8):
                idx = 16 + s * 8 + g
                if g == 0:
                    C("dve", ts(Hin[:, :, s], Hc[:, g, :], coef[:, idx:idx + 1], ALU.mult), r=[R_small["Hc"], R_const], w=[R_small["Hin"]])
                else:
                    C("dve", stt(Hin[:, :, s], Hc[:, g, :], coef[:, idx:idx + 1], Hin[:, :, s], ALU.mult, ALU.add), r=[R_small["Hc"], R_const], w=[R_small["Hin"]])
        out_fm(Hc[:, 8:9, :], 1, h_p[l:l + 1, :].rearrange("o (c p) -> (o c) p", p=128), R_small["Hc"])
        for c in range(KC):
            par = c % 2
            DMA("sp", dma(X1b[par], x1_d[:, c, :]), r=[R_x1d], w=[R_X1[par]])
            DMA("sp", dma(X2b[par], x2_d[:, c, :]), r=[R_x2d], w=[R_X2[par]])
            for s in range(2):
                sl = slice(s * 512, (s + 1) * 512)
                C("dve", stt(xn[:, c, sl], X2b[par][:, sl], Hin[:, c, s:s + 1], X1b[par][:, sl], ALU.mult, ALU.add),
                  r=[R_X1[par], R_X2[par], R_small["Hin"]], w=[R_xn])
            C("dve", cp(xn[:, c, 1024:T], X1b[par][:, 1024:T]), r=[R_X1[par]], w=[R_xn])
        for mb in range(8):
            blk = ws_take()
            for mi in range(2):
                s = next_set()
                mm_chunk(blk, KC, mi, xn, R_xn, s)
                resid_add(mb * 2 + mi, s)
            ws_release(blk)
        P.barrier()

    kf_t = [af(i * 256, 256) for i in range(2)]; R_kf = [Res("kf0"), Res("kf1")]
    kr_t2 = [af(512, 256), af(2080, 256)]; R_kr2 = [Res("kr0"), Res("kr1")]
    kb_t = [ab(768 + i * 128, 256) for i in range(2)]; R_kb = [Res("kb0"), Res("kb1")]
    kTh = ab(1024, 2 * T).rearrange("p (h t) -> p h t", t=T); R_kTh = Res("kTh")
    QT = ab(4096, NH * T).rearrange("p (h t) -> p h t", t=T); R_QT = Res("QT")
    psb = [ps[:, b, :].bitcast(BF16) for b in range(8)]
    tmcnt = [0]

    def proj_tm(blocks_cols, W2d, rope, sink):
        for nb, c0 in enumerate(blocks_cols):
            blk = ws_take()
            for tk in range(9):
                rows = 128 if tk < 8 else NSMP
                tc0 = tk * 128
                bank = tmcnt[0] % 2
                tmcnt[0] += 1
                for k in range(KC):
                    C("pe", mm(ps[0:rows, bank, 0:256], xn[:, k, tc0:tc0 + rows], blk[1][:, k, :], k == 0, k == KC - 1),
                      r=[R_wb[blk[0]], R_xn], w=[R_ps[bank]])
                sink(nb, tk, rows, bank)
            ws_release(blk)

    def rope_evac(tk, rows, bank, i):
        psv = ps[0:rows, bank, 0:256].rearrange("p (g d) -> p g d", d=64)
        kfv = kf_t[i][0:rows, :].rearrange("p (g d) -> p g d", d=64)
        kr_t = kr_t2[i]; R_kr = R_kr2[i]
        krv = kr_t[0:rows, :].rearrange("p (g d) -> p g d", d=64)
        sn = sin32[0:rows, tk:tk + 1, :].broadcast_to([rows, 4, 32])
        cb4 = cos32[0:rows, tk:tk + 1, :].unsqueeze(1).broadcast_to([rows, 4, 2, 32])
        C("dve", tt(kf_t[i][0:rows, :].rearrange("p (g h d) -> p g h d", h=2, d=32), ps[0:rows, bank, 0:256].rearrange("p (g h d) -> p g h d", h=2, d=32), cb4, ALU.mult),
          r=[R_ps[bank], R_const], w=[R_kf[i]])
        C("dve", stt(krv[:, :, 0:32], psv[:, :, 32:64], -1.0, sn, ALU.mult, ALU.mult), r=[R_ps[bank], R_const], w=[R_kr])
        C("dve", tt(krv[:, :, 32:64], psv[:, :, 0:32], sn, ALU.mult), r=[R_ps[bank], R_const], w=[R_kr])
        C("dve", tt(kf_t[i][0:rows, :], kf_t[i][0:rows, :], kr_t[0:rows, :], ALU.add), r=[R_kr], w=[R_kf[i]])
        C("act", acp(kb_t[i][0:rows, :], kf_t[i][0:rows, :]), r=[R_kf[i]], w=[R_kb[i]])

    def transpose_heads(tk, rows, i, dst3, R_dst, h0):
        tb = 2 + (tk % 2)
        for hh in range(2):
            C("pe", tr(psb[tb][:, hh * 128:hh * 128 + rows], kb_t[i][0:rows, hh * 128:(hh + 1) * 128], ident_b[0:rows, 0:rows]),
              r=[R_kb[i], R_const], w=[R_ps[tb]])
        C("act", acp(dst3[:, h0:h0 + 2, tk * 128:tk * 128 + rows], psb[tb][:, 0:256].rearrange("p (h t) -> p h t", t=128)[:, :, 0:rows]),
          r=[R_ps[tb]], w=[R_dst])

    vnew4 = ab(2560, 4096).rearrange("p (s h d) -> p s h d", h=4, d=128)
    vlv = [v_loc[q].rearrange("(h s p) (i d) -> p h s i d", s=2, p=128, d=128) for q in range(4)]

    def kv_phase():
        rmsnorm(V_NKV)
        C("dve", mset(vnew_sel[:], 1.0), w=[R_small["vnew_sel"]])
        cnt = [0]

        def sink(nb, tk, rows, bank):
            i = cnt[0] % 2
            cnt[0] += 1
            if nb < 8:
                rope_evac(tk, rows, bank, i)
                dst = k_p[tk * 128:(tk + 1) * 128, nb * 256:(nb + 1) * 256] if tk < 8 else k_s[:, nb * 256:(nb + 1) * 256]
                DMA("sp", dma(dst, kf_t[i][0:rows, :]), r=[R_kf[i]])
                transpose_heads(tk, rows, i, kTh, R_kTh, 0)
                if tk == 8:
                    DMA("sp", dma(kT_loc[nb // 2][(nb % 2) * 256:(nb % 2 + 1) * 256, :].rearrange("(h p) t -> p h t", p=128), kTh[:, :, 0:1024]), r=[R_kTh], w=[R_kTl])
                    C("dve", cp(ksT[:, 2 * nb:2 * nb + 2, :], kTh[:, :, 1024:T]), r=[R_kTh], w=[R_small["ksT"]])
            else:
                vb_ = nb - 8
                C("act", acp(kf_t[i][0:rows, :], ps[0:rows, bank, 0:256]), r=[R_ps[bank]], w=[R_kf[i]])
                C("dve", cp(kb_t[i][0:rows, :], kf_t[i][0:rows, :]), r=[R_kf[i]], w=[R_kb[i]])
                if tk < 8:
                    DMA("sp", dma(v_p[tk * 128:(tk + 1) * 128, vb_ * 256:(vb_ + 1) * 256], kf_t[i][0:rows, :]), r=[R_kf[i]])
                    DMA("sp", dma(vlv[vb_ // 2][:, 2 * (vb_ % 2):2 * (vb_ % 2) + 2, tk // 4, tk % 4, :], kb_t[i][:, :].rearrange("p (h d) -> p h d", d=128)), r=[R_kb[i]], w=[R_vl])
                else:
                    DMA("sp", dma(v_s[:, vb_ * 256:(vb_ + 1) * 256], kf_t[i][0:rows, :]), r=[R_kf[i]])
                    for hh in range(2):
                        for hx in range(2):
                            idx = 32 + hh * 16 + 2 * vb_ + hx
                            srcv = kb_t[i][0:NSMP, hx * 128:(hx + 1) * 128]
                            if vb_ == 0 and hx == 0:
                                C("dve", ts(vnew_sel[:, hh, 0:128], srcv, coef[0:NSMP, idx:idx + 1], ALU.mult), r=[R_kb[i], R_const], w=[R_small["vnew_sel"]])
                            else:
                                C("dve", stt(vnew_sel[:, hh, 0:128], srcv, coef[0:NSMP, idx:idx + 1], vnew_sel[:, hh, 0:128], ALU.mult, ALU.add),
                                  r=[R_kb[i], R_const], w=[R_small["vnew_sel"]])

        proj_tm([nb * 256 for nb in range(16)], w_kv, True, sink)
        for q in range(4):
            P.CC(lambda e, q=q: e.collective_compute("AllGather", ALU.bypass, replica_groups=GROUPS4, ins=[kT_loc[q].opt()], outs=[kT_g[q].opt()]), r=[R_kTl], w=[R_kTg])
            P.CC(lambda e, q=q: e.collective_compute("AllGather", ALU.bypass, replica_groups=GROUPS4, ins=[v_loc[q].opt()], outs=[v_g[q].opt()]), r=[R_vl], w=[R_vg])
        for hh in range(2):
            for h in range(NH):
                idx = 32 + hh * 16 + h
                if h == 0:
                    C("dve", ts(ksT_sel[:, hh, :], ksT[:, h, :], coef[:, idx:idx + 1], ALU.mult), r=[R_small["ksT"], R_const], w=[R_small["ksT_sel"]])
                else:
                    C("dve", stt(ksT_sel[:, hh, :], ksT[:, h, :], coef[:, idx:idx + 1], ksT_sel[:, hh, :], ALU.mult, ALU.add),
                      r=[R_small["ksT"], R_const], w=[R_small["ksT_sel"]])
        P.barrier()

    kp_t = [ab(i * 512, 512) for i in range(4)]; vp_t = [ab(i * 512 + 256, 512).rearrange("p (i d) -> p i d", d=128) for i in range(4)]
    R_kp = [Res("kp%d" % i) for i in range(4)]; R_vp = [Res("vp%d" % i) for i in range(4)]
    pt_t = [ab(2048 + i * 256, 512) for i in range(4)]; R_pt = [Res("pt%d" % i) for i in range(4)]
    f0 = af(3072, 512); f1 = af(3584, 512); R_f = [Res("f0"), Res("f1")]
    qs_t = sb("qs_t", [128, 2, NSMP]); R_qs = Res("qs")
    kTgv = [kT_g[q].rearrange("(r h p) t -> r h p t", h=4, p=128) for q in range(4)]
    kTlv = [kT_loc[q].rearrange("(h p) t -> h p t", p=128) for q in range(4)]
    vgv = [v_g[q].rearrange("(r h s p) (i d) -> r h s p i d", h=4, s=2, p=128, d=128) for q in range(4)]
    vllv = [v_loc[q].rearrange("(h s p) (i d) -> h s p i d", s=2, p=128, d=128) for q in range(4)]

    def attn_blocks(j):
        return [wblk(w_q[j], 0, KC, nb * 256) for nb in range(8)] + [wblk(w_do[j], 0, KC, mb * 256) for mb in range(8)]

    def attention(j):
        l = j + 2
        rmsnorm(V_NMIX + l)
        cnt = [0]

        def sink(nb, tk, rows, bank):
            i = cnt[0] % 2
            cnt[0] += 1
            rope_evac(tk, rows, bank, i)
            transpose_heads(tk, rows, i, QT, R_QT, 2 * nb)

        proj_tm([nb * 256 for nb in range(8)], w_q[j], True, sink)
        for hh in range(2):
            for h in range(NH):
                idx = 32 + hh * 16 + h
                if h == 0:
                    C("dve", ts(qs_t[:, hh, :], QT[:, h, 1024:T], coef[:, idx:idx + 1], ALU.mult), r=[R_QT, R_const], w=[R_qs])
                else:
                    C("dve", stt(qs_t[:, hh, :], QT[:, h, 1024:T], coef[:, idx:idx + 1], qs_t[:, hh, :], ALU.mult, ALU.add), r=[R_QT, R_const], w=[R_qs])
        C("dve", mset(qbd[:], 0.0), w=[R_small["qbd"]])
        for hh in range(2):
            C("dve", cp(qbd[0:64, hh, 0:32].rearrange("p (t s) -> p t s", s=8), qs_t[0:64, hh, :].rearrange("p (s t) -> p t s", t=4)), r=[R_qs], w=[R_small["qbd"]])
            C("dve", cp(qbd[64:128, hh, 32:64].rearrange("p (t s) -> p t s", s=8), qs_t[64:128, hh, :].rearrange("p (s t) -> p t s", t=4)), r=[R_qs], w=[R_small["qbd"]])
        P.barrier()
        LA = 2
        ucount = [0]
        spiece = [0]
        sample_setup()
        pc = [0]
        ptc = [0]
        pending = []

        def finalize(h, qb):
            qcols = slice(qb * 512, (qb + 1) * 512)
            C("dve", lambda e: e.reciprocal(out=f0, in_=ps[:, 2, :]), r=[R_ps[2]], w=[R_f[0]])
            C("dve", tt(f0, ps[:, 0, :], f0, ALU.mult), r=[R_ps[0]], w=[R_f[0]])
            C("dve", lambda e: e.reciprocal(out=f1, in_=ps[:, 3, :]), r=[R_ps[3]], w=[R_f[1]])
            C("dve", tt(f1, ps[:, 1, :], f1, ALU.mult), r=[R_ps[1]], w=[R_f[1]])
            C("dve", stt(f0, f1, lamt[:, j, 1:2], f0, ALU.mult, ALU.add), r=[R_f[1], R_small["lam"]], w=[R_f[0]])
            C("act", actf(f1, f0, AF.Square), r=[R_f[0]], w=[R_f[1]])
            C("pe", mm(ps[:, 7, :], ones_f[:], f1, True, True), r=[R_f[1], R_const], w=[R_ps[7]])
            C("act", actf(f1, ps[:, 7, :], AF.Sqrt, bias=eps_t[:], scale=1.0 / 128), r=[R_ps[7], R_const], w=[R_f[1]])
            C("dve", lambda e: e.reciprocal(out=f1, in_=f1), r=[R_f[1]], w=[R_f[1]])
            C("dve", stt(xn[:, h, qcols], f0, gsub[:, j:j + 1], f1, ALU.mult, ALU.mult), r=[R_f[0], R_f[1], R_small["gsub"]], w=[R_xn])

        def emit_av(u):
            (h, qb, sl, c, i, pi, first, last, fin) = u
            C("pe", mm(ps[:, c, :], vp_t[sl][:, i, :], pt_t[pi], first, last), r=[R_vp[sl], R_pt[pi]], w=[R_ps[c]])
            C("pe", mm(ps[:, 2 + c, :], ones_b[:], pt_t[pi], first, last), r=[R_const, R_pt[pi]], w=[R_ps[2 + c]])
            if fin:
                finalize(h, qb)

        def flush(keep):
            while len(pending) > keep:
                emit_av(pending.pop(0))

        for h in range(NH):
            for qb in range(2):
                qcols = slice(qb * 512, (qb + 1) * 512)
                plist = [("g", g) for g in range(3 if qb == 0 else 7)] + [("d", qb)]
                nun = len(plist) * 4
                un = [0, 0]
                for (kind, g) in plist:
                    sl = pc[0] % 4
                    pc[0] += 1
                    if kind == "g":
                        rk, s2 = (g, 0) if g < 4 else (7 - g, 1)
                        DMA("sp", dma(kp_t[sl], kTgv[h // 4][rk, h % 4, :, s2 * 512:(s2 + 1) * 512]), r=[R_kTg], w=[R_kp[sl]])
                        DMA("sp", dma(vp_t[sl], vgv[h // 4][rk, h % 4, s2, :, :, :]), r=[R_vg], w=[R_vp[sl]])
                        bias_ap = ebias[:, qb * 8 + g:qb * 8 + g + 1]
                    else:
                        DMA("sp", dma(kp_t[sl], kTlv[h // 4][h % 4, :, qb * 512:(qb + 1) * 512]), r=[R_kTl], w=[R_kp[sl]])
                        DMA("sp", dma(vp_t[sl], vllv[h // 4][h % 4, qb, :, :, :]), r=[R_vl], w=[R_vp[sl]])
                        bias_ap = ebias[:, 15:16]
                    for i in range(4):
                        for c in range(2):
                            rs_ = slice(c * 64, (c + 1) * 64)
                            sb_ = 4 + (ptc[0] % 2)
                            pi = ptc[0] % 4
                            ptc[0] += 1
                            C("pe", mm(ps[:, sb_, :], kp_t[sl][rs_, i * 128:(i + 1) * 128], QT[rs_, h, qcols], True, True),
                              r=[R_kp[sl], R_QT], w=[R_ps[sb_]])
                            C("act", actf(pt_t[pi], ps[:, sb_, :], AF.Exp, bias=bias_ap, scale=0.125), r=[R_ps[sb_], R_const], w=[R_pt[pi]])
                            if kind == "d":
                                C("dve", tt(pt_t[pi], pt_t[pi], mtab[:, 512 - 128 * i:1024 - 128 * i], ALU.mult), r=[R_const], w=[R_pt[pi]])
                            first = un[c] == 0
                            last = un[c] == nun - 1
                            un[c] += 1
                            pending.append((h, qb, sl, c, i, pi, first, last, last and c == 1))
                            flush(LA)
                            ucount[0] += 1
                            if ucount[0] % 5 == 0 and spiece[0] < NPIECE:
                                sample_step(spiece[0])
                                spiece[0] += 1
        flush(0)
        while spiece[0] < NPIECE:
            sample_step(spiece[0])
            spiece[0] += 1
        P.barrier()
        sample_finish(j)
        for mb in range(8):
            blk = ws_take()
            for mi in range(2):
                s = next_set()
                mm_chunk(blk, KC, mi, xn, R_xn, s)
                resid_add(mb * 2 + mi, s)
            ws_release(blk)
        P.barrier()

    SA = 12544
    kpc = [ab(SA + i * 512, 1024).rearrange("p (g h t) -> p g h t", h=2, t=128) for i in range(2)]; R_kpc = [Res("kpc0"), Res("kpc1")]
    vpc = [ab(SA + 1024 + i * 528, 1056).rearrange("p (g h d) -> p g h d", h=2, d=132) for i in range(2)]
    R_vpc = [[Res("vpc00"), Res("vpc01")], [Res("vpc10"), Res("vpc11")]]
    pexp = [ab(SA + 2080 + i * 256, 512).rearrange("p (h w) -> p h w", w=256) for i in range(2)]; R_pexp = [Res("pexp0"), Res("pexp1")]
    Et = [ab(SA + 2592 + i * 16, 32) for i in range(2)]; R_E = [Res("E0"), Res("E1")]
    vm_t = [ab(SA + 2624 + i * 16, 32) for i in range(2)]; R_vm = [Res("vm0"), Res("vm1")]
    pnew = ab(SA + 2656, 128).rearrange("p (h w) -> p h w", w=64); R_pnew = Res("pnew")
    oacc = af(SA + 2720, 264).rearrange("p (h d) -> p h d", d=132); R_oacc = Res("oacc")
    onrm = af(SA + 2984, 256).rearrange("p (h d) -> p h d", d=128); R_onrm = Res("onrm")
    od_t = af(SA + 3240, 256).rearrange("p (h d) -> p h d", d=128); R_od = Res("od")
    osq = af(SA + 3496, 256).rearrange("p (h d) -> p h d", d=128); R_osq = Res("osq")
    ss_t = af(SA + 3752, 4); R_ss = Res("ss")
    osend = af(SA + 3756, 64); R_osend = Res("osend")
    oall = af(0, 512).rearrange("p (r w) -> p r w", w=64); R_oall = Res("oall")
    ckv = ckT.rearrange("g p h t -> p g h t")
    cvv = cv.rearrange("g p h d -> p g h d")
    NPIECE = 320
    ptb4 = pt_f[:, :].unsqueeze(1).broadcast_to([128, 4, 8])
    iob4 = iota8[:, 0:4].unsqueeze(2).broadcast_to([128, 4, 8])
    pb7 = ps[0:64, 7, 0:264].rearrange("p (h d) -> p h d", d=132)[:, :, 0:129]

    def sample_setup():
        for i in range(2):
            C("dve", mset(vpc[i][:], 1.0), w=R_vpc[i])

    def sample_av(pc_):
        bj = pc_ % 2
        for hh in range(2):
            for g in range(4):
                C("pe", mm(ps[0:64, 7, hh * 132:hh * 132 + 129], pexp[bj][:, hh, g * 64:(g + 1) * 64], vpc[bj][:, g, hh, 0:129], g == 0, g == 3),
                  r=[R_pexp[bj], R_vpc[bj][hh]], w=[R_ps[7]])
        if pc_ == 0:
            C("dve", cp(oacc[0:64, :, 0:129], pb7), r=[R_ps[7]], w=[R_oacc])
        else:
            C("dve", tt(oacc[0:64, :, 0:129], oacc[0:64, :, 0:129], pb7, ALU.add), r=[R_ps[7]], w=[R_oacc])

    def sample_step(piece):
        bi = piece % 2
        g0 = piece * 4
        DMA("pool", dma(kpc[bi], ckv[:, g0:g0 + 4, :, :]), w=[R_kpc[bi]])
        for hh in range(2):
            DMA("pool", dma(vpc[bi][:, :, hh, 0:128], cvv[:, g0:g0 + 4, hh, :]), w=[R_vpc[bi][hh]])
        C("dve", stt(Et[bi].rearrange("p (a s) -> p a s", s=8), ptb4, float(g0), iob4, ALU.subtract, ALU.is_equal), r=[R_small["pt"], R_const], w=[R_E[bi]])
        C("pe", mm(ps[:, 7, 448:480], ones_b[:], Et[bi], True, True), r=[R_E[bi], R_const], w=[R_ps[7]])
        C("act", acp(vm_t[bi], ps[:, 7, 448:480]), r=[R_ps[7]], w=[R_vm[bi]])
        for hh in range(2):
            for g in range(4):
                C("pe", mm(ps[:, 6, hh * 256 + g * 64:hh * 256 + (g + 1) * 64], kpc[bi][:, g, hh, :], qbd[:, hh, :], True, True),
                  r=[R_kpc[bi], R_small["qbd"]], w=[R_ps[6]])
        C("act", actf(pexp[bi], ps[:, 6, :].rearrange("p (h w) -> p h w", w=256), AF.Exp, scale=0.125), r=[R_ps[6]], w=[R_pexp[bi]])
        vmb = vm_t[bi].rearrange("p (a s) -> p a s", s=8).unsqueeze(2).broadcast_to([128, 4, 8, 8])
        for hh in range(2):
            pv4 = pexp[bi][:, hh, :].rearrange("p (a c s) -> p a c s", c=8, s=8)
            C("dve", tt(pv4, pv4, vmb, ALU.mult), r=[R_vm[bi]], w=[R_pexp[bi]])
        if piece > 0:
            sample_av(piece - 1)

    def sample_finish(j):
        sample_av(NPIECE - 1)
        for hh in range(2):
            C("pe", mm(ps[0:32, 6, hh * 64:(hh + 1) * 64], ksT_sel[:, hh, :], qbd[:, hh, :], True, True), r=[R_small["ksT_sel"], R_small["qbd"]], w=[R_ps[6]])
        C("act", actf(pnew[0:32, :, :], ps[0:32, 6, 0:128].rearrange("p (h w) -> p h w", w=64), AF.Exp, scale=0.125), r=[R_ps[6]], w=[R_pnew])
        for hh in range(2):
            C("dve", tt(pnew[0:32, hh, :], pnew[0:32, hh, :], smask[:], ALU.mult), r=[R_const], w=[R_pnew])
            C("pe", mm(ps[0:64, 7, hh * 132:hh * 132 + 129], pnew[0:32, hh, :], vnew_sel[:, hh, 0:129], True, True), r=[R_pnew, R_small["vnew_sel"]], w=[R_ps[7]])
        C("dve", tt(oacc[0:64, :, 0:129], oacc[0:64, :, 0:129], pb7, ALU.add), r=[R_ps[7]], w=[R_oacc])
        for hh in range(2):
            C("dve", lambda e, hh=hh: e.reciprocal(out=ss_t[0:64, hh:hh + 1], in_=oacc[0:64, hh, 128:129]), r=[R_oacc], w=[R_ss])
            C("dve", ts(onrm[0:64, hh, :], oacc[0:64, hh, 0:128], ss_t[0:64, hh:hh + 1], ALU.mult), r=[R_oacc, R_ss], w=[R_onrm])
        C("dve", cp(osq[0:32, :, :], onrm[32:64, :, :]), r=[R_onrm], w=[R_osq])
        C("dve", stt(od_t[0:32, :, :], osq[0:32, :, :], lamt[0:32, j, 1:2], onrm[0:32, :, :], ALU.mult, ALU.add), r=[R_onrm, R_osq, R_small["lam"]], w=[R_od])
        C("dve", tt(osq[0:32, :, :], od_t[0:32, :, :], od_t[0:32, :, :], ALU.mult), r=[R_od], w=[R_osq])
        C("dve", lambda e: e.tensor_reduce(out=ss_t[0:32, 2:4], in_=osq[0:32, :, :], axis=AX.X, op=ALU.add), r=[R_osq], w=[R_ss])
        C("act", actf(ss_t[0:32, 2:4], ss_t[0:32, 2:4], AF.Sqrt, bias=eps_t[0:32, :], scale=1.0 / 128), r=[R_ss, R_const], w=[R_ss])
        C("dve", lambda e: e.reciprocal(out=ss_t[0:32, 2:4], in_=ss_t[0:32, 2:4]), r=[R_ss], w=[R_ss])
        for hh in range(2):
            C("dve", stt(osq[0:32, hh, :], od_t[0:32, hh, :], ss_t[0:32, 2 + hh:3 + hh], gsub_row[:, j, :], ALU.mult, ALU.mult),
              r=[R_od, R_ss, R_small["gsub"]], w=[R_osq])
        for hh in range(2):
            C("pe", tr(ps[:, 7, 128 + hh * 32:128 + (hh + 1) * 32], osq[0:32, hh, :], ident_f[0:32, 0:32]), r=[R_osq, R_const], w=[R_ps[7]])
        C("dve", cp(osend.rearrange("p (h s t) -> p h s t", h=2, t=4), ps[:, 7, 128:192].rearrange("p (h t s) -> p h s t", h=2, t=4)), r=[R_ps[7]], w=[R_osend])
        DMA("sp", dma(os_in_d, osend), r=[R_osend], w=[R_os_in])
        P.CC(lambda e: e.collective_compute("AllGather", ALU.bypass, replica_groups=GROUPS8, ins=[os_in_d.opt()], outs=[os_g_d.opt()]), r=[R_os_in], w=[R_os_g])
        DMA("sp", dma(oall, os_g_d.rearrange("(r p) f -> p r f", p=128)), r=[R_os_g], w=[R_oall])
        C("dve", cp(xn[:, :, 1024:T], oall.rearrange("p r (h w) -> p (r h) w", w=NSMP)), r=[R_oall], w=[R_xn])

    ytok = [af(i * 2048, 2048) for i in range(2)]
    R_ytok = [Res("ytok0"), Res("ytok1")]

    def final_out():
        rmsnorm(V_NFIN, out_is_x=True)
        for tk in range(9):
            rows = 128 if tk < 8 else NSMP
            b = tk % 2
            for cg in range(4):
                bank = 6 + (cg % 2)
                for cc in range(4):
                    c = cg * 4 + cc
                    C("pe", tr(ps[0:rows, bank, cc * 128:(cc + 1) * 128], xT[:, c, tk * 128:tk * 128 + rows], ident_f[:]), r=[R_x, R_const], w=[R_ps[bank]])
                if cg % 2:
                    C("act", acp(ytok[b][0:rows, cg * 512:(cg + 1) * 512], ps[0:rows, bank, :]), r=[R_ps[bank]], w=[R_ytok[b]])
                else:
                    C("dve", cp(ytok[b][0:rows, cg * 512:(cg + 1) * 512], ps[0:rows, bank, :]), r=[R_ps[bank]], w=[R_ytok[b]])
            dst = y_p[tk * 128:(tk + 1) * 128, :] if tk < 8 else y_s
            DMA("sp", dma(dst, ytok[b][0:rows, :]), r=[R_ytok[b]])

    phases = []
    for l in range(2):
        phases.append((rec_blocks(l), lambda l=l: recurrent(l)))
        phases.append((ffn_blocks(l), lambda l=l: ffn(l)))
    phases.append(([wblk(w_kv, 0, KC, nb * 256) for nb in range(16)], kv_phase))
    for j in range(2):
        phases.append((attn_blocks(j), lambda j=j: attention(j)))
        phases.append((ffn_blocks(j + 2), lambda j=j: ffn(j + 2)))
    phases = [phases[i] for i in PHASE_SEL] if PHASE_SEL is not None else phases[:NPHASE]
    ws_queue(phases[0][0])
    for i, (blks, fn) in enumerate(phases):
        if i + 1 < len(phases):
            ws_pending.extend(phases[i + 1][0])
        fn()
    if NPHASE >= 9:
        final_out()
    P.barrier()
    return nc, P, es


def emit_program(nc, P, es):
    sig = P.emit(nc, None)
    sem_eng = {e: es.enter_context(nc.semaphore("s_" + e)) for e in ENGS}
    sem_dma = {}
    for q, n in P.NDMA.items():
        for i in range(n):
            sem_dma[(q, i)] = es.enter_context(nc.semaphore("d_%s%d" % (q, i)))
    sem_dma["cc"] = es.enter_context(nc.semaphore("cc"))

    def run(eh, ename):
        waited = {}
        for node in P.ops[ename]:
            for d in sorted(node.deps, key=str):
                if d[0] == "c":
                    E = d[1]
                    if E == ename and (E == "pe" or not SAME_ENGINE_SYNC):
                        continue
                    sem = sem_eng[E]
                    v = sig[E][d[2]]
                    key = E
                else:
                    sem = sem_dma[d[1]]
                    v = d[2]
                    key = d[1]
                if waited.get(key, 0) < v:
                    eh.wait_ge(sem, v)
                    waited[key] = v
            if node.kind == "b":
                continue
            ins = node.fn(eh)
            if node.kind == "c":
                if node.signal:
                    ins.then_inc(sem_eng[ename], 1)
            elif node.kind == "d":
                ins.then_inc(sem_dma[node.sem], 16)
            else:
                ins.then_inc(sem_dma["cc"])

    with nc.Block() as block:
        @block.tensor
        def _(e):
            run(e, "pe")

        @block.scalar
        def _(e):
            run(e, "act")

        @block.vector
        def _(e):
            run(e, "dve")

        @block.gpsimd
        def _(e):
            run(e, "pool")

        @block.sync
        def _(e):
            run(e, "sp")


NPHASE = 9
PHASE_SEL = None
_CACHE = {}


def _get_program():
    if "nc" not in _CACHE:
        nc, P, es = build_program()
        emit_program(nc, P, es)
        _CACHE["nc"] = nc
        _CACHE["es"] = es
    return _CACHE["nc"]


def _host_tables(r):
    b, j = r // 4, r % 4
    g_seg = (j, 7 - j)
    pos = np.zeros((9, 128), np.float32)
    for tk in range(8):
        g = g_seg[tk // 4]
        pos[tk] = g * 512 + (tk % 4) * 128 + np.arange(128)
    pos[8] = PAST + (np.arange(128) % 4)
    inv = (10000.0 ** (-np.arange(32, dtype=np.float32) / 32)).astype(np.float32)
    ang = (pos[:, :, None].astype(np.float32) * inv[None, None, :]).astype(np.float32)
    cos32 = np.ascontiguousarray(np.cos(ang).astype(np.float32).transpose(1, 0, 2))
    sin32 = np.ascontiguousarray(np.sin(ang).astype(np.float32).transpose(1, 0, 2))
    coef = np.zeros((64,), np.float32)
    for s in range(2):
        for rk in range(4):
            for s2 in range(2):
                blk = rk if s2 == 0 else 7 - rk
                if blk == g_seg[s] - 1:
                    coef[s * 8 + rk * 2 + s2] = 1.0
        coef[16 + s * 8 + g_seg[s]] = 1.0
    for hh in range(2):
        coef[32 + hh * 16 + 2 * r + hh] = 1.0
    ebias = np.zeros((16,), np.float32)
    for qb in range(2):
        for g in range(7):
            ebias[qb * 8 + g] = 0.0 if g < g_seg[qb] else -30000.0
    return cos32, sin32, np.tile(coef[None, :], (128, 1)), np.tile(ebias[None, :], (128, 1))


def kernel(x_prompt, x_sample, cache_k, cache_v, page_table, state_conv, state_rglru,
           norm_mix, norm_ffn, norm_final, rg_w_x, rg_w_gate, rg_conv_w, rg_conv_b,
           rg_w_a, rg_b_a, rg_w_i, rg_b_i, rg_lambda, rg_w_out, kv_norm, w_kv,
           dif_w_q, dif_lq1, dif_lk1, dif_lq2, dif_lk2, dif_subln, dif_w_o,
           ffn_w_gate, ffn_w_up, ffn_w_down):
    f32 = lambda a: np.ascontiguousarray(np.asarray(a, dtype=np.float32))
    x_prompt = f32(x_prompt); x_sample = f32(x_sample)
    cache_k = np.asarray(cache_k, dtype=np.float32); cache_v = np.asarray(cache_v, dtype=np.float32)
    nc = _get_program()
    vec_list = [norm_mix[i] for i in range(4)] + [norm_ffn[i] for i in range(4)] + [norm_final, kv_norm]
    vec_list += [rg_conv_w[l, jj] for l in range(2) for jj in range(4)]
    vec_list += [rg_conv_b[l] for l in range(2)] + [rg_b_a[l] for l in range(2)] + [rg_b_i[l] for l in range(2)] + [rg_lambda[l] for l in range(2)]
    vecs = np.concatenate([f32(v).reshape(16, 128) for v in vec_list], axis=0)
    assert vecs.shape[0] == NV * 16
    kq = np.arange(128)
    xx = np.arange(1024) - 512
    maskd = (xx[None, :] >= kq[:, None]).astype(np.float32)
    smaskd = np.zeros((32, 64), np.float32)
    for s_ in range(8):
        for t_ in range(4):
            for q in range(4):
                if t_ <= q:
                    for c_ in range(2):
                        smaskd[s_ * 4 + t_, c_ * 32 + q * 8 + s_] = 1.0
    common = {
        "xs": f32(x_sample.reshape(NSMP, D)), "vecs": vecs, "maskd": maskd, "smaskd": smaskd,
        "ptab": np.ascontiguousarray(np.asarray(page_table, dtype=np.int32).reshape(8, NPG).T),
        "state_conv": f32(state_conv), "state_rglru": f32(state_rglru),
        "rg_w_x": f32(rg_w_x), "rg_w_gate": f32(rg_w_gate), "rg_w_out": f32(rg_w_out),
        "rg_w_a": f32(rg_w_a), "rg_w_i": f32(rg_w_i), "w_kv": f32(w_kv),
        "dif_w_q": f32(dif_w_q), "dif_w_o": f32(dif_w_o),
        "lqk": f32(np.stack([dif_lq1, dif_lk1, dif_lq2, dif_lk2], axis=0)),
        "dif_subln": f32(dif_subln),
        "ffn_w_gate": f32(ffn_w_gate), "ffn_w_up": f32(ffn_w_up), "ffn_w_down": f32(ffn_w_down),
        "identd": np.eye(128, dtype=np.float32),
        "iotad": np.tile(np.arange(8, dtype=np.float32)[None, :], (128, 1)),
    }
    in_maps = []
    for r in range(8):
        b, j = r // 4, r % 4
        cos32, sin32, coef, ebias = _host_tables(r)
        xp = np.concatenate([x_prompt[b, j * 512:(j + 1) * 512], x_prompt[b, (7 - j) * 512:(8 - j) * 512]], axis=0)
        ck = cache_k[:, :, 2 * r:2 * r + 2]
        ckT = np.ascontiguousarray(ck.transpose(0, 3, 4, 2, 1)).reshape(1280, 128, 2, 128)
        cvr = np.ascontiguousarray(cache_v[:, :, 2 * r:2 * r + 2, :])
        m = dict(common)
        m.update({"xp": f32(xp), "cos32": cos32, "sin32": sin32, "coefd": coef, "ebiasd": ebias, "ckT": ckT, "cv": cvr})
        in_maps.append(m)
    res = run_bass_kernel_spmd(nc, in_maps, core_ids=list(range(8)))
    R = res.results
    y_prompt = np.zeros((2, 4096, D), np.float32)
    k_prompt = np.zeros((2, 4096, D), np.float32)
    v_prompt = np.zeros((2, 4096, D), np.float32)
    for r in range(8):
        b, j = r // 4, r % 4
        for (dst, key) in ((y_prompt, "y_p"), (k_prompt, "k_p"), (v_prompt, "v_p")):
            dst[b, j * 512:(j + 1) * 512] = R[r][key][0:512]
            dst[b, (7 - j) * 512:(8 - j) * 512] = R[r][key][512:1024]
    conv_prompt = np.stack([R[0]["conv_p"], R[4]["conv_p"]], axis=1).astype(np.float32)
    h_prompt = np.stack([R[0]["h_p"], R[4]["h_p"]], axis=1).astype(np.float32)
    return (y_prompt,
            np.asarray(R[0]["y_s"], np.float32).reshape(8, 4, D),
            k_prompt.reshape(2, 4096, NH, 2, 64),
            v_prompt.reshape(2, 4096, NH, 128),
            conv_prompt, h_prompt,
            np.asarray(R[0]["k_s"], np.float32).reshape(8, 4, NH, 2, 64),
            np.asarray(R[0]["v_s"], np.float32).reshape(8, 4, NH, 128),
            np.asarray(R[0]["conv_s"], np.float32), np.asarray(R[0]["h_s"], np.float32))
```

```python
import math
from contextlib import ExitStack

import numpy as np
import concourse.bass as bass
import concourse.mybir as mybir
from concourse.bass_utils import run_bass_kernel_spmd

F32 = mybir.dt.float32
BF16 = mybir.dt.bfloat16
I32 = mybir.dt.int32
ALU = mybir.AluOpType
AF = mybir.ActivationFunctionType
AX = mybir.AxisListType

D = 2048
KC = 16
SEG = 512
NSMP = 32
T = 2 * SEG + NSMP
FF = 5632
NH = 16
EPS = 1e-6
PAST = 16384
NPG = 128
TT = ((0, 512), (512, 512), (1024, 32))
ENGS = ("pe", "act", "dve", "pool", "sp")
SAME_ENGINE_SYNC = True

V_NMIX, V_NFFN, V_NFIN, V_NKV, V_CW, V_CB, V_BA, V_BI, V_LAM = 0, 4, 8, 9, 10, 18, 20, 22, 24
NV = 26


class Res:
    __slots__ = ("name", "w", "r")

    def __init__(self, name):
        self.name = name
        self.w = None
        self.r = {}


class Node:
    __slots__ = ("eng", "fn", "deps", "kind", "signal", "sem", "val")

    def __init__(self, eng, fn, deps, kind):
        self.eng, self.fn, self.deps, self.kind = eng, fn, deps, kind
        self.signal = False
        self.sem = None
        self.val = 0


class Prog:
    NDMA = {"sp": 20, "pool": 20, "act": 6}

    def __init__(self):
        self.ops = {e: [] for e in ENGS}
        self.dma_rr = {q: 0 for q in self.NDMA}
        self.dma_cnt = {}
        self.cc_cnt = 0
        self.uid = 0
        self.dma_events = []

    def _deps(self, reads, writes):
        deps = set()
        for r in reads:
            if r.w is not None:
                deps.add(r.w)
        for w in writes:
            if w.w is not None:
                deps.add(w.w)
            deps.update(w.r.values())
        return deps

    def _commit(self, ev, key, reads, writes):
        for w in writes:
            w.w = ev
            w.r = {}
        for r in reads:
            if r not in writes:
                r.r[key] = ev

    def C(self, eng, fn, r=(), w=()):
        deps = self._deps(r, w)
        idx = len(self.ops[eng])
        self.ops[eng].append(Node(eng, fn, deps, "c"))
        ev = ("c", eng, idx)
        self._commit(ev, eng, r, w)
        return ev

    def DMA(self, q, fn, r=(), w=()):
        deps = self._deps(r, w)
        i = self.dma_rr[q]
        self.dma_rr[q] = (i + 1) % self.NDMA[q]
        n = self.dma_cnt.get((q, i), 0)
        if n > 0:
            deps.add(("d", (q, i), n * 16))
        node = Node(q, fn, deps, "d")
        node.sem = (q, i)
        node.val = (n + 1) * 16
        self.dma_cnt[(q, i)] = n + 1
        self.ops[q].append(node)
        ev = ("d", (q, i), node.val)
        self.uid += 1
        self._commit(ev, ("d", self.uid), r, w)
        self.dma_events.append(ev)
        return ev

    def CC(self, fn, r=(), w=()):
        deps = self._deps(r, w)
        node = Node("pool", fn, deps, "cc")
        self.cc_cnt += 1
        node.sem = "cc"
        node.val = self.cc_cnt
        self.ops["pool"].append(node)
        ev = ("d", "cc", node.val)
        self.uid += 1
        self._commit(ev, ("d", self.uid), r, w)
        self.dma_events.append(ev)
        return ev

    def barrier(self):
        last = {}
        for e in ENGS:
            for i in range(len(self.ops[e]) - 1, -1, -1):
                if self.ops[e][i].kind == "c":
                    last[e] = ("c", e, i)
                    break
        deps = set(last.values()) | set(self.dma_events)
        self.dma_events = []
        for e in ENGS:
            self.ops[e].append(Node(e, None, set(deps), "b"))

    def emit(self, nc, block_engs):
        for e in ENGS:
            for node in self.ops[e]:
                for d in node.deps:
                    if d[0] == "c":
                        if d[1] == e and (e == "pe" or not SAME_ENGINE_SYNC):
                            continue
                        self.ops[d[1]][d[2]].signal = True
        sigcount = {}
        for e in ENGS:
            c = 0
            arr = []
            for node in self.ops[e]:
                if node.kind == "c" and node.signal:
                    c += 1
                arr.append(c)
            sigcount[e] = arr
        self.sigcount = sigcount
        return sigcount


def build_program(nseq_groups=2):
    nc = bass.Bass("TRN2", target_bir_lowering=False)
    P = Prog()
    es = ExitStack()

    def din(name, shape, dt=F32):
        return nc.dram_tensor(name, list(shape), dt, kind="ExternalInput").ap()

    def dout(name, shape, dt=F32):
        return nc.dram_tensor(name, list(shape), dt, kind="ExternalOutput").ap()

    def dtmp(name, shape, dt=F32):
        return nc.dram_tensor(name, list(shape), dt).ap()

    def sb(name, shape, dt=F32):
        return es.enter_context(nc.sbuf_tensor(name, list(shape), dt))

    xp = din("xp", [1024, D])
    xs = din("xs", [NSMP, D])
    vecs = din("vecs", [NV * 16, 128])
    cosd = din("cos32", [128, 9, 32])
    sind = din("sin32", [128, 9, 32])
    maskd = din("maskd", [128, 1024])
    coefd = din("coefd", [128, 64])
    smaskd = din("smaskd", [32, 64])
    ptab = din("ptab", [NPG, 8], I32)
    iotad = din("iotad", [128, 8])
    stc = din("state_conv", [2, 8, 3, D])
    sth = din("state_rglru", [2, 8, D])
    ckT = din("ckT", [1280, 128, 2, 128])
    cv = din("cv", [1280, 128, 2, 128])
    w_x = din("rg_w_x", [2, D, D]); w_g = din("rg_w_gate", [2, D, D]); w_o = din("rg_w_out", [2, D, D])
    w_a = din("rg_w_a", [2, 8, 256, 256]); w_i = din("rg_w_i", [2, 8, 256, 256])
    w_kv = din("w_kv", [D, 2 * D])
    w_q = din("dif_w_q", [2, D, D]); w_do = din("dif_w_o", [2, D, D])
    lqk = din("lqk", [4, 2, 64])
    subln = din("dif_subln", [2, 128])
    f_g = din("ffn_w_gate", [4, D, FF]); f_u = din("ffn_w_up", [4, D, FF]); f_d = din("ffn_w_down", [4, FF, D])

    y_p = dout("y_p", [1024, D]); y_s = dout("y_s", [NSMP, D])
    k_p = dout("k_p", [1024, D]); v_p = dout("v_p", [1024, D])
    k_s = dout("k_s", [NSMP, D]); v_s = dout("v_s", [NSMP, D])
    conv_p = dout("conv_p", [2, 3, D]); h_p = dout("h_p", [2, D])
    conv_s = dout("conv_s", [2, 8, 3, D]); h_s = dout("h_s", [2, 8, D])

    halo_in_d = dtmp("halo_in_d", [128, 96]); halo_g_d = dtmp("halo_g_d", [4 * 128, 96])
    car_in_d = dtmp("car_in_d", [128, 64]); car_g_d = dtmp("car_g_d", [4 * 128, 64])
    x1_d = dtmp("x1_d", [128, KC, T], BF16); x2_d = dtmp("x2_d", [128, KC, 1024], BF16)
    kT_loc = [dtmp("kT_loc%d" % q, [4 * 128, 1024], BF16) for q in range(4)]; kT_g = [dtmp("kT_g%d" % q, [4 * 4 * 128, 1024], BF16) for q in range(4)]
    v_loc = [dtmp("v_loc%d" % q, [4 * 2 * 128, 512], BF16) for q in range(4)]; v_g = [dtmp("v_g%d" % q, [4 * 4 * 2 * 128, 512], BF16) for q in range(4)]
    vs_d = dtmp("vs_d", [NSMP, D], BF16)
    os_in_d = dtmp("os_in_d", [128, 64]); os_g_d = dtmp("os_g_d", [8 * 128, 64])

    xT = sb("xT", [128, KC, T]); R_x = Res("xT")
    xn = sb("xn", [128, KC, T], BF16); R_xn = Res("xn")
    AW = 16384
    arena = sb("arena", [128, AW]); arena_b = arena[:].bitcast(BF16)
    wbuf = sb("wbuf", [128, 2, 16 * 256], BF16); R_wb = [Res("wb0"), Res("wb1")]
    ps = es.enter_context(nc.psum_tensor("ps", [128, 8, 512], F32)); R_ps = [Res("ps%d" % i) for i in range(8)]
    pv = sb("pv", [128, NV * 16]); R_pv = Res("pv")
    ident_f = sb("ident_f", [128, 128]); ident_b = sb("ident_b", [128, 128], BF16)
    ones_f = sb("ones_f", [128, 128]); ones_b = sb("ones_b", [128, 128], BF16)
    R_const = Res("const")
    mtab = sb("mtab", [128, 1024], BF16)
    zeros_b = mtab[:, 0:512]
    cos32 = sb("cos32_sb", [128, 9, 32]); sin32 = sb("sin32_sb", [128, 9, 32])
    coef = sb("coef", [128, 64])
    smask = sb("smask", [32, 64])
    pt_i = sb("pt_i", [128, 8], I32); pt_f = sb("pt_f", [128, 8]); iota8 = sb("iota8", [128, 8])
    nsp = sb("nsp", [128, 2, 2, KC])
    lamt = sb("lamt", [128, 2, 4])
    gsub = sb("gsub", [128, 2]); gsub_row = sb("gsub_row", [32, 2, 128])
    lq_sb = arena[:, 1024:1536].rearrange("p (a l d) -> p a l d", a=4, l=2)
    halo_own = sb("halo_own", [128, KC, 2, 3]); halo_sel = sb("halo_sel", [128, KC, 2, 3]); halo_all = sb("halo_all", [128, 4, 96])
    hsmp = sb("hsmp", [128, KC, 24]); h0s = sb("h0s", [128, KC, 8])
    car = sb("car", [128, 2, KC, 2]); car_all = halo_all[:, :, 0:64]; Hc = sb("Hc", [128, 9, KC]); Hin = sb("Hin", [128, KC, 2])
    hlast_s = sb("hlast_s", [128, KC, 8]); csmp = sb("csmp", [128, KC, 24])
    ksT = sb("ksT", [128, NH, NSMP], BF16); ksT_sel = sb("ksT_sel", [128, 2, NSMP], BF16)
    vnew_sel = sb("vnew_sel", [32, 2, 132], BF16)
    qbd = sb("qbd", [128, 2, 64], BF16)
    R_small = {n: Res(n) for n in ("nsp", "lam", "gsub", "halo_own", "halo_sel", "halo_all", "hsmp", "h0s", "car", "car_all",
                                   "Hc", "Hin", "hlast_s", "csmp", "ksT", "ksT_sel", "vnew", "vnew_sel", "qbd", "lq", "pt")}
    R_small["car_all"] = R_small["halo_all"]

    def af(off, n):
        return arena[:, off:off + n]

    def ab(off, n):
        return arena_b[:, 2 * off:2 * off + n]

    NORM_OFF = AW - 3 * T
    sq_t = [af(NORM_OFF + i * T, T) for i in range(2)]; R_sq = [Res("sq0"), Res("sq1")]
    rs_t = af(NORM_OFF + 2 * T, T); R_rs = Res("rs")

    def ps_seg(s):
        return ps[:, 3 * s:3 * s + 2, :]

    def ps_smp(s):
        return ps[:, 3 * s + 2, 0:NSMP]

    def R_set(s):
        return R_ps[3 * s:3 * s + 3]

    def segv(ap2d):
        return ap2d[:, 0:1024].rearrange("p (s w) -> p s w", w=512)

    def smpv(ap2d):
        return ap2d[:, 1024:T]

    C, DMA = P.C, P.DMA

    C("dve", lambda e: e.memset(ones_f[:], 1.0), w=[R_const])
    C("dve", lambda e: e.memset(ones_b[:], 1.0), w=[R_const])
    identd = din("identd", [128, 128])
    DMA("sp", lambda e: e.dma_start(out=ident_f[:], in_=identd), w=[R_const])
    C("dve", lambda e: e.tensor_copy(out=ident_b[:], in_=ident_f[:]), r=[R_const], w=[R_const])
    DMA("sp", lambda e: e.dma_start(out=cos32[:], in_=cosd), w=[R_const])
    DMA("sp", lambda e: e.dma_start(out=sin32[:], in_=sind), w=[R_const])
    DMA("sp", lambda e: e.dma_start(out=coef[:], in_=coefd), w=[R_const])
    DMA("sp", lambda e: e.dma_start(out=smask[:], in_=smaskd), w=[R_const])
    DMA("sp", lambda e: e.dma_start(out=pt_i[:], in_=ptab), w=[R_small["pt"]])
    DMA("sp", lambda e: e.dma_start(out=iota8[:], in_=iotad), w=[R_const])
    C("dve", lambda e: e.tensor_copy(out=pt_f[:], in_=pt_i[:]), r=[R_small["pt"]], w=[R_small["pt"]])
    DMA("pool", lambda e: e.dma_start(out=mtab[:], in_=maskd), w=[R_const])
    vin = af(0, 4 * 128).rearrange("p (a b) -> p a b", b=128)
    R_vin = Res("vin")
    for a in range(4):
        rows = min(128, NV * 16 - a * 128)
        DMA("sp", lambda e, a=a, rows=rows: e.dma_start(out=vin[0:rows, a, :], in_=vecs[a * 128:a * 128 + rows, :]), w=[R_vin])
    for a in range(4):
        rows = min(128, NV * 16 - a * 128)
        C("pe", lambda e, a=a, rows=rows: e.transpose(ps[:, 0, a * 128:a * 128 + rows], vin[0:rows, a, :], ident_f[0:rows, 0:rows]),
          r=[R_vin, R_const], w=[R_ps[0]])
    C("dve", lambda e: e.tensor_copy(out=pv[:], in_=ps[:, 0, 0:NV * 16]), r=[R_ps[0]], w=[R_pv])

    def pvc(v, c):
        return pv[:, v * 16 + c:v * 16 + c + 1]

    lam_v = pv[:, V_LAM * 16:(V_LAM + 2) * 16].rearrange("p (l c) -> p l c", c=KC)
    C("act", lambda e: e.activation(out=nsp[:, :, 0, :], in_=lam_v, func=AF.Exp, scale=-1.0), r=[R_pv], w=[R_small["nsp"]])
    C("act", lambda e: e.activation(out=nsp[:, :, 0, :], in_=nsp[:, :, 0, :], func=AF.Ln, bias=1.0, scale=1.0), r=[R_small["nsp"]], w=[R_small["nsp"]])
    C("dve", lambda e: e.tensor_scalar(out=nsp[:, :, 1, :], in0=nsp[:, :, 0, :], scalar1=-16.0, scalar2=None, op0=ALU.mult), r=[R_small["nsp"]], w=[R_small["nsp"]])
    C("dve", lambda e: e.tensor_scalar(out=nsp[:, :, 0, :], in0=nsp[:, :, 0, :], scalar1=-8.0, scalar2=None, op0=ALU.mult), r=[R_small["nsp"]], w=[R_small["nsp"]])
    DMA("sp", lambda e: e.dma_start(out=arena[:, 1024:1536],
                                     in_=lqk.rearrange("a l d -> (a l d)").partition_broadcast(128)), w=[R_small["lq"]])
    lprod = af(2048, 256).rearrange("p (a l d) -> p a l d", a=2, l=2)
    lsum = af(2304, 4).rearrange("p (a l) -> p a l", a=2)
    R_lt = Res("ltmp")
    for a in range(2):
        C("dve", lambda e, a=a: e.tensor_tensor(out=lprod[:, a, :, :], in0=lq_sb[:, 2 * a, :, :], in1=lq_sb[:, 2 * a + 1, :, :], op=ALU.mult),
          r=[R_small["lq"], R_vin], w=[R_lt])
    C("dve", lambda e: e.tensor_reduce(out=lsum, in_=lprod, axis=AX.X, op=ALU.add), r=[R_lt], w=[R_lt])
    C("act", lambda e: e.activation(out=lsum, in_=lsum, func=AF.Exp), r=[R_lt], w=[R_lt])
    for j in range(2):
        lam_init = 0.8 - 0.6 * math.exp(-0.3 * (j + 2))
        C("dve", lambda e, j=j: e.tensor_tensor(out=lamt[:, j, 0:1], in0=lsum[:, 0, j:j + 1], in1=lsum[:, 1, j:j + 1], op=ALU.subtract),
          r=[R_lt], w=[R_small["lam"]])
        C("dve", lambda e, j=j, li=lam_init: e.tensor_scalar(out=lamt[:, j, 0:1], in0=lamt[:, j, 0:1], scalar1=li, scalar2=None, op0=ALU.add),
          r=[R_small["lam"]], w=[R_small["lam"]])
        C("dve", lambda e, j=j: e.tensor_scalar(out=lamt[:, j, 1:2], in0=lamt[:, j, 0:1], scalar1=-1.0, scalar2=None, op0=ALU.mult),
          r=[R_small["lam"]], w=[R_small["lam"]])
        DMA("sp", lambda e, j=j: e.dma_start(out=gsub[:, j:j + 1], in_=subln[j:j + 1, :].rearrange("o d -> d o")), w=[R_small["gsub"]])
        DMA("sp", lambda e, j=j: e.dma_start(out=gsub_row[:, j, :], in_=subln[j, :].partition_broadcast(32)), w=[R_small["gsub"]])
        C("dve", lambda e, j=j, li=lam_init: e.tensor_scalar(out=gsub[:, j:j + 1], in0=gsub[:, j:j + 1], scalar1=1.0 - li, scalar2=None, op0=ALU.mult),
          r=[R_small["gsub"]], w=[R_small["gsub"]])
        C("dve", lambda e, j=j, li=lam_init: e.tensor_scalar(out=gsub_row[:, j, :], in0=gsub_row[:, j, :], scalar1=1.0 - li, scalar2=None, op0=ALU.mult),
          r=[R_small["gsub"]], w=[R_small["gsub"]])


    def mm(out, lhsT, rhs, st, sp):
        return lambda e: e.matmul(out, lhsT=lhsT, rhs=rhs, start=st, stop=sp)

    def tr(out, in_, idn):
        return lambda e: e.transpose(out, in_, idn)

    def actf(out, in_, func, bias=None, scale=None):
        kw = {}
        if bias is not None:
            kw["bias"] = bias
        if scale is not None:
            kw["scale"] = scale
        return lambda e: e.activation(out=out, in_=in_, func=func, **kw)

    def tt(out, in0, in1, op):
        return lambda e: e.tensor_tensor(out=out, in0=in0, in1=in1, op=op)

    def ts(out, in0, s1, op0, s2=None, op1=None):
        if op1 is None:
            return lambda e: e.tensor_scalar(out=out, in0=in0, scalar1=s1, scalar2=None, op0=op0)
        return lambda e: e.tensor_scalar(out=out, in0=in0, scalar1=s1, scalar2=s2, op0=op0, op1=op1)

    def stt(out, in0, scalar, in1, op0, op1):
        return lambda e: e.scalar_tensor_tensor(out=out, in0=in0, scalar=scalar, in1=in1, op0=op0, op1=op1)

    def cp(out, in_):
        return lambda e: e.tensor_copy(out=out, in_=in_)

    def acp(out, in_):
        return lambda e: e.activation(out=out, in_=in_, func=AF.Copy)

    def dma(out, in_):
        return lambda e: e.dma_start(out=out, in_=in_)

    def mset(ap, v):
        return lambda e: e.memset(ap, v)

    GROUPS4 = [[0, 1, 2, 3], [4, 5, 6, 7]]
    GROUPS8 = [list(range(8))]
    R_hd_in, R_hd_g, R_cd_in, R_cd_g, R_x1d, R_x2d = (Res(n) for n in ("hd_in", "hd_g", "cd_in", "cd_g", "x1d", "x2d"))
    R_kTl, R_kTg, R_vl, R_vg, R_vsd, R_os_in, R_os_g = (Res(n) for n in ("kTl", "kTg", "vl", "vg", "vsd", "os_in", "os_g"))
    eps_t = sb("eps_t", [128, 1])
    C("dve", mset(eps_t[:], EPS), w=[R_const])
    one_t = sb("one_t", [128, 1])
    C("dve", mset(one_t[:], 1.0), w=[R_const])
    ebias = sb("ebias", [128, 16])
    ebiasd = din("ebiasd", [128, 16])
    DMA("sp", dma(ebias[:], ebiasd), w=[R_const])
    P.barrier()

    ws_pending, ws_loaded, ws_free = [], [], [0, 1]

    def ws_pump():
        while ws_free and ws_pending:
            src3, kcn, ncols = ws_pending.pop(0)
            s = ws_free.pop(0)
            dst = wbuf[:, s, 0:kcn * ncols].rearrange("p (k n) -> p k n", n=ncols)
            DMA("pool", dma(dst, src3), w=[R_wb[s]])
            ws_loaded.append((s, dst))

    def ws_queue(blocks):
        ws_pending.extend(blocks)
        ws_pump()

    def ws_take():
        ws_pump()
        return ws_loaded.pop(0)

    def ws_release(blk):
        ws_free.append(blk[0])
        ws_pump()

    def wblk(W2d, r0, kcn, c0, ncols=256):
        return (W2d[r0:r0 + kcn * 128, c0:c0 + ncols].rearrange("(k p) n -> p k n", p=128), kcn, ncols)

    pset = [0]

    def next_set():
        s = pset[0]
        pset[0] ^= 1
        return s

    def mm_chunk(blk, kcn, mm_i, rhs3, R_rhs, s):
        slot, wd = blk
        for k in range(kcn):
            for ti, (c0, w) in enumerate(TT):
                C("pe", mm(ps[:, 3 * s + ti, 0:w], wd[:, k, mm_i * 128:(mm_i + 1) * 128], rhs3[:, k, c0:c0 + w], k == 0, k == kcn - 1),
                  r=[R_wb[slot], R_rhs], w=[R_ps[3 * s + ti]])

    tmpo = sb("tmpo", [128, 384]); R_tmpo = Res("tmpo")
    tmpo2 = sb("tmpo2", [128, 128]); R_tmpo2 = Res("tmpo2")

    def out_fm(src_view, A, dram_rows, R_src):
        n = A * KC
        C("dve", cp(tmpo[:, 0:n].rearrange("p (a c) -> p a c", c=KC), src_view), r=[R_src], w=[R_tmpo])
        for g0 in range(0, n, 128):
            gw = min(128, n - g0)
            C("pe", tr(ps[0:gw, 7, 0:128], tmpo[:, g0:g0 + gw], ident_f[:]), r=[R_tmpo, R_const], w=[R_ps[7]])
            C("dve", cp(tmpo2[0:gw, :], ps[0:gw, 7, 0:128]), r=[R_ps[7]], w=[R_tmpo2])
            DMA("sp", dma(dram_rows[g0:g0 + gw, :], tmpo2[0:gw, :]), r=[R_tmpo2])

    tmpi = af(4096, 2048); R_tmpi = Res("tmpi")

    def in_tm(dram2d, Rr, dst_view, R_dst):
        DMA("sp", dma(tmpi[0:Rr, :], dram2d), w=[R_tmpi])
        for c in range(KC):
            C("pe", tr(ps[:, 7, c * Rr:(c + 1) * Rr], tmpi[0:Rr, c * 128:(c + 1) * 128], ident_f[0:Rr, 0:Rr]), r=[R_tmpi, R_const], w=[R_ps[7]])
        C("dve", cp(dst_view, ps[:, 7, 0:KC * Rr].rearrange("p (c r) -> p c r", r=Rr)), r=[R_ps[7]], w=[R_dst])

    xin = [af(i * 2048, 2048) for i in range(2)]
    R_xin = [Res("xin0"), Res("xin1")]
    for tk in range(9):
        rows = 128 if tk < 8 else NSMP
        src = xp[tk * 128:(tk + 1) * 128, :] if tk < 8 else xs
        b = tk % 2
        DMA("sp", dma(xin[b][0:rows, :], src), w=[R_xin[b]])
        for cg in range(4):
            bank = 6 + (cg % 2)
            for cc in range(4):
                c = cg * 4 + cc
                C("pe", tr(ps[:, bank, cc * 128:cc * 128 + rows], xin[b][0:rows, c * 128:(c + 1) * 128], ident_f[0:rows, 0:rows]),
                  r=[R_xin[b], R_const], w=[R_ps[bank]])
            srcv = ps[:, bank, :].rearrange("p (a b) -> p a b", b=128)[:, :, 0:rows]
            dstv = xT[:, cg * 4:(cg + 1) * 4, tk * 128:tk * 128 + rows]
            if cg % 2:
                C("act", acp(dstv, srcv), r=[R_ps[bank]], w=[R_x])
            else:
                C("dve", cp(dstv, srcv), r=[R_ps[bank]], w=[R_x])
    P.barrier()

    def rmsnorm(v, out3=None, R_out=None, out_is_x=False):
        for c in range(KC):
            i = c % 2
            C("act", actf(sq_t[i], xT[:, c, :], AF.Square), r=[R_x], w=[R_sq[i]])
            for ti, (c0, w) in enumerate(TT):
                C("pe", mm(ps[:, ti, 0:w], ones_f[:], sq_t[i][:, c0:c0 + w], c == 0, c == KC - 1), r=[R_sq[i], R_const], w=[R_ps[ti]])
        C("act", actf(segv(rs_t), ps_seg(0), AF.Sqrt, bias=eps_t[:], scale=1.0 / D), r=[R_ps[0], R_ps[1], R_const], w=[R_rs])
        C("act", actf(smpv(rs_t), ps_smp(0), AF.Sqrt, bias=eps_t[:], scale=1.0 / D), r=[R_ps[2], R_const], w=[R_rs])
        C("dve", lambda e: e.reciprocal(out=rs_t, in_=rs_t), r=[R_rs], w=[R_rs])
        for c in range(KC):
            if out_is_x:
                C("dve", stt(xT[:, c, :], xT[:, c, :], pvc(v, c), rs_t, ALU.mult, ALU.mult), r=[R_x, R_rs, R_pv], w=[R_x])
            else:
                C("dve", stt(xn[:, c, :], xT[:, c, :], pvc(v, c), rs_t, ALU.mult, ALU.mult), r=[R_x, R_rs, R_pv], w=[R_xn])

    def resid_add(m, s):
        C("dve", tt(segv(xT[:, m, :]), ps_seg(s), segv(xT[:, m, :]), ALU.add), r=[R_ps[3 * s], R_ps[3 * s + 1]], w=[R_x])
        C("dve", tt(smpv(xT[:, m, :]), ps_smp(s), smpv(xT[:, m, :]), ALU.add), r=[R_ps[3 * s + 2]], w=[R_x])

    hT = ab(0, 12 * T).rearrange("p (f t) -> p f t", t=T)
    R_hT = Res("hT")
    sg_t = [ab(6336 + i * 528, T) for i in range(2)]
    FPARTS = (6, 6, 5, 5)
    R_sg = [Res("sg0"), Res("sg1")]

    def ffn_blocks(l):
        blocks = []
        b0 = 0
        for nbk in FPARTS:
            for fb in range(nbk):
                c0 = (b0 + fb) * 256
                blocks.append(wblk(f_g[l], 0, KC, c0))
                blocks.append(wblk(f_u[l], 0, KC, c0))
            for mb in range(8):
                blocks.append(wblk(f_d[l], b0 * 256, 2 * nbk, mb * 256))
            b0 += nbk
        return blocks

    def ffn(l):
        rmsnorm(V_NFFN + l)
        for nbk in FPARTS:
            for fb in range(nbk):
                bg = ws_take()
                bu = ws_take()
                for mi in range(2):
                    f = fb * 2 + mi
                    sgi = f % 2
                    s0 = next_set()
                    mm_chunk(bg, KC, mi, xn, R_xn, s0)
                    s1 = next_set()
                    mm_chunk(bu, KC, mi, xn, R_xn, s1)
                    C("act", actf(segv(sg_t[sgi]), ps_seg(s0), AF.Silu), r=[R_ps[3 * s0], R_ps[3 * s0 + 1]], w=[R_sg[sgi]])
                    C("act", actf(smpv(sg_t[sgi]), ps_smp(s0), AF.Silu), r=[R_ps[3 * s0 + 2]], w=[R_sg[sgi]])
                    C("dve", tt(segv(hT[:, f, :]), ps_seg(s1), segv(sg_t[sgi]), ALU.mult), r=[R_ps[3 * s1], R_ps[3 * s1 + 1], R_sg[sgi]], w=[R_hT])
                    C("dve", tt(smpv(hT[:, f, :]), ps_smp(s1), smpv(sg_t[sgi]), ALU.mult), r=[R_ps[3 * s1 + 2], R_sg[sgi]], w=[R_hT])
                ws_release(bg)
                ws_release(bu)
            for mb in range(8):
                bd = ws_take()
                for mi in range(2):
                    s = next_set()
                    mm_chunk(bd, 2 * nbk, mi, hT, R_hT, s)
                    resid_add(mb * 2 + mi, s)
                ws_release(bd)
        P.barrier()

    UBW = 1088
    ub = af(0, 2 * UBW).rearrange("p (m w) -> p m w", w=UBW); R_ub = [Res("ub0"), Res("ub1")]
    gt = ab(2176, 2 * T).rearrange("p (m w) -> p m w", w=T); R_gt = [Res("gt0"), Res("gt1")]
    uc = af(3232, 2 * T).rearrange("p (m w) -> p m w", w=T); R_uc = [Res("uc0"), Res("uc1")]
    ucb = ab(5344, 2 * T).rearrange("p (m w) -> p m w", w=T); R_ucb = Res("ucb")
    t1, t2, t3, t4 = (af(6400 + i * T, T) for i in range(4))
    R_t = [Res("t%d" % i) for i in range(4)]
    X1b = [ab(10624 + i * 528, T) for i in range(2)]; R_X1 = [Res("X1b0"), Res("X1b1")]
    X2b = [ab(11680 + i * 512, 1024) for i in range(2)]; R_X2 = [Res("X2b0"), Res("X2b1")]
    wab = ab(12704, 1024).rearrange("p (a k m) -> p a k m", a=2, k=2); R_wab = Res("wab")
    tmp8 = sb("tmp8", [128, 8]); R_tmp8 = Res("tmp8")
    xh = ab(0, KC * 6).rearrange("p (c w) -> p c w", w=6); R_xh = Res("xh")

    def ubseg(mi):
        return ub[:, mi, 0:1030].rearrange("p (s w) -> p s w", w=515)

    def ubsmp(mi):
        return ub[:, mi, 1030:1086].rearrange("p (s w) -> p s w", w=7)

    def smp84(ap1d):
        return ap1d.rearrange("p (s t) -> p s t", t=4)

    def rec_blocks(l):
        blocks = [wblk(w_x[l], 0, KC, nb * 256) for nb in range(8)]
        for n in range(8):
            blocks.append(wblk(w_x[l], 0, KC, n * 256))
            blocks.append(wblk(w_g[l], 0, KC, n * 256))
        blocks += [wblk(w_o[l], 0, KC, mb * 256) for mb in range(8)]
        return blocks

    def recurrent(l):
        rmsnorm(V_NMIX + l)
        in_tm(stc[l].rearrange("s t f -> (s t) f"), 24, hsmp[:], R_small["hsmp"])
        in_tm(sth[l], 8, h0s[:], R_small["h0s"])
        C("dve", cp(xh[:, :, 0:3], xn[:, :, 509:512]), r=[R_xn], w=[R_xh])
        C("dve", cp(xh[:, :, 3:6], xn[:, :, 1021:1024]), r=[R_xn], w=[R_xh])
        for nb in range(8):
            blk = ws_take()
            for mi in range(2):
                m = nb * 2 + mi
                for k in range(KC):
                    C("pe", mm(ps[:, 6, m * 6:(m + 1) * 6], blk[1][:, k, mi * 128:(mi + 1) * 128], xh[:, k, :], k == 0, k == KC - 1),
                      r=[R_wb[blk[0]], R_xh], w=[R_ps[6]])
            ws_release(blk)
        C("dve", cp(halo_own[:].rearrange("p c s t -> p c (s t)"), ps[:, 6, 0:96].rearrange("p (c w) -> p c w", w=6)), r=[R_ps[6]], w=[R_small["halo_own"]])
        DMA("sp", dma(halo_in_d, halo_own[:].rearrange("p c s t -> p (c s t)")), r=[R_small["halo_own"]], w=[R_hd_in])
        P.CC(lambda e: e.collective_compute("AllGather", ALU.bypass, replica_groups=GROUPS4, ins=[halo_in_d.opt()], outs=[halo_g_d.opt()]),
             r=[R_hd_in], w=[R_hd_g])
        DMA("sp", dma(halo_all[:], halo_g_d.rearrange("(r p) f -> p r f", p=128)), r=[R_hd_g], w=[R_small["halo_all"]])
        hav = halo_all[:].rearrange("p r (c s t) -> p r c s t", s=2, t=3)
        for s in range(2):
            first = True
            for rk in range(4):
                for s2 in range(2):
                    idx = s * 8 + rk * 2 + s2
                    if first:
                        C("dve", ts(halo_sel[:, :, s, :], hav[:, rk, :, s2, :], coef[:, idx:idx + 1], ALU.mult), r=[R_small["halo_all"], R_const], w=[R_small["halo_sel"]])
                        first = False
                    else:
                        C("dve", stt(halo_sel[:, :, s, :], hav[:, rk, :, s2, :], coef[:, idx:idx + 1], halo_sel[:, :, s, :], ALU.mult, ALU.add),
                          r=[R_small["halo_all"], R_const], w=[R_small["halo_sel"]])
        out_fm(hav[:, 0, :, 1, :].rearrange("p c t -> p t c"), 3, conv_p[l].rearrange("t (c p) -> (t c) p", p=128), R_small["halo_all"])
        for n in range(8):
            DMA("pool", dma(wab[:, 0, :, :], w_a[l, n].rearrange("(k p) m -> p k m", p=128)), w=[R_wab])
            DMA("pool", dma(wab[:, 1, :, :], w_i[l, n].rearrange("(k p) m -> p k m", p=128)), w=[R_wab])
            bx = ws_take()
            bgt = ws_take()
            su = [next_set(), next_set()]
            for mi in range(2):
                c = 2 * n + mi
                mm_chunk(bx, KC, mi, xn, R_xn, su[mi])
                s = su[mi]
                C("act", acp(ubseg(mi)[:, :, 3:515], ps_seg(s)), r=[R_ps[3 * s], R_ps[3 * s + 1]], w=[R_ub[mi]])
                C("act", acp(ubsmp(mi)[:, :, 3:7], smp84(ps_smp(s))), r=[R_ps[3 * s + 2]], w=[R_ub[mi]])
                C("dve", cp(ubseg(mi)[:, :, 0:3], halo_sel[:, c, :, :]), r=[R_small["halo_sel"]], w=[R_ub[mi]])
                C("dve", cp(ubsmp(mi)[:, :, 0:3], hsmp[:, c, :].rearrange("p (s t) -> p s t", t=3)), r=[R_small["hsmp"]], w=[R_ub[mi]])
                C("dve", cp(csmp[:, c, :].rearrange("p (s t) -> p s t", t=3), ubsmp(mi)[:, :, 4:7]), r=[R_ub[mi]], w=[R_small["csmp"]])
                for (uv, ov) in ((lambda j, mi=mi: ubseg(mi)[:, :, 3 - j:515 - j], segv(uc[:, mi, :])),
                                 (lambda j, mi=mi: ubsmp(mi)[:, :, 3 - j:7 - j], smp84(smpv(uc[:, mi, :])))):
                    C("act", actf(ov, uv(0), AF.Identity, bias=pvc(V_CB + l, c), scale=pvc(V_CW + 4 * l + 0, c)), r=[R_ub[mi], R_pv], w=[R_uc[mi]])
                    for j in range(1, 4):
                        C("dve", stt(ov, uv(j), pvc(V_CW + 4 * l + j, c), ov, ALU.mult, ALU.add), r=[R_ub[mi], R_pv], w=[R_uc[mi]])
                C("act", acp(ucb[:, mi, :], uc[:, mi, :]), r=[R_uc[mi]], w=[R_ucb])
            ws_release(bx)
            for mi in range(2):
                s = next_set()
                mm_chunk(bgt, KC, mi, xn, R_xn, s)
                for (pv_, ov, tv) in ((ps_seg(s), segv(gt[:, mi, :]), segv(t4)), (ps_smp(s), smpv(gt[:, mi, :]), smpv(t4))):
                    rr = [R_ps[3 * s], R_ps[3 * s + 1], R_ps[3 * s + 2]]
                    C("act", actf(tv, pv_, AF.Square, scale=math.sqrt(0.044715)), r=rr, w=[R_t[3]])
                    C("dve", stt(tv, tv, 1.0, pv_, ALU.add, ALU.mult), r=rr, w=[R_t[3]])
                    C("act", actf(tv, tv, AF.Sigmoid, scale=1.5957691216057308), r=[R_t[3]], w=[R_t[3]])
                    C("dve", tt(ov, tv, pv_, ALU.mult), r=rr + [R_t[3]], w=[R_gt[mi]])
            ws_release(bgt)
            for mi in range(2):
                c = 2 * n + mi
                par = c % 2
                sr = next_set()
                si = next_set()
                for (gi, s) in ((0, sr), (1, si)):
                    for kk in range(2):
                        for ti, (c0, w) in enumerate(TT):
                            C("pe", mm(ps[:, 3 * s + ti, 0:w], wab[:, gi, kk, mi * 128:(mi + 1) * 128], ucb[:, kk, c0:c0 + w], kk == 0, kk == 1),
                              r=[R_wab, R_ucb], w=[R_ps[3 * s + ti]])
                C("act", actf(segv(t1), ps_seg(sr), AF.Sigmoid, bias=pvc(V_BA + l, c)), r=[R_ps[3 * sr], R_ps[3 * sr + 1], R_pv], w=[R_t[0]])
                C("act", actf(smpv(t1), ps_smp(sr), AF.Sigmoid, bias=pvc(V_BA + l, c)), r=[R_ps[3 * sr + 2], R_pv], w=[R_t[0]])
                C("act", actf(segv(t3), ps_seg(si), AF.Sigmoid, bias=pvc(V_BI + l, c)), r=[R_ps[3 * si], R_ps[3 * si + 1], R_pv], w=[R_t[2]])
                C("act", actf(smpv(t3), ps_smp(si), AF.Sigmoid, bias=pvc(V_BI + l, c)), r=[R_ps[3 * si + 2], R_pv], w=[R_t[2]])
                C("act", actf(t2, t1, AF.Exp, scale=nsp[:, l, 1, c:c + 1]), r=[R_t[0], R_small["nsp"]], w=[R_t[1]])
                C("act", actf(t1, t1, AF.Exp, scale=nsp[:, l, 0, c:c + 1]), r=[R_t[0], R_small["nsp"]], w=[R_t[0]])
                C("act", actf(t2, t2, AF.Sqrt, bias=one_t[:], scale=-1.0), r=[R_t[1], R_const], w=[R_t[1]])
                C("dve", tt(t2, t2, t3, ALU.mult), r=[R_t[1], R_t[2]], w=[R_t[1]])
                C("dve", tt(t2, t2, uc[:, mi, :], ALU.mult), r=[R_t[1], R_uc[mi]], w=[R_t[1]])
                C("dve", tt(tmp8[:], smp84(smpv(t1))[:, :, 0], h0s[:, c, :], ALU.mult), r=[R_t[0], R_small["h0s"]], w=[R_tmp8])
                C("dve", tt(smp84(smpv(t2))[:, :, 0], smp84(smpv(t2))[:, :, 0], tmp8[:], ALU.add), r=[R_tmp8, R_t[1]], w=[R_t[1]])
                C("dve", mset(smp84(smpv(t1))[:, :, 0], 0.0), r=[R_tmp8], w=[R_t[0]])
                for sg_ in range(2):
                    sl = slice(sg_ * 512, (sg_ + 1) * 512)
                    C("dve", lambda e, sl=sl: e.tensor_tensor_scan(out=t3[:, sl], data0=t1[:, sl], data1=t2[:, sl], initial=0.0, op0=ALU.mult, op1=ALU.add),
                      r=[R_t[0], R_t[1]], w=[R_t[2]])
                    C("dve", lambda e, sl=sl: e.tensor_tensor_scan(out=t4[:, sl], data0=t1[:, sl], data1=zeros_b[:], initial=1.0, op0=ALU.mult, op1=ALU.add),
                      r=[R_t[0], R_const], w=[R_t[3]])
                C("dve", lambda e: e.tensor_tensor_scan(out=smpv(t3), data0=smpv(t1), data1=smpv(t2), initial=0.0, op0=ALU.mult, op1=ALU.add),
                  r=[R_t[0], R_t[1]], w=[R_t[2]])
                C("dve", cp(car[:, 0, c, :], t4[:, 511:1024:512]), r=[R_t[3]], w=[R_small["car"]])
                C("dve", cp(car[:, 1, c, :], t3[:, 511:1024:512]), r=[R_t[2]], w=[R_small["car"]])
                C("dve", cp(hlast_s[:, c, :], smp84(smpv(t3))[:, :, 3]), r=[R_t[2]], w=[R_small["hlast_s"]])
                C("dve", tt(X1b[par], t3, gt[:, mi, :], ALU.mult), r=[R_t[2], R_gt[mi]], w=[R_X1[par]])
                C("dve", tt(X2b[par], t4[:, 0:1024], gt[:, mi, 0:1024], ALU.mult), r=[R_t[3], R_gt[mi]], w=[R_X2[par]])
                DMA("sp", dma(x1_d[:, c, :], X1b[par]), r=[R_X1[par]], w=[R_x1d])
                DMA("sp", dma(x2_d[:, c, :], X2b[par]), r=[R_X2[par]], w=[R_x2d])
        out_fm(hlast_s[:].rearrange("p c s -> p s c"), 8, h_s[l].rearrange("s (c p) -> (s c) p", p=128), R_small["hlast_s"])
        out_fm(csmp[:].rearrange("p c st -> p st c"), 24, conv_s[l].rearrange("s t (c p) -> (s t c) p", p=128), R_small["csmp"])
        DMA("sp", dma(car_in_d, car[:].rearrange("p a c s -> p (a c s)")), r=[R_small["car"]], w=[R_cd_in])
        P.CC(lambda e: e.collective_compute("AllGather", ALU.bypass, replica_groups=GROUPS4, ins=[car_in_d.opt()], outs=[car_g_d.opt()]),
             r=[R_cd_in], w=[R_cd_g])
        DMA("sp", dma(car_all, car_g_d.rearrange("(r p) f -> p r f", p=128)), r=[R_cd_g], w=[R_small["car_all"]])
        cav = car_all.rearrange("p r (a c s) -> p r a c s", a=2, s=2)
        C("dve", mset(Hc[:, 0, :], 0.0), w=[R_small["Hc"]])
        for g in range(8):
            rk, s2 = (g, 0) if g < 4 else (7 - g, 1)
            C("dve", tt(Hc[:, g + 1, :], cav[:, rk, 0, :, s2], Hc[:, g, :], ALU.mult), r=[R_small["car_all"]], w=[R_small["Hc"]])
            C("dve", tt(Hc[:, g + 1, :], Hc[:, g + 1, :], cav[:, rk, 1, :, s2], ALU.add), r=[R_small["car_all"]], w=[R_small["Hc"]])
        for s in range(2):
            for g in range(8):
                idx = 16 + s * 8 + g
                if g == 0:
                    C("dve", ts(Hin[:, :, s], Hc[:, g, :], coef[:, idx:idx + 1], ALU.mult), r=[R_small["Hc"], R_const], w=[R_small["Hin"]])
                else:
                    C("dve", stt(Hin[:, :, s], Hc[:, g, :], coef[:, idx:idx + 1], Hin[:, :, s], ALU.mult, ALU.add), r=[R_small["Hc"], R_const], w=[R_small["Hin"]])
        out_fm(Hc[:, 8:9, :], 1, h_p[l:l + 1, :].rearrange("o (c p) -> (o c) p", p=128), R_small["Hc"])
        for c in range(KC):
            par = c % 2
            DMA("sp", dma(X1b[par], x1_d[:, c, :]), r=[R_x1d], w=[R_X1[par]])
            DMA("sp", dma(X2b[par], x2_d[:, c, :]), r=[R_x2d], w=[R_X2[par]])
            for s in range(2):
                sl = slice(s * 512, (s + 1) * 512)
                C("dve", stt(xn[:, c, sl], X2b[par][:, sl], Hin[:, c, s:s + 1], X1b[par][:, sl], ALU.mult, ALU.add),
                  r=[R_X1[par], R_X2[par], R_small["Hin"]], w=[R_xn])
            C("dve", cp(xn[:, c, 1024:T], X1b[par][:, 1024:T]), r=[R_X1[par]], w=[R_xn])
        for mb in range(8):
            blk = ws_take()
            for mi in range(2):
                s = next_set()
                mm_chunk(blk, KC, mi, xn, R_xn, s)
                resid_add(mb * 2 + mi, s)
            ws_release(blk)
        P.barrier()

    kf_t = [af(i * 256, 256) for i in range(2)]; R_kf = [Res("kf0"), Res("kf1")]
    kr_t2 = [af(512, 256), af(2080, 256)]; R_kr2 = [Res("kr0"), Res("kr1")]
    kb_t = [ab(768 + i * 128, 256) for i in range(2)]; R_kb = [Res("kb0"), Res("kb1")]
    kTh = ab(1024, 2 * T).rearrange("p (h t) -> p h t", t=T); R_kTh = Res("kTh")
    QT = ab(4096, NH * T).rearrange("p (h t) -> p h t", t=T); R_QT = Res("QT")
    psb = [ps[:, b, :].bitcast(BF16) for b in range(8)]
    tmcnt = [0]

    def proj_tm(blocks_cols, W2d, rope, sink):
        for nb, c0 in enumerate(blocks_cols):
            blk = ws_take()
            for tk in range(9):
                rows = 128 if tk < 8 else NSMP
                tc0 = tk * 128
                bank = tmcnt[0] % 2
                tmcnt[0] += 1
                for k in range(KC):
                    C("pe", mm(ps[0:rows, bank, 0:256], xn[:, k, tc0:tc0 + rows], blk[1][:, k, :], k == 0, k == KC - 1),
                      r=[R_wb[blk[0]], R_xn], w=[R_ps[bank]])
                sink(nb, tk, rows, bank)
            ws_release(blk)

    def rope_evac(tk, rows, bank, i, need_f32=True):
        psv = ps[0:rows, bank, 0:256].rearrange("p (g d) -> p g d", d=64)
        kfv = kf_t[i][0:rows, :].rearrange("p (g d) -> p g d", d=64)
        kr_t = kr_t2[i]; R_kr = R_kr2[i]
        krv = kr_t[0:rows, :].rearrange("p (g d) -> p g d", d=64)
        sn = sin32[0:rows, tk:tk + 1, :].broadcast_to([rows, 4, 32])
        cb4 = cos32[0:rows, tk:tk + 1, :].unsqueeze(1).broadcast_to([rows, 4, 2, 32])
        C("dve", tt(kf_t[i][0:rows, :].rearrange("p (g h d) -> p g h d", h=2, d=32), ps[0:rows, bank, 0:256].rearrange("p (g h d) -> p g h d", h=2, d=32), cb4, ALU.mult),
          r=[R_ps[bank], R_const], w=[R_kf[i]])
        C("dve", stt(krv[:, :, 0:32], psv[:, :, 32:64], -1.0, sn, ALU.mult, ALU.mult), r=[R_ps[bank], R_const], w=[R_kr])
        C("dve", tt(krv[:, :, 32:64], psv[:, :, 0:32], sn, ALU.mult), r=[R_ps[bank], R_const], w=[R_kr])
        if need_f32:
            C("dve", tt(kf_t[i][0:rows, :], kf_t[i][0:rows, :], kr_t[0:rows, :], ALU.add), r=[R_kr], w=[R_kf[i]])
            C("act", acp(kb_t[i][0:rows, :], kf_t[i][0:rows, :]), r=[R_kf[i]], w=[R_kb[i]])
        else:
            C("dve", tt(kb_t[i][0:rows, :], kf_t[i][0:rows, :], kr_t[0:rows, :], ALU.add), r=[R_kr, R_kf[i]], w=[R_kb[i]])

    def transpose_heads(tk, rows, i, dst3, R_dst, h0):
        tb = 2 + (tk % 2)
        for hh in range(2):
            C("pe", tr(psb[tb][:, hh * 128:hh * 128 + rows], kb_t[i][0:rows, hh * 128:(hh + 1) * 128], ident_b[0:rows, 0:rows]),
              r=[R_kb[i], R_const], w=[R_ps[tb]])
        C("act", acp(dst3[:, h0:h0 + 2, tk * 128:tk * 128 + rows], psb[tb][:, 0:256].rearrange("p (h t) -> p h t", t=128)[:, :, 0:rows]),
          r=[R_ps[tb]], w=[R_dst])

    vnew4 = ab(2560, 4096).rearrange("p (s h d) -> p s h d", h=4, d=128)
    vlv = [v_loc[q].rearrange("(h s p) (i d) -> p h s i d", s=2, p=128, d=128) for q in range(4)]

    def kv_phase():
        rmsnorm(V_NKV)
        C("dve", mset(vnew_sel[:], 1.0), w=[R_small["vnew_sel"]])
        cnt = [0]

        def sink(nb, tk, rows, bank):
            i = cnt[0] % 2
            cnt[0] += 1
            if nb < 8:
                rope_evac(tk, rows, bank, i)
                dst = k_p[tk * 128:(tk + 1) * 128, nb * 256:(nb + 1) * 256] if tk < 8 else k_s[:, nb * 256:(nb + 1) * 256]
                DMA("sp", dma(dst, kf_t[i][0:rows, :]), r=[R_kf[i]])
                transpose_heads(tk, rows, i, kTh, R_kTh, 0)
                if tk == 8:
                    DMA("sp", dma(kT_loc[nb // 2][(nb % 2) * 256:(nb % 2 + 1) * 256, :].rearrange("(h p) t -> p h t", p=128), kTh[:, :, 0:1024]), r=[R_kTh], w=[R_kTl])
                    C("dve", cp(ksT[:, 2 * nb:2 * nb + 2, :], kTh[:, :, 1024:T]), r=[R_kTh], w=[R_small["ksT"]])
            else:
                vb_ = nb - 8
                C("act", acp(kf_t[i][0:rows, :], ps[0:rows, bank, 0:256]), r=[R_ps[bank]], w=[R_kf[i]])
                C("dve", cp(kb_t[i][0:rows, :], kf_t[i][0:rows, :]), r=[R_kf[i]], w=[R_kb[i]])
                if tk < 8:
                    DMA("sp", dma(v_p[tk * 128:(tk + 1) * 128, vb_ * 256:(vb_ + 1) * 256], kf_t[i][0:rows, :]), r=[R_kf[i]])
                    DMA("sp", dma(vlv[vb_ // 2][:, 2 * (vb_ % 2):2 * (vb_ % 2) + 2, tk // 4, tk % 4, :], kb_t[i][:, :].rearrange("p (h d) -> p h d", d=128)), r=[R_kb[i]], w=[R_vl])
                else:
                    DMA("sp", dma(v_s[:, vb_ * 256:(vb_ + 1) * 256], kf_t[i][0:rows, :]), r=[R_kf[i]])
                    for hh in range(2):
                        for hx in range(2):
                            idx = 32 + hh * 16 + 2 * vb_ + hx
                            srcv = kb_t[i][0:NSMP, hx * 128:(hx + 1) * 128]
                            if vb_ == 0 and hx == 0:
                                C("dve", ts(vnew_sel[:, hh, 0:128], srcv, coef[0:NSMP, idx:idx + 1], ALU.mult), r=[R_kb[i], R_const], w=[R_small["vnew_sel"]])
                            else:
                                C("dve", stt(vnew_sel[:, hh, 0:128], srcv, coef[0:NSMP, idx:idx + 1], vnew_sel[:, hh, 0:128], ALU.mult, ALU.add),
                                  r=[R_kb[i], R_const], w=[R_small["vnew_sel"]])

        proj_tm([nb * 256 for nb in range(16)], w_kv, True, sink)
        for q in range(4):
            P.CC(lambda e, q=q: e.collective_compute("AllGather", ALU.bypass, replica_groups=GROUPS4, ins=[kT_loc[q].opt()], outs=[kT_g[q].opt()]), r=[R_kTl], w=[R_kTg])
            P.CC(lambda e, q=q: e.collective_compute("AllGather", ALU.bypass, replica_groups=GROUPS4, ins=[v_loc[q].opt()], outs=[v_g[q].opt()]), r=[R_vl], w=[R_vg])
        for hh in range(2):
            for h in range(NH):
                idx = 32 + hh * 16 + h
                if h == 0:
                    C("dve", ts(ksT_sel[:, hh, :], ksT[:, h, :], coef[:, idx:idx + 1], ALU.mult), r=[R_small["ksT"], R_const], w=[R_small["ksT_sel"]])
                else:
                    C("dve", stt(ksT_sel[:, hh, :], ksT[:, h, :], coef[:, idx:idx + 1], ksT_sel[:, hh, :], ALU.mult, ALU.add),
                      r=[R_small["ksT"], R_const], w=[R_small["ksT_sel"]])
        P.barrier()

    kp_t = [ab(i * 512, 512) for i in range(4)]; vp_t = [ab(i * 512 + 256, 512).rearrange("p (i d) -> p i d", d=128) for i in range(4)]
    R_kp = [Res("kp%d" % i) for i in range(4)]; R_vp = [Res("vp%d" % i) for i in range(4)]
    pt_t = [ab(2048 + i * 256, 512) for i in range(4)]; R_pt = [Res("pt%d" % i) for i in range(4)]
    f0 = af(3072, 512); f1 = af(3584, 512); R_f = [Res("f0"), Res("f1")]
    qs_t = sb("qs_t", [128, 2, NSMP]); R_qs = Res("qs")
    kTgv = [kT_g[q].rearrange("(r h p) t -> r h p t", h=4, p=128) for q in range(4)]
    kTlv = [kT_loc[q].rearrange("(h p) t -> h p t", p=128) for q in range(4)]
    vgv = [v_g[q].rearrange("(r h s p) (i d) -> r h s p i d", h=4, s=2, p=128, d=128) for q in range(4)]
    vllv = [v_loc[q].rearrange("(h s p) (i d) -> h s p i d", s=2, p=128, d=128) for q in range(4)]

    def attn_blocks(j):
        return [wblk(w_q[j], 0, KC, nb * 256) for nb in range(8)] + [wblk(w_do[j], 0, KC, mb * 256) for mb in range(8)]

    def attention(j):
        l = j + 2
        rmsnorm(V_NMIX + l)
        cnt = [0]

        def sink(nb, tk, rows, bank):
            i = cnt[0] % 2
            cnt[0] += 1
            rope_evac(tk, rows, bank, i, need_f32=False)
            transpose_heads(tk, rows, i, QT, R_QT, 2 * nb)

        proj_tm([nb * 256 for nb in range(8)], w_q[j], True, sink)
        for hh in range(2):
            for h in range(NH):
                idx = 32 + hh * 16 + h
                if h == 0:
                    C("dve", ts(qs_t[:, hh, :], QT[:, h, 1024:T], coef[:, idx:idx + 1], ALU.mult), r=[R_QT, R_const], w=[R_qs])
                else:
                    C("dve", stt(qs_t[:, hh, :], QT[:, h, 1024:T], coef[:, idx:idx + 1], qs_t[:, hh, :], ALU.mult, ALU.add), r=[R_QT, R_const], w=[R_qs])
        C("dve", mset(qbd[:], 0.0), w=[R_small["qbd"]])
        for hh in range(2):
            C("dve", cp(qbd[0:64, hh, 0:32].rearrange("p (t s) -> p t s", s=8), qs_t[0:64, hh, :].rearrange("p (s t) -> p t s", t=4)), r=[R_qs], w=[R_small["qbd"]])
            C("dve", cp(qbd[64:128, hh, 32:64].rearrange("p (t s) -> p t s", s=8), qs_t[64:128, hh, :].rearrange("p (s t) -> p t s", t=4)), r=[R_qs], w=[R_small["qbd"]])
        P.barrier()
        LA = 2
        ucount = [0]
        spiece = [0]
        sample_setup()
        pc = [0]
        ptc = [0]
        pending = []

        def finalize(h, qb):
            qcols = slice(qb * 512, (qb + 1) * 512)
            C("dve", lambda e: e.reciprocal(out=f0, in_=ps[:, 2, :]), r=[R_ps[2]], w=[R_f[0]])
            C("dve", tt(f0, ps[:, 0, :], f0, ALU.mult), r=[R_ps[0]], w=[R_f[0]])
            C("dve", lambda e: e.reciprocal(out=f1, in_=ps[:, 3, :]), r=[R_ps[3]], w=[R_f[1]])
            C("dve", tt(f1, ps[:, 1, :], f1, ALU.mult), r=[R_ps[1]], w=[R_f[1]])
            C("dve", stt(f0, f1, lamt[:, j, 1:2], f0, ALU.mult, ALU.add), r=[R_f[1], R_small["lam"]], w=[R_f[0]])
            C("act", actf(f1, f0, AF.Square), r=[R_f[0]], w=[R_f[1]])
            C("pe", mm(ps[:, 7, :], ones_f[:], f1, True, True), r=[R_f[1], R_const], w=[R_ps[7]])
            C("act", actf(f1, ps[:, 7, :], AF.Sqrt, bias=eps_t[:], scale=1.0 / 128), r=[R_ps[7], R_const], w=[R_f[1]])
            C("dve", lambda e: e.reciprocal(out=f1, in_=f1), r=[R_f[1]], w=[R_f[1]])
            C("dve", stt(xn[:, h, qcols], f0, gsub[:, j:j + 1], f1, ALU.mult, ALU.mult), r=[R_f[0], R_f[1], R_small["gsub"]], w=[R_xn])

        def emit_av(u):
            (h, qb, sl, c, i, pi, first, last, fin) = u
            C("pe", mm(ps[:, c, :], vp_t[sl][:, i, :], pt_t[pi], first, last), r=[R_vp[sl], R_pt[pi]], w=[R_ps[c]])
            C("pe", mm(ps[:, 2 + c, :], ones_b[:], pt_t[pi], first, last), r=[R_const, R_pt[pi]], w=[R_ps[2 + c]])
            if fin:
                finalize(h, qb)

        def flush(keep):
            while len(pending) > keep:
                emit_av(pending.pop(0))

        for h in range(NH):
            for qb in range(2):
                qcols = slice(qb * 512, (qb + 1) * 512)
                plist = [("g", g) for g in range(3 if qb == 0 else 7)] + [("d", qb)]
                nun = len(plist) * 4
                un = [0, 0]
                for (kind, g) in plist:
                    sl = pc[0] % 4
                    pc[0] += 1
                    if kind == "g":
                        rk, s2 = (g, 0) if g < 4 else (7 - g, 1)
                        DMA("sp", dma(kp_t[sl], kTgv[h // 4][rk, h % 4, :, s2 * 512:(s2 + 1) * 512]), r=[R_kTg], w=[R_kp[sl]])
                        DMA("sp", dma(vp_t[sl], vgv[h // 4][rk, h % 4, s2, :, :, :]), r=[R_vg], w=[R_vp[sl]])
                        bias_ap = ebias[:, qb * 8 + g:qb * 8 + g + 1]
                    else:
                        DMA("sp", dma(kp_t[sl], kTlv[h // 4][h % 4, :, qb * 512:(qb + 1) * 512]), r=[R_kTl], w=[R_kp[sl]])
                        DMA("sp", dma(vp_t[sl], vllv[h // 4][h % 4, qb, :, :, :]), r=[R_vl], w=[R_vp[sl]])
                        bias_ap = ebias[:, 15:16]
                    for i in range(4):
                        for c in range(2):
                            rs_ = slice(c * 64, (c + 1) * 64)
                            sb_ = 4 + (ptc[0] % 2)
                            pi = ptc[0] % 4
                            ptc[0] += 1
                            C("pe", mm(ps[:, sb_, :], kp_t[sl][rs_, i * 128:(i + 1) * 128], QT[rs_, h, qcols], True, True),
                              r=[R_kp[sl], R_QT], w=[R_ps[sb_]])
                            C("act", actf(pt_t[pi], ps[:, sb_, :], AF.Exp, bias=bias_ap, scale=0.125), r=[R_ps[sb_], R_const], w=[R_pt[pi]])
                            if kind == "d":
                                C("dve", tt(pt_t[pi], pt_t[pi], mtab[:, 512 - 128 * i:1024 - 128 * i], ALU.mult), r=[R_const], w=[R_pt[pi]])
                            first = un[c] == 0
                            last = un[c] == nun - 1
                            un[c] += 1
                            pending.append((h, qb, sl, c, i, pi, first, last, last and c == 1))
                            flush(LA)
                            ucount[0] += 1
                            if ucount[0] % 5 == 0 and spiece[0] < NPIECE:
                                sample_step(spiece[0])
                                spiece[0] += 1
        flush(0)
        while spiece[0] < NPIECE:
            sample_step(spiece[0])
            spiece[0] += 1
        P.barrier()
        sample_finish(j)
        for mb in range(8):
            blk = ws_take()
            for mi in range(2):
                s = next_set()
                mm_chunk(blk, KC, mi, xn, R_xn, s)
                resid_add(mb * 2 + mi, s)
            ws_release(blk)
        P.barrier()

    SA = 12544
    kpc = [ab(SA + i * 512, 1024).rearrange("p (g h t) -> p g h t", h=2, t=128) for i in range(2)]; R_kpc = [Res("kpc0"), Res("kpc1")]
    vpc = [ab(SA + 1024 + i * 528, 1056).rearrange("p (g h d) -> p g h d", h=2, d=132) for i in range(2)]
    R_vpc = [[Res("vpc00"), Res("vpc01")], [Res("vpc10"), Res("vpc11")]]
    pexp = [ab(SA + 2080 + i * 256, 512).rearrange("p (h w) -> p h w", w=256) for i in range(2)]; R_pexp = [Res("pexp0"), Res("pexp1")]
    Et = [ab(SA + 2592 + i * 16, 32) for i in range(2)]; R_E = [Res("E0"), Res("E1")]
    vm_t = [ab(SA + 2624 + i * 16, 32) for i in range(2)]; R_vm = [Res("vm0"), Res("vm1")]
    pnew = ab(SA + 2656, 128).rearrange("p (h w) -> p h w", w=64); R_pnew = Res("pnew")
    oacc = af(SA + 2720, 264).rearrange("p (h d) -> p h d", d=132); R_oacc = Res("oacc")
    onrm = af(SA + 2984, 256).rearrange("p (h d) -> p h d", d=128); R_onrm = Res("onrm")
    od_t = af(SA + 3240, 256).rearrange("p (h d) -> p h d", d=128); R_od = Res("od")
    osq = af(SA + 3496, 256).rearrange("p (h d) -> p h d", d=128); R_osq = Res("osq")
    ss_t = af(SA + 3752, 4); R_ss = Res("ss")
    osend = af(SA + 3756, 64); R_osend = Res("osend")
    oall = af(0, 512).rearrange("p (r w) -> p r w", w=64); R_oall = Res("oall")
    ckv = ckT.rearrange("g p h t -> p g h t")
    cvv = cv.rearrange("g p h d -> p g h d")
    NPIECE = 320
    ptb4 = pt_f[:, :].unsqueeze(1).broadcast_to([128, 4, 8])
    iob4 = iota8[:, 0:4].unsqueeze(2).broadcast_to([128, 4, 8])
    pb7 = ps[0:64, 7, 0:264].rearrange("p (h d) -> p h d", d=132)[:, :, 0:129]

    def sample_setup():
        for i in range(2):
            C("dve", mset(vpc[i][:], 1.0), w=R_vpc[i])

    def sample_av(pc_):
        bj = pc_ % 2
        for hh in range(2):
            for g in range(4):
                C("pe", mm(ps[0:64, 7, hh * 132:hh * 132 + 129], pexp[bj][:, hh, g * 64:(g + 1) * 64], vpc[bj][:, g, hh, 0:129], g == 0, g == 3),
                  r=[R_pexp[bj], R_vpc[bj][hh]], w=[R_ps[7]])
        if pc_ == 0:
            C("dve", cp(oacc[0:64, :, 0:129], pb7), r=[R_ps[7]], w=[R_oacc])
        else:
            C("dve", tt(oacc[0:64, :, 0:129], oacc[0:64, :, 0:129], pb7, ALU.add), r=[R_ps[7]], w=[R_oacc])

    def sample_step(piece):
        bi = piece % 2
        g0 = piece * 4
        DMA("pool", dma(kpc[bi], ckv[:, g0:g0 + 4, :, :]), w=[R_kpc[bi]])
        for hh in range(2):
            DMA("pool", dma(vpc[bi][:, :, hh, 0:128], cvv[:, g0:g0 + 4, hh, :]), w=[R_vpc[bi][hh]])
        C("dve", stt(Et[bi].rearrange("p (a s) -> p a s", s=8), ptb4, float(g0), iob4, ALU.subtract, ALU.is_equal), r=[R_small["pt"], R_const], w=[R_E[bi]])
        C("pe", mm(ps[:, 7, 448:480], ones_b[:], Et[bi], True, True), r=[R_E[bi], R_const], w=[R_ps[7]])
        C("act", acp(vm_t[bi], ps[:, 7, 448:480]), r=[R_ps[7]], w=[R_vm[bi]])
        for hh in range(2):
            for g in range(4):
                C("pe", mm(ps[:, 6, hh * 256 + g * 64:hh * 256 + (g + 1) * 64], kpc[bi][:, g, hh, :], qbd[:, hh, :], True, True),
                  r=[R_kpc[bi], R_small["qbd"]], w=[R_ps[6]])
        C("act", actf(pexp[bi], ps[:, 6, :].rearrange("p (h w) -> p h w", w=256), AF.Exp, scale=0.125), r=[R_ps[6]], w=[R_pexp[bi]])
        vmb = vm_t[bi].rearrange("p (a s) -> p a s", s=8).unsqueeze(2).broadcast_to([128, 4, 8, 8])
        for hh in range(2):
            pv4 = pexp[bi][:, hh, :].rearrange("p (a c s) -> p a c s", c=8, s=8)
            C("dve", tt(pv4, pv4, vmb, ALU.mult), r=[R_vm[bi]], w=[R_pexp[bi]])
        if piece > 0:
            sample_av(piece - 1)

    def sample_finish(j):
        sample_av(NPIECE - 1)
        for hh in range(2):
            C("pe", mm(ps[0:32, 6, hh * 64:(hh + 1) * 64], ksT_sel[:, hh, :], qbd[:, hh, :], True, True), r=[R_small["ksT_sel"], R_small["qbd"]], w=[R_ps[6]])
        C("act", actf(pnew[0:32, :, :], ps[0:32, 6, 0:128].rearrange("p (h w) -> p h w", w=64), AF.Exp, scale=0.125), r=[R_ps[6]], w=[R_pnew])
        for hh in range(2):
            C("dve", tt(pnew[0:32, hh, :], pnew[0:32, hh, :], smask[:], ALU.mult), r=[R_const], w=[R_pnew])
            C("pe", mm(ps[0:64, 7, hh * 132:hh * 132 + 129], pnew[0:32, hh, :], vnew_sel[:, hh, 0:129], True, True), r=[R_pnew, R_small["vnew_sel"]], w=[R_ps[7]])
        C("dve", tt(oacc[0:64, :, 0:129], oacc[0:64, :, 0:129], pb7, ALU.add), r=[R_ps[7]], w=[R_oacc])
        for hh in range(2):
            C("dve", lambda e, hh=hh: e.reciprocal(out=ss_t[0:64, hh:hh + 1], in_=oacc[0:64, hh, 128:129]), r=[R_oacc], w=[R_ss])
            C("dve", ts(onrm[0:64, hh, :], oacc[0:64, hh, 0:128], ss_t[0:64, hh:hh + 1], ALU.mult), r=[R_oacc, R_ss], w=[R_onrm])
        C("dve", cp(osq[0:32, :, :], onrm[32:64, :, :]), r=[R_onrm], w=[R_osq])
        C("dve", stt(od_t[0:32, :, :], osq[0:32, :, :], lamt[0:32, j, 1:2], onrm[0:32, :, :], ALU.mult, ALU.add), r=[R_onrm, R_osq, R_small["lam"]], w=[R_od])
        C("dve", tt(osq[0:32, :, :], od_t[0:32, :, :], od_t[0:32, :, :], ALU.mult), r=[R_od], w=[R_osq])
        C("dve", lambda e: e.tensor_reduce(out=ss_t[0:32, 2:4], in_=osq[0:32, :, :], axis=AX.X, op=ALU.add), r=[R_osq], w=[R_ss])
        C("act", actf(ss_t[0:32, 2:4], ss_t[0:32, 2:4], AF.Sqrt, bias=eps_t[0:32, :], scale=1.0 / 128), r=[R_ss, R_const], w=[R_ss])
        C("dve", lambda e: e.reciprocal(out=ss_t[0:32, 2:4], in_=ss_t[0:32, 2:4]), r=[R_ss], w=[R_ss])
        for hh in range(2):
            C("dve", stt(osq[0:32, hh, :], od_t[0:32, hh, :], ss_t[0:32, 2 + hh:3 + hh], gsub_row[:, j, :], ALU.mult, ALU.mult),
              r=[R_od, R_ss, R_small["gsub"]], w=[R_osq])
        for hh in range(2):
            C("pe", tr(ps[:, 7, 128 + hh * 32:128 + (hh + 1) * 32], osq[0:32, hh, :], ident_f[0:32, 0:32]), r=[R_osq, R_const], w=[R_ps[7]])
        C("dve", cp(osend.rearrange("p (h s t) -> p h s t", h=2, t=4), ps[:, 7, 128:192].rearrange("p (h t s) -> p h s t", h=2, t=4)), r=[R_ps[7]], w=[R_osend])
        DMA("sp", dma(os_in_d, osend), r=[R_osend], w=[R_os_in])
        P.CC(lambda e: e.collective_compute("AllGather", ALU.bypass, replica_groups=GROUPS8, ins=[os_in_d.opt()], outs=[os_g_d.opt()]), r=[R_os_in], w=[R_os_g])
        DMA("sp", dma(oall, os_g_d.rearrange("(r p) f -> p r f", p=128)), r=[R_os_g], w=[R_oall])
        C("dve", cp(xn[:, :, 1024:T], oall.rearrange("p r (h w) -> p (r h) w", w=NSMP)), r=[R_oall], w=[R_xn])

    ytok = [af(i * 2048, 2048) for i in range(2)]
    R_ytok = [Res("ytok0"), Res("ytok1")]

    def final_out():
        rmsnorm(V_NFIN, out_is_x=True)
        for tk in range(9):
            rows = 128 if tk < 8 else NSMP
            b = tk % 2
            for cg in range(4):
                bank = 6 + (cg % 2)
                for cc in range(4):
                    c = cg * 4 + cc
                    C("pe", tr(ps[0:rows, bank, cc * 128:(cc + 1) * 128], xT[:, c, tk * 128:tk * 128 + rows], ident_f[:]), r=[R_x, R_const], w=[R_ps[bank]])
                if cg % 2:
                    C("act", acp(ytok[b][0:rows, cg * 512:(cg + 1) * 512], ps[0:rows, bank, :]), r=[R_ps[bank]], w=[R_ytok[b]])
                else:
                    C("dve", cp(ytok[b][0:rows, cg * 512:(cg + 1) * 512], ps[0:rows, bank, :]), r=[R_ps[bank]], w=[R_ytok[b]])
            dst = y_p[tk * 128:(tk + 1) * 128, :] if tk < 8 else y_s
            DMA("sp", dma(dst, ytok[b][0:rows, :]), r=[R_ytok[b]])

    phases = []
    for l in range(2):
        phases.append((rec_blocks(l), lambda l=l: recurrent(l)))
        phases.append((ffn_blocks(l), lambda l=l: ffn(l)))
    phases.append(([wblk(w_kv, 0, KC, nb * 256) for nb in range(16)], kv_phase))
    for j in range(2):
        phases.append((attn_blocks(j), lambda j=j: attention(j)))
        phases.append((ffn_blocks(j + 2), lambda j=j: ffn(j + 2)))
    phases = [phases[i] for i in PHASE_SEL] if PHASE_SEL is not None else phases[:NPHASE]
    ws_queue(phases[0][0])
    for i, (blks, fn) in enumerate(phases):
        if i + 1 < len(phases):
            ws_pending.extend(phases[i + 1][0])
        fn()
    if NPHASE >= 9:
        final_out()
    P.barrier()
    return nc, P, es


def emit_program(nc, P, es):
    sig = P.emit(nc, None)
    sem_eng = {e: es.enter_context(nc.semaphore("s_" + e)) for e in ENGS}
    sem_dma = {}
    for q, n in P.NDMA.items():
        for i in range(n):
            sem_dma[(q, i)] = es.enter_context(nc.semaphore("d_%s%d" % (q, i)))
    sem_dma["cc"] = es.enter_context(nc.semaphore("cc"))

    def run(eh, ename):
        waited = {}
        for node in P.ops[ename]:
            for d in sorted(node.deps, key=str):
                if d[0] == "c":
                    E = d[1]
                    if E == ename and (E == "pe" or not SAME_ENGINE_SYNC):
                        continue
                    sem = sem_eng[E]
                    v = sig[E][d[2]]
                    key = E
                else:
                    sem = sem_dma[d[1]]
                    v = d[2]
                    key = d[1]
                if waited.get(key, 0) < v:
                    eh.wait_ge(sem, v)
                    waited[key] = v
            if node.kind == "b":
                continue
            ins = node.fn(eh)
            if node.kind == "c":
                if node.signal:
                    ins.then_inc(sem_eng[ename], 1)
            elif node.kind == "d":
                ins.then_inc(sem_dma[node.sem], 16)
            else:
                ins.then_inc(sem_dma["cc"])

    with nc.Block() as block:
        @block.tensor
        def _(e):
            run(e, "pe")

        @block.scalar
        def _(e):
            run(e, "act")

        @block.vector
        def _(e):
            run(e, "dve")

        @block.gpsimd
        def _(e):
            run(e, "pool")

        @block.sync
        def _(e):
            run(e, "sp")


NPHASE = 9
PHASE_SEL = None
_CACHE = {}


def _get_program():
    if "nc" not in _CACHE:
        nc, P, es = build_program()
        emit_program(nc, P, es)
        _CACHE["nc"] = nc
        _CACHE["es"] = es
    return _CACHE["nc"]


def _host_tables(r):
    b, j = r // 4, r % 4
    g_seg = (j, 7 - j)
    pos = np.zeros((9, 128), np.float32)
    for tk in range(8):
        g = g_seg[tk // 4]
        pos[tk] = g * 512 + (tk % 4) * 128 + np.arange(128)
    pos[8] = PAST + (np.arange(128) % 4)
    inv = (10000.0 ** (-np.arange(32, dtype=np.float32) / 32)).astype(np.float32)
    ang = (pos[:, :, None].astype(np.float32) * inv[None, None, :]).astype(np.float32)
    cos32 = np.ascontiguousarray(np.cos(ang).astype(np.float32).transpose(1, 0, 2))
    sin32 = np.ascontiguousarray(np.sin(ang).astype(np.float32).transpose(1, 0, 2))
    coef = np.zeros((64,), np.float32)
    for s in range(2):
        for rk in range(4):
            for s2 in range(2):
                blk = rk if s2 == 0 else 7 - rk
                if blk == g_seg[s] - 1:
                    coef[s * 8 + rk * 2 + s2] = 1.0
        coef[16 + s * 8 + g_seg[s]] = 1.0
    for hh in range(2):
        coef[32 + hh * 16 + 2 * r + hh] = 1.0
    ebias = np.zeros((16,), np.float32)
    for qb in range(2):
        for g in range(7):
            ebias[qb * 8 + g] = 0.0 if g < g_seg[qb] else -30000.0
    return cos32, sin32, np.tile(coef[None, :], (128, 1)), np.tile(ebias[None, :], (128, 1))


def kernel(x_prompt, x_sample, cache_k, cache_v, page_table, state_conv, state_rglru,
           norm_mix, norm_ffn, norm_final, rg_w_x, rg_w_gate, rg_conv_w, rg_conv_b,
           rg_w_a, rg_b_a, rg_w_i, rg_b_i, rg_lambda, rg_w_out, kv_norm, w_kv,
           dif_w_q, dif_lq1, dif_lk1, dif_lq2, dif_lk2, dif_subln, dif_w_o,
           ffn_w_gate, ffn_w_up, ffn_w_down):
    f32 = lambda a: np.ascontiguousarray(np.asarray(a, dtype=np.float32))
    x_prompt = f32(x_prompt); x_sample = f32(x_sample)
    cache_k = np.asarray(cache_k, dtype=np.float32); cache_v = np.asarray(cache_v, dtype=np.float32)
    nc = _get_program()
    vec_list = [norm_mix[i] for i in range(4)] + [norm_ffn[i] for i in range(4)] + [norm_final, kv_norm]
    vec_list += [rg_conv_w[l, jj] for l in range(2) for jj in range(4)]
    vec_list += [rg_conv_b[l] for l in range(2)] + [rg_b_a[l] for l in range(2)] + [rg_b_i[l] for l in range(2)] + [rg_lambda[l] for l in range(2)]
    vecs = np.concatenate([f32(v).reshape(16, 128) for v in vec_list], axis=0)
    assert vecs.shape[0] == NV * 16
    kq = np.arange(128)
    xx = np.arange(1024) - 512
    maskd = (xx[None, :] >= kq[:, None]).astype(np.float32)
    smaskd = np.zeros((32, 64), np.float32)
    for s_ in range(8):
        for t_ in range(4):
            for q in range(4):
                if t_ <= q:
                    for c_ in range(2):
                        smaskd[s_ * 4 + t_, c_ * 32 + q * 8 + s_] = 1.0
    common = {
        "xs": f32(x_sample.reshape(NSMP, D)), "vecs": vecs, "maskd": maskd, "smaskd": smaskd,
        "ptab": np.ascontiguousarray(np.asarray(page_table, dtype=np.int32).reshape(8, NPG).T),
        "state_conv": f32(state_conv), "state_rglru": f32(state_rglru),
        "rg_w_x": f32(rg_w_x), "rg_w_gate": f32(rg_w_gate), "rg_w_out": f32(rg_w_out),
        "rg_w_a": f32(rg_w_a), "rg_w_i": f32(rg_w_i), "w_kv": f32(w_kv),
        "dif_w_q": f32(dif_w_q), "dif_w_o": f32(dif_w_o),
        "lqk": f32(np.stack([dif_lq1, dif_lk1, dif_lq2, dif_lk2], axis=0)),
        "dif_subln": f32(dif_subln),
        "ffn_w_gate": f32(ffn_w_gate), "ffn_w_up": f32(ffn_w_up), "ffn_w_down": f32(ffn_w_down),
        "identd": np.eye(128, dtype=np.float32),
        "iotad": np.tile(np.arange(8, dtype=np.float32)[None, :], (128, 1)),
    }
    in_maps = []
    for r in range(8):
        b, j = r // 4, r % 4
        cos32, sin32, coef, ebias = _host_tables(r)
        xp = np.concatenate([x_prompt[b, j * 512:(j + 1) * 512], x_prompt[b, (7 - j) * 512:(8 - j) * 512]], axis=0)
        ck = cache_k[:, :, 2 * r:2 * r + 2]
        ckT = np.ascontiguousarray(ck.transpose(0, 3, 4, 2, 1)).reshape(1280, 128, 2, 128)
        cvr = np.ascontiguousarray(cache_v[:, :, 2 * r:2 * r + 2, :])
        m = dict(common)
        m.update({"xp": f32(xp), "cos32": cos32, "sin32": sin32, "coefd": coef, "ebiasd": ebias, "ckT": ckT, "cv": cvr})
        in_maps.append(m)
    res = run_bass_kernel_spmd(nc, in_maps, core_ids=list(range(8)))
    R = res.results
    y_prompt = np.zeros((2, 4096, D), np.float32)
    k_prompt = np.zeros((2, 4096, D), np.float32)
    v_prompt = np.zeros((2, 4096, D), np.float32)
    for r in range(8):
        b, j = r // 4, r % 4
        for (dst, key) in ((y_prompt, "y_p"), (k_prompt, "k_p"), (v_prompt, "v_p")):
            dst[b, j * 512:(j + 1) * 512] = R[r][key][0:512]
            dst[b, (7 - j) * 512:(8 - j) * 512] = R[r][key][512:1024]
    conv_prompt = np.stack([R[0]["conv_p"], R[4]["conv_p"]], axis=1).astype(np.float32)
    h_prompt = np.stack([R[0]["h_p"], R[4]["h_p"]], axis=1).astype(np.float32)
    return (y_prompt,
            np.asarray(R[0]["y_s"], np.float32).reshape(8, 4, D),
            k_prompt.reshape(2, 4096, NH, 2, 64),
            v_prompt.reshape(2, 4096, NH, 128),
            conv_prompt, h_prompt,
            np.asarray(R[0]["k_s"], np.float32).reshape(8, 4, NH, 2, 64),
            np.asarray(R[0]["v_s"], np.float32).reshape(8, 4, NH, 128),
            np.asarray(R[0]["conv_s"], np.float32), np.asarray(R[0]["h_s"], np.float32))
```
